# Optimizing a Trainium2 kernel written in Bass

```python
import math
import jax, jax.numpy as jnp
from jax import lax
import numpy as np

D_MODEL = 2048
BATCH = 4
SEQ = 2048
DEPTH = 4
DEC_BATCH = 8
DEC_SEQ = 4
PAST_LEN = 16384
PAGE_SIZE = 128

N_A = DEPTH // 2
N_B = DEPTH - N_A
H_A = 8
DK_A = D_MODEL // H_A
DV_A = 2 * DK_A
QK_A = H_A * DK_A
V_A = H_A * DV_A
RET_CHUNK = 128
H_B = 16
DH_B = D_MODEL // H_B
MOBA_BLOCK = 256
MOBA_TOPK = 3
Q_CHUNK = 16
ROPE_BASE = 10000.0
EPS = 1e-6

kernel_name = 'yoco_retention_moba_decoder_step'


def _rmsnorm(x, g):
    x32 = x.astype(jnp.float32)
    y = x32 * lax.rsqrt(jnp.mean(x32 * x32, axis=-1, keepdims=True) + EPS)
    return (y * g.astype(jnp.float32)).astype(x.dtype)


def _modulated_norm(x, g, shift, scale):
    return _rmsnorm(x, g) * (1 + scale[:, None, :]) + shift[:, None, :]


def _rotary(x, pos):
    half = x.shape[-1] // 2
    inv = 1.0 / (ROPE_BASE ** jnp.linspace(0.0, 1.0, half, dtype=jnp.float32))
    ang = pos.astype(jnp.float32)[:, None] * inv[None, :]
    cos = jnp.cos(ang)[None, :, None, :]
    sin = jnp.sin(ang)[None, :, None, :]
    x1, x2 = x[..., :half], x[..., half:]
    return jnp.concatenate([x1 * cos - x2 * sin, x1 * sin + x2 * cos], axis=-1)


def _log_gamma():
    return jnp.log1p(-(2.0 ** (-5.0 - jnp.arange(H_A, dtype=jnp.float32))))


def _retention_chunk(s, q, k, v, log_g):
    L = q.shape[1]
    i = jnp.arange(L, dtype=jnp.float32)
    diff = i[:, None] - i[None, :]
    decay = jnp.where(diff >= 0, jnp.exp(jnp.maximum(diff, 0.0)[None] * log_g[:, None, None]), 0.0)
    scores = jnp.einsum('blhd,bmhd->bhlm', q, k) * decay[None]
    inner = jnp.einsum('bhlm,bmhe->blhe', scores, v)
    cross = jnp.einsum('blhd,bhde->blhe', q, s) * jnp.exp((i + 1.0)[:, None] * log_g[None, :])[None, :, :, None]
    wk = jnp.exp((L - 1.0 - i)[:, None] * log_g[None, :])
    s_new = jnp.exp(L * log_g)[None, :, None, None] * s + jnp.einsum('blhd,blhe->bhde', k * wk[None, :, :, None], v)
    return s_new, inner + cross


def _retention(q, k, v, s0):
    B, T = q.shape[:2]
    C = math.gcd(T, RET_CHUNK)
    nc = T // C
    log_g = _log_gamma()

    def to_chunks(a):
        return a.reshape(B, nc, C, *a.shape[2:]).swapaxes(0, 1)

    def step(s, qkv):
        return _retention_chunk(s, qkv[0], qkv[1], qkv[2], log_g)

    s_new, o = lax.scan(step, s0, (to_chunks(q), to_chunks(k), to_chunks(v)))
    return o.swapaxes(0, 1).reshape(B, T, H_A, DV_A), s_new


def _retention_layer(h, pos, s0, w_in, w_out):
    B, T, _ = h.shape
    q, k, v, g = jnp.split(h @ w_in, [QK_A, 2 * QK_A, 2 * QK_A + V_A], axis=-1)
    q = _rotary(q.reshape(B, T, H_A, DK_A).astype(jnp.float32), pos)
    k = _rotary(k.reshape(B, T, H_A, DK_A).astype(jnp.float32), pos) * (DK_A ** -0.5)
    v = v.reshape(B, T, H_A, DV_A).astype(jnp.float32)
    o, s_new = _retention(q, k, v, s0.astype(jnp.float32))
    o = o * lax.rsqrt(jnp.mean(o * o, axis=-1, keepdims=True) + EPS)
    o = o.reshape(B, T, V_A).astype(h.dtype)
    return (jax.nn.silu(g) * o) @ w_out, s_new.astype(s0.dtype)


def _moba_attend(q, q_pos, k_full, v_full):
    B, T, H, DH = q.shape
    Lk = k_full.shape[1]
    nb = -(-Lk // MOBA_BLOCK)
    pad = nb * MOBA_BLOCK - Lk

    def to_blocks(a):
        a = jnp.pad(a, ((0, 0), (0, pad), (0, 0), (0, 0)))
        return a.reshape(B, nb, MOBA_BLOCK, H, DH).transpose(0, 3, 1, 2, 4)

    kb, vb = to_blocks(k_full), to_blocks(v_full)
    means = jnp.mean(kb.astype(jnp.float32), axis=3)
    n_sel = min(MOBA_TOPK, nb)
    qc_len = math.gcd(T, Q_CHUNK)
    nq = T // qc_len
    qc = q.reshape(B, nq, qc_len, H, DH).transpose(1, 0, 3, 2, 4)
    pc = q_pos.reshape(nq, qc_len)
    bi = jnp.arange(B)[:, None, None, None]
    hi = jnp.arange(H)[None, :, None, None]
    offs = jnp.arange(MOBA_BLOCK, dtype=jnp.int32)
    scale = DH ** -0.5

    def attend(args):
        qi, pi = args
        own = pi // MOBA_BLOCK
        gate = jnp.einsum('bhqd,bhnd->bhqn', qi.astype(jnp.float32), means)
        is_past = jnp.arange(nb, dtype=jnp.int32)[None, :] < own[:, None]
        gate = jnp.where(is_past, gate, -jnp.inf)
        _, sel = lax.top_k(gate, n_sel)
        sel = sel.astype(jnp.int32)
        sel_ok = sel < own[:, None]
        own_b = jnp.broadcast_to(own[:, None], (B, H, qc_len, 1)).astype(jnp.int32)
        blocks = jnp.concatenate([sel, own_b], axis=-1)
        blk_ok = jnp.concatenate([sel_ok, jnp.ones(own_b.shape, dtype=bool)], axis=-1)
        kg = kb[bi, hi, blocks].reshape(B, H, qc_len, -1, DH)
        vg = vb[bi, hi, blocks].reshape(B, H, qc_len, -1, DH)
        key_pos = blocks[..., None] * MOBA_BLOCK + offs
        ok = (blk_ok[..., None] & (key_pos <= pi[None, None, :, None, None])).reshape(B, H, qc_len, -1)
        logits = jnp.einsum('bhqd,bhqkd->bhqk', qi, kg, preferred_element_type=jnp.float32) * scale
        p = jax.nn.softmax(jnp.where(ok, logits, -jnp.inf), axis=-1)
        return jnp.einsum('bhqk,bhqkd->bhqd', p.astype(vg.dtype), vg)

    o = lax.map(attend, (qc, pc))
    return o.transpose(1, 0, 3, 2, 4).reshape(B, T, H, DH)


def _moba_layer(h, pos, k_full, v_full, w_q, w_o):
    B, T, _ = h.shape
    q, g = jnp.split(h @ w_q, 2, axis=-1)
    o = _moba_attend(q.reshape(B, T, H_B, DH_B), pos, k_full, v_full).reshape(B, T, D_MODEL)
    return (jax.nn.silu(g) * o) @ w_o


def _run_group(x, c, s_in, past_k, past_v, norm_g, w_mod, b_mod, w_in_a, w_out_a, w_q_b, w_o_b,
               kv_norm_g, w_mod_kv, b_mod_kv, w_kv, final_g, w_mod_f, b_mod_f):
    B, T, _ = x.shape
    past = past_k.shape[1]
    pos = past + jnp.arange(T, dtype=jnp.int32)
    new_s = []
    k_new = v_new = k_full = v_full = None
    for l in range(DEPTH):
        shift, scale, gate = jnp.split(c @ w_mod[l] + b_mod[l], 3, axis=-1)
        h = _modulated_norm(x, norm_g[l], shift, scale)
        if l < N_A:
            out, s = _retention_layer(h, pos, s_in[l], w_in_a[l], w_out_a[l])
            new_s.append(s)
        else:
            out = _moba_layer(h, pos, k_full, v_full, w_q_b[l - N_A], w_o_b[l - N_A])
        x = x + gate[:, None, :] * out
        if l == N_A - 1:
            kv_shift, kv_scale = jnp.split(c @ w_mod_kv + b_mod_kv, 2, axis=-1)
            hk = _modulated_norm(x, kv_norm_g, kv_shift, kv_scale)
            k_new, v_new = jnp.split(hk @ w_kv, 2, axis=-1)
            k_new = k_new.reshape(B, T, H_B, DH_B)
            v_new = v_new.reshape(B, T, H_B, DH_B)
            k_full = jnp.concatenate([past_k.astype(k_new.dtype), k_new], axis=1)
            v_full = jnp.concatenate([past_v.astype(v_new.dtype), v_new], axis=1)
    f_shift, f_scale = jnp.split(c @ w_mod_f + b_mod_f, 2, axis=-1)
    y = _modulated_norm(x, final_g, f_shift, f_scale)
    return y, jnp.stack(new_s), k_new, v_new


def setup_inputs(seed: int = 0) -> dict:
    key = jax.random.key(seed)
    ks = jax.random.split(key, 24)
    f32 = jnp.float32
    n_pages = PAST_LEN // PAGE_SIZE
    n_used = DEC_BATCH * n_pages
    n_pool = n_used + max(1, n_used // 4)
    D = D_MODEL

    def nrm(k, shape, s):
        return jax.random.normal(k, shape, f32) * s

    page_table = jax.random.permutation(ks[0], n_pool)[:n_used].reshape(DEC_BATCH, n_pages).astype(jnp.int32)
    mod_s = 0.5 * D ** -0.5
    return {
        'x_prompt': nrm(ks[1], (BATCH, SEQ, D), 1.0),
        'x_sample': nrm(ks[2], (DEC_BATCH, DEC_SEQ, D), 1.0),
        'state_ret': nrm(ks[3], (N_A, DEC_BATCH, H_A, DK_A, DV_A), 0.5),
        'cache_k': nrm(ks[4], (n_pool, PAGE_SIZE, H_B, DH_B), 1.0),
        'cache_v': nrm(ks[5], (n_pool, PAGE_SIZE, H_B, DH_B), 1.0),
        'page_table': page_table,
        'c_prompt': nrm(ks[6], (BATCH, D), 1.0),
        'c_sample': nrm(ks[7], (DEC_BATCH, D), 1.0),
        'norm_g': 1.0 + nrm(ks[8], (DEPTH, D), 0.02),
        'w_mod': nrm(ks[9], (DEPTH, D, 3 * D), mod_s),
        'b_mod': nrm(ks[10], (DEPTH, 3 * D), 0.02),
        'w_in_a': nrm(ks[11], (N_A, D, 2 * QK_A + 2 * V_A), D ** -0.5),
        'w_out_a': nrm(ks[12], (N_A, V_A, D), V_A ** -0.5),
        'w_q_b': nrm(ks[13], (N_B, D, 2 * D), D ** -0.5),
        'w_o_b': nrm(ks[14], (N_B, D, D), D ** -0.5),
        'kv_norm_g': 1.0 + nrm(ks[15], (D,), 0.02),
        'w_mod_kv': nrm(ks[16], (D, 2 * D), mod_s),
        'b_mod_kv': nrm(ks[17], (2 * D,), 0.02),
        'w_kv': nrm(ks[18], (D, 2 * H_B * DH_B), D ** -0.5),
        'final_g': 1.0 + nrm(ks[19], (D,), 0.02),
        'w_mod_f': nrm(ks[20], (D, 2 * D), mod_s),
        'b_mod_f': nrm(ks[21], (2 * D,), 0.02),
    }


def reference(x_prompt, x_sample, state_ret, cache_k, cache_v, page_table, c_prompt, c_sample,
              norm_g, w_mod, b_mod, w_in_a, w_out_a, w_q_b, w_o_b,
              kv_norm_g, w_mod_kv, b_mod_kv, w_kv, final_g, w_mod_f, b_mod_f):
    Bp = x_prompt.shape[0]
    s0_prompt = jnp.zeros((N_A, Bp, H_A, DK_A, DV_A), state_ret.dtype)
    empty_k = jnp.zeros((Bp, 0, H_B, DH_B), cache_k.dtype)
    y_prompt, state_ret_prompt, k_prompt, v_prompt = _run_group(
        x_prompt, c_prompt, s0_prompt, empty_k, empty_k,
        norm_g, w_mod, b_mod, w_in_a, w_out_a, w_q_b, w_o_b,
        kv_norm_g, w_mod_kv, b_mod_kv, w_kv, final_g, w_mod_f, b_mod_f)
    Bd, n_pages = page_table.shape
    past_len = n_pages * cache_k.shape[1]
    past_k = cache_k[page_table].reshape(Bd, past_len, H_B, DH_B)
    past_v = cache_v[page_table].reshape(Bd, past_len, H_B, DH_B)
    y_sample, state_ret_sample, k_sample, v_sample = _run_group(
        x_sample, c_sample, state_ret, past_k, past_v,
        norm_g, w_mod, b_mod, w_in_a, w_out_a, w_q_b, w_o_b,
        kv_norm_g, w_mod_kv, b_mod_kv, w_kv, final_g, w_mod_f, b_mod_f)
    return (y_prompt, y_sample, state_ret_prompt, state_ret_sample, k_prompt, v_prompt, k_sample, v_sample)
```

```python
import contextlib
import math
import numpy as np
import ml_dtypes
import concourse.bass as bass
import concourse.mybir as mybir
from concourse.bass_utils import run_bass_kernel_spmd

F32 = mybir.dt.float32
BF16 = mybir.dt.bfloat16
I32 = mybir.dt.int32
AF = mybir.ActivationFunctionType
ALU = mybir.AluOpType
AX = mybir.AxisListType

D = 2048
SEQ = 2048
TW = 512
NSEG = 4
H_A = 8
H_B = 16
EPS = 1e-6
NPAGES = 128
SAME_ENGINE_RAW_SYNC = True


class Res:
    __slots__ = ("name", "last_w", "readers", "dsem", "dcnt", "last_dma")

    def __init__(self, name):
        self.name = name
        self.last_w = None
        self.readers = []
        self.dsem = None
        self.dcnt = 0
        self.last_dma = None


class Op:
    __slots__ = ("eng", "fn", "is_dma", "deps", "marked", "seq", "dres", "dval")

    def __init__(self, eng, fn, is_dma):
        self.eng = eng
        self.fn = fn
        self.is_dma = is_dma
        self.deps = []
        self.marked = False
        self.seq = 0
        self.dres = None
        self.dval = 0


class Sched:
    ENGS = ("pe", "act", "dve", "pool", "sp")

    def __init__(self, nc):
        self.nc = nc
        self.streams = {e: [] for e in self.ENGS}
        self.dma_res = []

    def add(self, eng, fn, reads=(), writes=(), dma=None, extra_deps=()):
        op = Op(eng, fn, dma is not None)
        deps = {}
        for r in reads:
            lw = r.last_w
            if lw is not None:
                deps[id(lw)] = (lw, True)
        for w in writes:
            lw = w.last_w
            if lw is not None and id(lw) not in deps:
                deps[id(lw)] = (lw, False)
            for rd in w.readers:
                if id(rd) not in deps:
                    deps[id(rd)] = (rd, False)
        for d in extra_deps:
            deps[id(d)] = (d, True)
        for d, raw in deps.values():
            if d is op:
                continue
            if (not d.is_dma) and d.eng == eng:
                if eng == "pe" or eng == "sp":
                    continue
                if not (raw and SAME_ENGINE_RAW_SYNC):
                    continue
            op.deps.append(d)
            d.marked = True
        for r in reads:
            r.readers.append(op)
        for w in writes:
            w.last_w = op
            w.readers = []
        if dma is not None:
            if dma.dsem is None:
                self.dma_res.append(dma)
                dma.dsem = True
            dma.dcnt += 16
            dma.last_dma = op
            op.dres = dma
            op.dval = dma.dcnt
            op.marked = True
        self.streams[eng].append(op)
        return op

    def emit(self):
        nc = self.nc
        with contextlib.ExitStack() as es:
            esem = {e: es.enter_context(nc.semaphore("sem_" + e)) for e in self.ENGS}
            for r in self.dma_res:
                r.dsem = es.enter_context(nc.semaphore("d_" + r.name))
            for e in self.ENGS:
                n = 0
                for op in self.streams[e]:
                    if op.marked and not op.is_dma:
                        n += 1
                        op.seq = n
            block = es.enter_context(nc.Block())
            streams = self.streams
            dma_res = self.dma_res

            def run(e, eng):
                waited = {}
                for op in streams[e]:
                    need = {}
                    for d in op.deps:
                        if d.is_dma:
                            key = ("d", id(d.dres))
                            sem, val = d.dres.dsem, d.dval
                        else:
                            key = ("e", d.eng)
                            sem, val = esem[d.eng], d.seq
                        if waited.get(key, 0) >= val:
                            continue
                        if key not in need or need[key][1] < val:
                            need[key] = (sem, val)
                    for key, (sem, val) in need.items():
                        eng.wait_ge(sem, val)
                        waited[key] = val
                    inst = op.fn(eng)
                    if op.is_dma:
                        inst.then_inc(op.dres.dsem, 16)
                    elif op.marked:
                        inst.then_inc(esem[e], 1)
                if e == "sp":
                    for r in dma_res:
                        if waited.get(("d", id(r)), 0) < r.dcnt:
                            eng.wait_ge(r.dsem, r.dcnt)

            @block.tensor
            def _(eng):
                run("pe", eng)

            @block.scalar
            def _(eng):
                run("act", eng)

            @block.vector
            def _(eng):
                run("dve", eng)

            @block.gpsimd
            def _(eng):
                run("pool", eng)

            @block.sync
            def _(eng):
                run("sp", eng)


def _gammas():
    return [1.0 - 2.0 ** (-5.0 - h) for h in range(H_A)]


def _rot_tables(pos, L):
    n = len(pos)
    inv = (1.0 / (10000.0 ** np.linspace(0.0, 1.0, 128, dtype=np.float32))).astype(np.float32)
    ang = (pos.astype(np.float32)[None, :] * inv[:, None]).astype(np.float32)
    cos = np.cos(ang).astype(np.float64)
    sin = np.sin(ang).astype(np.float64)
    out = np.zeros((H_A, 4, 128, n), np.float32)
    i = (np.arange(n) % L).astype(np.float64)
    for h, g in enumerate(_gammas()):
        fq = g ** (i + 1.0)
        fk = g ** (-(i + 1.0)) / 16.0
        out[h, 0] = cos * fq
        out[h, 1] = sin * fq
        out[h, 2] = cos * fk
        out[h, 3] = sin * fk
    return out


def _consts():
    c = {}
    c["ident_bf"] = np.eye(128, dtype=np.float32).astype(ml_dtypes.bfloat16)
    c["ident_f"] = np.eye(128, dtype=np.float32)
    m = np.arange(128)
    c["caus_ml"] = (m[:, None] <= m[None, :]).astype(np.float32)
    c["tri_qk"] = (m[None, :] <= m[:, None]).astype(np.float32)
    rot = np.zeros((NSEG + 1, H_A, 4, 128, TW), np.float32)
    for s in range(NSEG):
        rot[s] = _rot_tables(np.arange(s * TW, (s + 1) * TW), 128)
    rot[NSEG, :, :, :, :4] = _rot_tables(np.arange(16384, 16388), 4)
    c["rot"] = rot
    wk = np.zeros((2, 128, H_A), np.float32)
    for h, g in enumerate(_gammas()):
        wk[0, :, h] = g ** 128.0
        wk[1, :4, h] = g ** 4.0
    c["wk"] = wk
    return c


class Prog:
    def __init__(self, n_pool, do_sample=True, nseg=NSEG, debug=False, ns=2):
        self.ns = ns
        self.n_pool = n_pool
        self.cur_es = None
        self.do_sample = do_sample
        self.nseg = nseg
        self.nc = nc = bass.Bass("TRN2", target_bir_lowering=False)
        self.es = contextlib.ExitStack()
        self.S = Sched(nc)
        self.wctr = 0
        self.pctr = 0
        self.tctr = 0

        def din(name, shape, dt=F32):
            return nc.dram_tensor(name, list(shape), dt, kind="ExternalInput").ap()

        def dout(name, shape, dt=F32):
            return nc.dram_tensor(name, list(shape), dt, kind="ExternalOutput").ap()

        def dscr(name, shape, dt=F32):
            return nc.dram_tensor(name, list(shape), dt, kind="Internal").ap()

        self.xp = din("xp", [SEQ, D])
        self.cp = din("cp", [D])
        self.norm_g = din("norm_g", [4, D])
        self.w_mod = din("w_mod", [4, D, 3 * D])
        self.b_mod = din("b_mod", [4, 3 * D])
        self.w_in_a = din("w_in_a", [2, D, 12288])
        self.w_out_a = din("w_out_a", [2, 4096, D])
        self.w_q_b = din("w_q_b", [2, D, 2 * D])
        self.w_o_b = din("w_o_b", [2, D, D])
        self.kv_norm_g = din("kv_norm_g", [D])
        self.w_mod_kv = din("w_mod_kv", [D, 2 * D])
        self.b_mod_kv = din("b_mod_kv", [2 * D])
        self.w_kv = din("w_kv", [D, 2 * D])
        self.final_g = din("final_g", [D])
        self.w_mod_f = din("w_mod_f", [D, 2 * D])
        self.b_mod_f = din("b_mod_f", [2 * D])
        self.c_ident_bf = din("ident_bf", [128, 128], BF16)
        self.c_ident_f = din("ident_f", [128, 128])
        self.c_caus_ml = din("caus_ml", [128, 128])
        self.c_tri_qk = din("tri_qk", [128, 128])
        self.c_rot = din("rot", [NSEG + 1, H_A, 4, 128, TW])
        self.c_wk = din("wk", [2, 128, H_A])
        if do_sample:
            self.xs = din("xs", [ns, 4, D])
            self.cs = din("cs", [ns, D])
            self.st_in = din("st_in", [ns, 2, H_A, 256, 512])
            self.c_onehot = din("onehot", [4, 4, 128], BF16)
            self.c_blockdiag = din("blockdiag", [64, D], BF16)
            self.c_selq = din("selq", [64, 4], BF16)
            self.c_negmaskn = din("negmaskn", [4, 64])
            self.c_piota = din("piota", [128, 1])
            self.cache_k = din("cache_k", [n_pool * 128, D])
            self.cache_v = din("cache_v", [n_pool * 128, D])
            self.ptab = din("ptab", [ns, NPAGES], I32)

        self.y_p = dout("y_p", [SEQ, D])
        self.stp = dout("stp", [2, H_A, 256, 512])
        self.k_p = dout("k_p", [SEQ, D])
        self.v_p = dout("v_p", [SEQ, D])
        if do_sample:
            self.y_s = dout("y_s", [ns, 4, D])
            self.sts = dout("sts", [ns, 2, H_A, 256, 512])
            self.k_s = dout("k_s", [ns, 4, D])
            self.v_s = dout("v_s", [ns, 4, D])

        self.NMOD = 4 * 3 * D + 2 * 2 * D
        self.mod_scr = dscr("mod_scr", [3, self.NMOD])
        self.bar_scr = dscr("bar_scr", [128, 1])
        self.r_mod_scr = Res("mod_scr")
        self.st_scr = dscr("st_scr", [2, H_A, 256, 512])
        self.r_st_scr = [[Res(f"st_scr{l}_{h}") for h in range(H_A)] for l in range(2)]
        self.kt_scr = dscr("kt_scr", [H_B, 128, SEQ], BF16)
        self.v_scr = dscr("v_scr", [SEQ, D], BF16)
        self.r_kt_scr = Res("kt_scr")
        self.r_v_scr = Res("v_scr")
        self.r_out = Res("outputs")

    def sb(self, name, shape, dt):
        es = self.cur_es if self.cur_es is not None else self.es
        return es.enter_context(self.nc.sbuf_tensor("sb_" + name, list(shape), dt))

    def ps(self, name, shape, dt):
        return self.es.enter_context(self.nc.psum_tensor(name, list(shape), dt))

    def alloc_common(self):
        self.ident_bf = self.sb("ident_bf_t", [128, 128], BF16)
        self.ident_f = self.sb("ident_f_t", [128, 128], F32)
        self.caus_ml = self.sb("caus_ml_t", [128, 128], F32)
        self.tri_qk = self.sb("tri_qk_t", [128, 128], F32)
        self.wk_t = self.sb("wk_t", [128, 2, H_A], F32)
        self.r_const = Res("const")
        S = self.S
        self.bar_t = self.sb("bar_t", [128, 4], F32)
        self.r_bar = [Res(f"bar{i}") for i in range(4)]
        self.eps_t = self.sb("eps_t", [128, 1], F32)
        S.add("dve", lambda e: e.memset(self.eps_t[:], EPS), writes=[self.r_const])
        S.add("sp", lambda e: e.dma_start(out=self.ident_bf[:], in_=self.c_ident_bf), writes=[self.r_const], dma=self.r_const)
        S.add("sp", lambda e: e.dma_start(out=self.ident_f[:], in_=self.c_ident_f), writes=[self.r_const], dma=self.r_const)
        S.add("sp", lambda e: e.dma_start(out=self.caus_ml[:], in_=self.c_caus_ml), writes=[self.r_const], dma=self.r_const)
        S.add("sp", lambda e: e.dma_start(out=self.tri_qk[:], in_=self.c_tri_qk), writes=[self.r_const], dma=self.r_const)
        S.add("sp", lambda e: e.dma_start(out=self.wk_t[:], in_=self.c_wk.rearrange("a p h -> p a h")), writes=[self.r_const], dma=self.r_const)
        self.NSLOT = 6
        self.wslot = [self.sb(f"wslot{i}", [128, 2048], BF16) for i in range(self.NSLOT)]
        self.r_w = [Res(f"wslot{i}") for i in range(self.NSLOT)]
        self.NPS = 6
        self.psf = [self.ps(f"psf{i}", [128, 512], F32) for i in range(self.NPS)]
        self.r_psf = [Res(f"psf{i}") for i in range(self.NPS)]
        self.psb = [self.ps(f"psb{i}", [128, 1024], BF16) for i in range(2)]
        self.r_psb = [Res(f"psb{i}") for i in range(2)]

    def barrier(self):
        S = self.S
        lasts = [S.streams[e][-1] for e in S.ENGS if S.streams[e]]
        dmas = [r.last_dma for r in S.dma_res if r.last_dma is not None]
        deps = lasts + dmas
        bt = self.bar_t
        S.add("dve", lambda e: e.memset(bt[:, 0:1], 0.0), writes=[self.r_bar[0]], extra_deps=deps)
        S.add("act", lambda e: e.activation(out=bt[:, 1:2], in_=self.eps_t[:, 0:1], func=AF.Copy), writes=[self.r_bar[1]], extra_deps=deps)
        S.add("pool", lambda e: e.memset(bt[:, 2:3], 0.0), writes=[self.r_bar[2]], extra_deps=deps)
        S.add("pe", lambda e: e.matmul(self.psf[0][0:1, 0:1], lhsT=self.ident_bf[0:1, 0:1], rhs=self.ident_bf[0:1, 0:1], start=True, stop=True),
              writes=[self.r_psf[0]], extra_deps=deps)
        S.add("sp", lambda e: e.dma_start(out=self.bar_scr, in_=self.eps_t[:, 0:1]), writes=[self.r_bar[3]], dma=self.r_bar[3], extra_deps=deps)

    def next_slot(self):
        i = self.wctr % self.NSLOT
        self.wctr += 1
        return i

    def next_ps(self):
        i = self.pctr % self.NPS
        self.pctr += 1
        return i

    def next_psb(self):
        i = self.tctr % 2
        self.tctr += 1
        return i

    def wload_T(self, W, c0):
        i = self.next_slot()
        src = W.rearrange("(k p) n -> p k n", p=128)[:, :, c0:c0 + 128]
        dst = self.wslot[i][:].rearrange("p (k n) -> p k n", n=128)
        self.S.add("pool", lambda e: e.dma_start(out=dst, in_=src), writes=[self.r_w[i]], dma=self.r_w[i])
        return i

    def wload_R(self, W, r0, c0):
        i = self.next_slot()
        src = W[r0:r0 + 512, c0:c0 + 512].rearrange("(j p) n -> p j n", p=128)
        dst = self.wslot[i][:].rearrange("p (j n) -> p j n", n=512)
        self.S.add("pool", lambda e: e.dma_start(out=dst, in_=src), writes=[self.r_w[i]], dma=self.r_w[i])
        return i

    def slotT(self, i):
        return self.wslot[i][:].rearrange("p (k n) -> p k n", n=128)

    def slotR(self, i):
        return self.wslot[i][:].rearrange("p (j n) -> p j n", n=512)

    def prepass_mods(self):
        S = self.S
        with contextlib.ExitStack() as es:
            nc = self.nc
            ccol = es.enter_context(nc.sbuf_tensor("ccol", [128, 3, 16], F32))
            cT = es.enter_context(nc.sbuf_tensor("cT", [128, 16, 128], BF16))
            brep = es.enter_context(nc.sbuf_tensor("brep", [128, 128], F32))
            mrow = [es.enter_context(nc.sbuf_tensor(f"mrow{i}", [128, 128], F32)) for i in range(2)]
            r_ccol, r_cT, r_brep = Res("ccol"), Res("cT"), Res("brep")
            r_mrow = [Res("mrow0"), Res("mrow1")]
            S.add("sp", lambda e: e.dma_start(out=ccol[:, 0, :], in_=self.cp.rearrange("(k p) -> p k", p=128),
                                              allow_slow_non_contiguous=True),
                  writes=[r_ccol], dma=r_ccol)
            S.add("dve", lambda e: e.memset(ccol[:, 1:3, :], 0.0), writes=[r_ccol])
            if self.do_sample:
                for si in range(self.ns):
                    S.add("sp", lambda e, si=si: e.dma_start(out=ccol[:, 1 + si, :], in_=self.cs[si].rearrange("(k p) -> p k", p=128),
                                                             allow_slow_non_contiguous=True),
                          writes=[r_ccol], dma=r_ccol)
            for g in range(4):
                S.add("dve", lambda e, g=g: e.tensor_copy(out=cT[:, :, g * 32:(g + 1) * 32],
                                                          in_=ccol[:, min(g, 2), :].unsqueeze(2).to_broadcast([128, 16, 32])),
                      reads=[r_ccol], writes=[r_cT])
            mats = [(self.w_mod[l], self.b_mod[l], 3 * D) for l in range(4)] + \
                   [(self.w_mod_kv, self.b_mod_kv, 2 * D), (self.w_mod_f, self.b_mod_f, 2 * D)]
            off = 0
            it = 0
            for W, b, n in mats:
                for c0 in range(0, n, 128):
                    si = self.wload_T(W, c0)
                    pi = self.next_ps()
                    for k in range(16):
                        S.add("pe", lambda e, k=k, si=si, pi=pi: e.matmul(self.psf[pi][:, 0:128], lhsT=cT[:, k, :], rhs=self.slotT(si)[:, k, :],
                                                                            start=(k == 0), stop=(k == 15)),
                              reads=[r_cT, self.r_w[si]], writes=[self.r_psf[pi]])
                    S.add("sp", lambda e, b=b, c0=c0: e.dma_start(out=brep[:], in_=b[c0:c0 + 128].partition_broadcast(128)),
                          writes=[r_brep], dma=r_brep)
                    mi = it % 2
                    S.add("dve", lambda e, pi=pi, mi=mi: e.tensor_tensor(out=mrow[mi][:], in0=self.psf[pi][:, 0:128], in1=brep[:], op=ALU.add),
                          reads=[self.r_psf[pi], r_brep], writes=[r_mrow[mi]])
                    o = off + c0
                    S.add("sp", lambda e, mi=mi, o=o: e.dma_start(out=self.mod_scr[0:1, o:o + 128], in_=mrow[mi][0:1, :]),
                          reads=[r_mrow[mi]], writes=[self.r_mod_scr], dma=r_mrow[mi])
                    S.add("sp", lambda e, mi=mi, o=o: e.dma_start(out=self.mod_scr[1:2, o:o + 128], in_=mrow[mi][32:33, :]),
                          reads=[r_mrow[mi]], writes=[self.r_mod_scr], dma=r_mrow[mi])
                    S.add("sp", lambda e, mi=mi, o=o: e.dma_start(out=self.mod_scr[2:3, o:o + 128], in_=mrow[mi][64:65, :]),
                          reads=[r_mrow[mi]], writes=[self.r_mod_scr], dma=r_mrow[mi])
                    it += 1
                off += n

    def alloc_pass(self, tag, TT, NT):
        B = type("B", (), {})()
        B.TT, B.NT = TT, NT
        B.x = self.sb(f"x_{tag}", [128, TT, D], F32)
        B.r_x = [Res(f"x_{tag}{t}") for t in range(TT)]
        B.hT = self.sb(f"hT_{tag}", [128, 16, NT], BF16)
        B.r_hT = Res(f"hT_{tag}")
        B.a_rep = self.sb(f"a_rep_{tag}", [128, D], F32)
        B.sh_rep = self.sb(f"sh_rep_{tag}", [128, D], F32)
        B.gate_rep = self.sb(f"gate_rep_{tag}", [128, D], F32)
        B.r_a, B.r_sh, B.r_gate = Res("a_rep" + tag), Res("sh_rep" + tag), Res("gate_rep" + tag)
        B.ntmp = self.sb(f"ntmp_{tag}", [128, D], F32)
        B.r_ntmp = Res("ntmp" + tag)
        B.hb = self.sb(f"hb_{tag}", [128, D], BF16)
        B.r_hb = Res("hb" + tag)
        B.ss = self.sb(f"ss_{tag}", [128, 8], F32)
        B.r_ss = Res("ss" + tag)
        B.qT = self.sb(f"qT_{tag}", [128, 2, NT], BF16)
        B.kT = self.sb(f"kT_{tag}", [128, 2, NT], BF16)
        B.r_qT, B.r_kT = Res("qT" + tag), Res("kT" + tag)
        B.kw = self.sb(f"kw_{tag}", [128, TT, 256], BF16)
        B.r_kw = Res("kw" + tag)
        B.vT = self.sb(f"vT_{tag}", [128, 4, NT], BF16)
        B.r_vT = Res("vT" + tag)
        B.v = self.sb(f"v_{tag}", [128, TT, 512], BF16)
        B.r_v = Res("v" + tag)
        B.sgT = self.sb(f"sgT_{tag}", [128, 16, NT], BF16)
        B.r_sgT = Res("sgT" + tag)
        B.gT = self.sb(f"gT_{tag}", [128, 16, NT], BF16)
        B.r_gT = Res("gT" + tag)
        B.St = self.sb(f"St_{tag}", [128, 2, 512], F32)
        B.Sb = self.sb(f"Sb_{tag}", [128, 2, 512], BF16)
        B.r_St, B.r_Sb = Res("St" + tag), Res("Sb" + tag)
        B.rot = self.sb(f"rot_{tag}", [128, 4, NT], F32)
        B.r_rot = Res("rot" + tag)
        B.rt = self.sb(f"rt_{tag}", [128, 2, NT], F32)
        B.r_rt = Res("rt" + tag)
        B.AT = self.sb(f"AT_{tag}", [128, 128], BF16)
        B.r_AT = Res("AT" + tag)
        B.onb = self.sb(f"onb_{tag}", [128, 512], BF16)
        B.r_onb = Res("onb" + tag)
        return B

    def load_rep(self, B, dst, r_dst, row, off, n=D):
        self.S.add("sp", lambda e: e.dma_start(out=dst[:, 0:n], in_=self.mod_scr[row, off:off + n].partition_broadcast(128)),
                   reads=[self.r_mod_scr], writes=[r_dst], dma=r_dst)

    def mod_norm(self, B, row, g_dram, off_shift, off_scale, rows_last):
        S = self.S
        TT = B.TT
        self.load_rep(B, B.sh_rep, B.r_sh, row, off_shift)
        self.load_rep(B, B.a_rep, B.r_a, row, off_scale)
        S.add("sp", lambda e: e.dma_start(out=B.ntmp[:], in_=g_dram.partition_broadcast(128)), writes=[B.r_ntmp], dma=B.r_ntmp)
        S.add("dve", lambda e: e.scalar_tensor_tensor(out=B.a_rep[:], in0=B.a_rep[:], scalar=1.0, in1=B.ntmp[:], op0=ALU.add, op1=ALU.mult),
              reads=[B.r_a, B.r_ntmp], writes=[B.r_a])
        for t in range(TT):
            R = rows_last if t == TT - 1 else 128
            S.add("dve", lambda e, R=R: e.memset(B.ss[:R, 0:1], 0.0), writes=[B.r_ss])
            S.add("act", lambda e, t=t, R=R: e.activation(out=B.ntmp[:R, :], in_=B.x[:R, t, :], func=AF.Square, accum_out=B.ss[:R, 0:1]),
                  reads=[B.r_x[t], B.r_ss], writes=[B.r_ntmp, B.r_ss])
            S.add("act", lambda e, R=R: e.activation(out=B.ss[:R, 1:2], in_=B.ss[:R, 0:1], func=AF.Sqrt, scale=1.0 / D, bias=self.eps_t[:R, 0:1]),
                  reads=[B.r_ss, self.r_const], writes=[B.r_ss])
            S.add("dve", lambda e, R=R: e.reciprocal(out=B.ss[:R, 2:3], in_=B.ss[:R, 1:2]),
                  reads=[B.r_ss], writes=[B.r_ss])
            S.add("dve", lambda e, t=t, R=R: e.scalar_tensor_tensor(out=B.ntmp[:R, :], in0=B.x[:R, t, :], scalar=B.ss[:R, 2:3], in1=B.a_rep[:R, :],
                                                                     op0=ALU.mult, op1=ALU.mult),
                  reads=[B.r_x[t], B.r_ss, B.r_a], writes=[B.r_ntmp])
            S.add("dve", lambda e, R=R: e.tensor_tensor(out=B.hb[:R, :], in0=B.ntmp[:R, :], in1=B.sh_rep[:R, :], op=ALU.add),
                  reads=[B.r_ntmp, B.r_sh], writes=[B.r_hb])
            for g in range(2):
                pb = self.next_psb()
                for j in range(8):
                    k = g * 8 + j
                    S.add("pe", lambda e, k=k, j=j, pb=pb, R=R: e.transpose(out=self.psb[pb][:, j * 128:j * 128 + R], in_=B.hb[:R, k * 128:(k + 1) * 128],
                                                                            identity=self.ident_bf[:R, :R]),
                          reads=[B.r_hb, self.r_const], writes=[self.r_psb[pb]])
                S.add("act", lambda e, g=g, pb=pb, t=t, R=R: e.activation(
                    out=B.hT[:, g * 8:(g + 1) * 8, t * 128:t * 128 + R],
                    in_=self.psb[pb][:].rearrange("p (j n) -> p j n", n=128)[:, :, 0:R], func=AF.Copy),
                    reads=[self.r_psb[pb]], writes=[B.r_hT])

    def proj_T(self, B, W, c0, evac):
        S = self.S
        si = self.wload_T(W, c0)
        pi = self.next_ps()
        for k in range(16):
            S.add("pe", lambda e, k=k, si=si, pi=pi: e.matmul(self.psf[pi][:, 0:B.NT], lhsT=self.slotT(si)[:, k, :], rhs=B.hT[:, k, :],
                                                                start=(k == 0), stop=(k == 15)),
                  reads=[self.r_w[si], B.r_hT], writes=[self.r_psf[pi]])
        evac(pi)

    def residual_add(self, B, t, R, cb, pi):
        S = self.S
        S.add("dve", lambda e: e.tensor_tensor(out=B.ntmp[:R, cb * 512:(cb + 1) * 512], in0=self.psf[pi][:R, :], in1=B.gate_rep[:R, cb * 512:(cb + 1) * 512],
                                               op=ALU.mult),
              reads=[self.r_psf[pi], B.r_gate], writes=[B.r_ntmp])
        S.add("dve", lambda e: e.tensor_tensor(out=B.x[:R, t, cb * 512:(cb + 1) * 512], in0=B.x[:R, t, cb * 512:(cb + 1) * 512],
                                               in1=B.ntmp[:R, cb * 512:(cb + 1) * 512], op=ALU.add),
              reads=[B.r_ntmp, B.r_x[t]], writes=[B.r_x[t]])

    def retention_layer(self, B, l, pidx, row, rows_last, first, last, sample, st_in=None, sts=None):
        S = self.S
        TT, NT = B.TT, B.NT
        W = self.w_in_a[l]
        moff = l * 3 * D
        self.mod_norm(B, row, self.norm_g[l], moff, moff + D, rows_last)
        self.load_rep(B, B.gate_rep, B.r_gate, row, moff + 2 * D)
        gam = _gammas()
        L = rows_last if TT == 1 else 128
        for h in range(H_A):
            S.add("sp", lambda e, h=h: e.dma_start(out=B.rot[:], in_=self.c_rot[pidx, h, :, :, 0:NT].rearrange("a p n -> p a n")),
                  writes=[B.r_rot], dma=B.r_rot)
            for which, dstT, r_dst in ((0, B.qT, B.r_qT), (1, B.kT, B.r_kT)):
                pis = []
                for c in range(2):
                    self.proj_T(B, W, which * 2048 + h * 256 + c * 128, lambda pi: pis.append(pi))
                p1, p2 = pis
                cs_, sn_ = 2 * which, 2 * which + 1
                S.add("dve", lambda e, p1=p1, cs_=cs_: e.tensor_tensor(out=B.rt[:, 0, :], in0=self.psf[p1][:, 0:NT], in1=B.rot[:, cs_, :], op=ALU.mult),
                      reads=[self.r_psf[p1], B.r_rot], writes=[B.r_rt])
                S.add("dve", lambda e, p2=p2, sn_=sn_: e.tensor_tensor(out=B.rt[:, 1, :], in0=self.psf[p2][:, 0:NT], in1=B.rot[:, sn_, :], op=ALU.mult),
                      reads=[self.r_psf[p2], B.r_rot], writes=[B.r_rt])
                S.add("dve", lambda e, dstT=dstT: e.tensor_tensor(out=dstT[:, 0, :], in0=B.rt[:, 0, :], in1=B.rt[:, 1, :], op=ALU.subtract),
                      reads=[B.r_rt], writes=[r_dst])
                S.add("dve", lambda e, p1=p1, sn_=sn_: e.tensor_tensor(out=B.rt[:, 0, :], in0=self.psf[p1][:, 0:NT], in1=B.rot[:, sn_, :], op=ALU.mult),
                      reads=[self.r_psf[p1], B.r_rot, r_dst], writes=[B.r_rt])
                S.add("dve", lambda e, p2=p2, cs_=cs_: e.tensor_tensor(out=B.rt[:, 1, :], in0=self.psf[p2][:, 0:NT], in1=B.rot[:, cs_, :], op=ALU.mult),
                      reads=[self.r_psf[p2], B.r_rot], writes=[B.r_rt])
                S.add("dve", lambda e, dstT=dstT: e.tensor_tensor(out=dstT[:, 1, :], in0=B.rt[:, 0, :], in1=B.rt[:, 1, :], op=ALU.add),
                      reads=[B.r_rt], writes=[r_dst])
            for t in range(TT):
                R = rows_last if t == TT - 1 else 128
                pb = self.next_psb()
                for c in range(2):
                    S.add("pe", lambda e, c=c, pb=pb, t=t, R=R: e.transpose(out=self.psb[pb][:R, c * 128:(c + 1) * 128], in_=B.kT[:, c, t * 128:t * 128 + R],
                                                                            identity=self.ident_bf[:]),
                          reads=[B.r_kT, self.r_const], writes=[self.r_psb[pb]])
                S.add("act", lambda e, pb=pb, t=t, R=R, h=h: e.activation(out=B.kw[:R, t, :], in_=self.psb[pb][:R, 0:256], func=AF.Copy,
                                                                          scale=self.wk_t[:R, 1 if sample else 0, h:h + 1]),
                      reads=[self.r_psb[pb], self.r_const], writes=[B.r_kw])
            for j in range(4):
                self.proj_T(B, W, 4096 + h * 512 + j * 128,
                            lambda pi, j=j: S.add("act", lambda e: e.activation(out=B.vT[:, j, :], in_=self.psf[pi][:, 0:NT], func=AF.Copy),
                                                  reads=[self.r_psf[pi]], writes=[B.r_vT]))
            for t in range(TT):
                R = rows_last if t == TT - 1 else 128
                pb = self.next_psb()
                for j in range(4):
                    S.add("pe", lambda e, j=j, pb=pb, t=t, R=R: e.transpose(out=self.psb[pb][:R, j * 128:(j + 1) * 128], in_=B.vT[:, j, t * 128:t * 128 + R],
                                                                            identity=self.ident_bf[:]),
                          reads=[B.r_vT, self.r_const], writes=[self.r_psb[pb]])
                S.add("act", lambda e, pb=pb, t=t, R=R: e.activation(out=B.v[:R, t, :], in_=self.psb[pb][:R, 0:512], func=AF.Copy),
                      reads=[self.r_psb[pb]], writes=[B.r_v])
            for j in range(4):
                self.proj_T(B, W, 8192 + h * 512 + j * 128,
                            lambda pi, j=j: S.add("act", lambda e: e.activation(out=B.sgT[:, j, :], in_=self.psf[pi][:, 0:NT], func=AF.Silu),
                                                  reads=[self.r_psf[pi]], writes=[B.r_sgT]))
            r_scr = self.r_st_scr[l][h]
            if sample:
                S.add("sp", lambda e, h=h: e.dma_start(out=B.St[:], in_=st_in[l, h].rearrange("(c p) e -> p c e", p=128)),
                      writes=[B.r_St], dma=B.r_St)
            elif first:
                S.add("dve", lambda e: e.memset(B.St[:], 0.0), writes=[B.r_St])
            else:
                S.add("sp", lambda e, h=h: e.dma_start(out=B.St[:], in_=self.st_scr[l, h].rearrange("(c p) e -> p c e", p=128)),
                      reads=[r_scr], writes=[B.r_St], dma=B.r_St)
            S.add("act", lambda e: e.activation(out=B.Sb[:], in_=B.St[:], func=AF.Copy), reads=[B.r_St], writes=[B.r_Sb])
            gL = gam[h] ** L
            for t in range(TT):
                R = rows_last if t == TT - 1 else 128
                ts = slice(t * 128, t * 128 + R)
                pa = self.next_ps()
                for c in range(2):
                    S.add("pe", lambda e, c=c, pa=pa, ts=ts, R=R: e.matmul(self.psf[pa][:R, 0:R], lhsT=B.kT[:, c, ts], rhs=B.qT[:, c, ts], start=(c == 0), stop=(c == 1)),
                          reads=[B.r_kT, B.r_qT], writes=[self.r_psf[pa]])
                S.add("dve", lambda e, pa=pa, R=R: e.tensor_tensor(out=B.AT[:R, :R], in0=self.psf[pa][:R, 0:R], in1=self.caus_ml[:R, :R], op=ALU.mult),
                      reads=[self.r_psf[pa], self.r_const], writes=[B.r_AT])
                po = self.next_ps()
                for c in range(2):
                    S.add("pe", lambda e, c=c, po=po, ts=ts, R=R: e.matmul(self.psf[po][:R, :], lhsT=B.qT[:, c, ts], rhs=B.Sb[:, c, :], start=(c == 0), stop=False),
                          reads=[B.r_qT, B.r_Sb], writes=[self.r_psf[po]])
                S.add("pe", lambda e, po=po, t=t, R=R: e.matmul(self.psf[po][:R, :], lhsT=B.AT[:R, :R], rhs=B.v[:R, t, :], start=False, stop=True),
                      reads=[B.r_AT, B.r_v], writes=[self.r_psf[po]])
                S.add("dve", lambda e, R=R: e.memset(B.ss[:R, 4:5], 0.0), writes=[B.r_ss])
                S.add("act", lambda e, po=po, R=R: e.activation(out=B.ntmp[:R, 0:512], in_=self.psf[po][:R, :], func=AF.Square, accum_out=B.ss[:R, 4:5]),
                      reads=[self.r_psf[po], B.r_ss], writes=[B.r_ntmp, B.r_ss])
                S.add("act", lambda e, R=R: e.activation(out=B.ss[:R, 5:6], in_=B.ss[:R, 4:5], func=AF.Sqrt, scale=1.0 / 512, bias=self.eps_t[:R, 0:1]),
                      reads=[B.r_ss, self.r_const], writes=[B.r_ss])
                S.add("dve", lambda e, R=R: e.reciprocal(out=B.ss[:R, 6:7], in_=B.ss[:R, 5:6]),
                      reads=[B.r_ss], writes=[B.r_ss])
                S.add("dve", lambda e, po=po, R=R: e.tensor_scalar(out=B.onb[:R, :], in0=self.psf[po][:R, :], scalar1=B.ss[:R, 6:7], scalar2=None, op0=ALU.mult),
                      reads=[self.r_psf[po], B.r_ss], writes=[B.r_onb])
                pb = self.next_psb()
                for j in range(4):
                    S.add("pe", lambda e, j=j, pb=pb, R=R: e.transpose(out=self.psb[pb][:, j * 128:j * 128 + R], in_=B.onb[:R, j * 128:(j + 1) * 128],
                                                                       identity=self.ident_bf[:R, :R]),
                          reads=[B.r_onb, self.r_const], writes=[self.r_psb[pb]])
                S.add("dve", lambda e, pb=pb, ts=ts, R=R: e.tensor_tensor(out=B.gT[:, 0:4, ts],
                                                                          in0=self.psb[pb][:, 0:512].rearrange("p (j n) -> p j n", n=128)[:, :, 0:R],
                                                                          in1=B.sgT[:, 0:4, ts], op=ALU.mult),
                      reads=[self.r_psb[pb], B.r_sgT], writes=[B.r_gT])
                for c in range(2):
                    pss = self.next_ps()
                    S.add("pe", lambda e, c=c, pss=pss, t=t, R=R: e.matmul(self.psf[pss][:, :], lhsT=B.kw[:R, t, c * 128:(c + 1) * 128], rhs=B.v[:R, t, :], start=True, stop=True),
                          reads=[B.r_kw, B.r_v], writes=[self.r_psf[pss]])
                    S.add("dve", lambda e, c=c, pss=pss, gL=gL: e.scalar_tensor_tensor(out=B.St[:, c, :], in0=B.St[:, c, :], scalar=float(gL), in1=self.psf[pss][:, :],
                                                                               op0=ALU.mult, op1=ALU.add),
                          reads=[B.r_St, self.r_psf[pss]], writes=[B.r_St])
                if t < TT - 1:
                    S.add("act", lambda e: e.activation(out=B.Sb[:], in_=B.St[:], func=AF.Copy), reads=[B.r_St], writes=[B.r_Sb])
            if sample:
                S.add("sp", lambda e, h=h: e.dma_start(out=sts[l, h].rearrange("(c p) e -> p c e", p=128), in_=B.St[:]),
                      reads=[B.r_St], writes=[self.r_out], dma=B.r_St)
            elif last:
                S.add("sp", lambda e, h=h: e.dma_start(out=self.stp[l, h].rearrange("(c p) e -> p c e", p=128), in_=B.St[:]),
                      reads=[B.r_St], writes=[self.r_out], dma=B.r_St)
            else:
                S.add("sp", lambda e, h=h: e.dma_start(out=self.st_scr[l, h].rearrange("(c p) e -> p c e", p=128), in_=B.St[:]),
                      reads=[B.r_St], writes=[r_scr], dma=B.r_St)
            for cb in range(4):
                si = self.wload_R(self.w_out_a[l], h * 512, cb * 512)
                for t in range(TT):
                    R = rows_last if t == TT - 1 else 128
                    pi = self.next_ps()
                    for j in range(4):
                        S.add("pe", lambda e, j=j, si=si, pi=pi, t=t, R=R: e.matmul(self.psf[pi][:R, :], lhsT=B.gT[:, j, t * 128:t * 128 + R], rhs=self.slotR(si)[:, j, :],
                                                                                   start=(j == 0), stop=(j == 3)),
                              reads=[B.r_gT, self.r_w[si]], writes=[self.r_psf[pi]])
                    self.residual_add(B, t, R, cb, pi)

    def load_x(self, B, src, rows_last):
        for t in range(B.TT):
            R = rows_last if t == B.TT - 1 else 128
            self.S.add("sp", lambda e, t=t, R=R: e.dma_start(out=B.x[:R, t, :], in_=src[t * 128:t * 128 + R, :]),
                       writes=[B.r_x[t]], dma=B.r_x[t])

    def dump_x(self, B, dst, rows_last):
        for t in range(B.TT):
            R = rows_last if t == B.TT - 1 else 128
            self.S.add("sp", lambda e, t=t, R=R: e.dma_start(out=dst[t * 128:t * 128 + R, :], in_=B.x[:R, t, :]),
                       reads=[B.r_x[t]], writes=[self.r_out], dma=B.r_x[t])


def build_debug(stage):
    P = Prog(0, do_sample=False)
    P.dbg = P.nc.dram_tensor("dbg", [TW, D], F32, kind="ExternalOutput").ap()
    with P.es:
        P.alloc_common()
        P.prepass_mods()
        B = P.alloc_pass("p", 4, TW)
        P.load_x(B, P.xp[0:TW, :], 128)
        if stage >= 1:
            P.retention_layer(B, 0, 0, 0, 128, True, False, False)
        P.dump_x(B, P.dbg, 128)
        P.S.emit()
    return P


def _alloc_attn(P, tag, sample=False):
    A = type("A", (), {})()
    nb_ = 1 if sample else 2
    A.nb = nb_
    A.kvf = [P.sb(f"kvf{i}{tag}", [128, 512], F32) for i in range(nb_)]
    A.r_kvf = [Res(f"kvf{i}{tag}") for i in range(nb_)]
    A.kvb = [P.sb(f"kvb{i}{tag}", [128, 512], BF16) for i in range(nb_)]
    A.r_kvb = [Res(f"kvb{i}{tag}") for i in range(nb_)]
    if not sample:
        A.ktt = [P.sb(f"ktt{i}{tag}", [128, 4, 128], BF16) for i in range(2)]
        A.r_ktt = [Res(f"ktt{i}{tag}") for i in range(2)]
    A.msum = P.sb(f"msum{tag}", [128, H_B, 4], F32)
    A.r_msum = Res("msum" + tag)
    A.ctr = 0
    if not sample:
        A.KTh = P.sb("KTh", [128, SEQ], BF16)
        A.Vh = P.sb("Vh", [128, 16, 128], BF16)
        A.r_KTh, A.r_Vh = Res("KTh"), Res("Vh")
        A.Pm = P.sb("Pm", [128, SEQ], BF16)
        A.r_Pm = Res("Pm")
        A.PT = P.sb("PT", [128, 16, 128], BF16)
        A.r_PT = Res("PT")
        A.Pown = P.sb("Pown", [128, 128], F32)
        A.r_Pown = Res("Pown")
        A.sm = P.sb("sm", [128, 64], F32)
        A.r_sm = Res("sm")
    return A


def kv_proj(P, B, A, pidx, row, rows_last, sample, samp_i=0):
    S = P.S
    TT = B.TT
    off = 4 * 3 * D
    P.mod_norm(B, row, P.kv_norm_g, off, off + D, rows_last)
    import os
    kvmode = os.environ.get("KVMODE", "kv")
    for cb in range(8):
        isK = cb < 4
        if (isK and "k" not in kvmode) or ((not isK) and "v" not in kvmode):
            continue
        slots = [P.wload_R(P.w_kv, kk * 512, cb * 512) for kk in range(4)]
        for t in range(TT):
            R = rows_last if t == TT - 1 else 128
            pi = P.next_ps()
            for kk in range(4):
                for j in range(4):
                    S.add("pe", lambda e, kk=kk, j=j, pi=pi, t=t, R=R, si=slots[kk]: e.matmul(
                        P.psf[pi][:R, :], lhsT=B.hT[:, kk * 4 + j, t * 128:t * 128 + R], rhs=P.slotR(si)[:, j, :],
                        start=(kk == 0 and j == 0), stop=(kk == 3 and j == 3)),
                        reads=[B.r_hT, P.r_w[slots[kk]]], writes=[P.r_psf[pi]])
            bi = A.ctr % A.nb
            A.ctr += 1
            kvstep = int(os.environ.get("KVSTEP", "9"))
            if kvstep < 1:
                continue
            S.add("act", lambda e, pi=pi, bi=bi, R=R: e.activation(out=A.kvf[bi][:R, :], in_=P.psf[pi][:R, :], func=AF.Copy),
                  reads=[P.r_psf[pi]], writes=[A.r_kvf[bi]])
            if kvstep < 2:
                continue
            S.add("dve", lambda e, pi=pi, bi=bi, R=R: e.tensor_copy(out=A.kvb[bi][:R, :], in_=A.kvf[bi][:R, :]),
                  reads=[A.r_kvf[bi]], writes=[A.r_kvb[bi]])
            if kvstep < 3:
                continue
            cc = (cb % 4) * 512
            if sample:
                dst = (P.k_s if isK else P.v_s)[samp_i, 0:R, cc:cc + 512]
            else:
                r0 = pidx * TW + t * 128
                dst = (P.k_p if isK else P.v_p)[r0:r0 + R, cc:cc + 512]
            if os.environ.get("NOKVOUT") != "1":
                S.add("sp", lambda e, dst=dst, bi=bi, R=R: e.dma_start(out=dst, in_=A.kvf[bi][:R, :]),
                      reads=[A.r_kvf[bi]], writes=[P.r_out], dma=A.r_kvf[bi])
            if sample:
                if isK:
                    pb = P.next_psb()
                    for hh in range(4):
                        S.add("pe", lambda e, hh=hh, pb=pb, bi=bi, R=R: e.transpose(out=P.psb[pb][:, hh * 128:hh * 128 + R], in_=A.kvb[bi][:R, hh * 128:(hh + 1) * 128],
                                                                                  identity=P.ident_bf[:R, :R]),
                              reads=[A.r_kvb[bi], P.r_const], writes=[P.r_psb[pb]])
                    S.add("act", lambda e, pb=pb, cb=cb, R=R: e.activation(out=A.KTn[:, cb * 4:(cb + 1) * 4, 0:R],
                                                                           in_=P.psb[pb][:, 0:512].rearrange("p (h n) -> p h n", n=128)[:, :, 0:R], func=AF.Copy),
                          reads=[P.r_psb[pb]], writes=[A.r_KTn])
                else:
                    S.add("dve", lambda e, bi=bi, cc=cc, R=R: e.tensor_copy(out=A.Vn[:R, cc:cc + 512], in_=A.kvb[bi][:R, :]),
                          reads=[A.r_kvb[bi]], writes=[A.r_Vn])
                continue
            r0 = pidx * TW + t * 128
            if not isK:
                if os.environ.get("NOVSCR") != "1":
                    S.add("sp", lambda e, bi=bi, r0=r0, cc=cc: e.dma_start(out=P.v_scr[r0:r0 + 128, cc:cc + 512], in_=A.kvb[bi][:, :]),
                          reads=[A.r_kvb[bi]], writes=[P.r_v_scr], dma=A.r_kvb[bi])
            else:
                pb = P.next_psb()
                for hh in range(4):
                    S.add("pe", lambda e, hh=hh, pb=pb, bi=bi: e.transpose(out=P.psb[pb][:, hh * 128:(hh + 1) * 128], in_=A.kvb[bi][:, hh * 128:(hh + 1) * 128],
                                                                        identity=P.ident_bf[:]),
                          reads=[A.r_kvb[bi], P.r_const], writes=[P.r_psb[pb]])
                S.add("act", lambda e, pb=pb, bi=bi: e.activation(out=A.ktt[bi][:], in_=P.psb[pb][:, 0:512].rearrange("p (h n) -> p h n", n=128), func=AF.Copy),
                      reads=[P.r_psb[pb]], writes=[A.r_ktt[bi]])
                S.add("dve", lambda e, bi=bi, cb=cb, t=t: e.tensor_reduce(out=A.msum[:, cb * 4:(cb + 1) * 4, t], in_=A.ktt[bi][:], axis=AX.X, op=ALU.add),
                      reads=[A.r_ktt[bi]], writes=[A.r_msum])
                S.add("sp", lambda e, bi=bi, cb=cb, r0=r0: e.dma_start(out=P.kt_scr[cb * 4:(cb + 1) * 4, :, r0:r0 + 128].rearrange("h p n -> p h n"), in_=A.ktt[bi][:]),
                      reads=[A.r_ktt[bi]], writes=[P.r_kt_scr], dma=A.r_ktt[bi])
    if not sample:
        for bb in range(2):
            blk = pidx * 2 + bb
            S.add("dve", lambda e, bb=bb, blk=blk: e.tensor_tensor(out=P.meansT[:, :, blk], in0=A.msum[:, :, 2 * bb], in1=A.msum[:, :, 2 * bb + 1], op=ALU.add),
                  reads=[A.r_msum], writes=[P.r_means])
            S.add("dve", lambda e, blk=blk: e.tensor_scalar(out=P.meansT_bf[:, :, blk], in0=P.meansT[:, :, blk], scalar1=1.0 / 256, scalar2=None, op0=ALU.mult),
                  reads=[P.r_means], writes=[P.r_means_bf])


def moba_layer_prompt(P, B, A, l, pidx, row):
    S = P.S
    TT, NT = B.TT, B.NT
    li = l - 2
    W = P.w_q_b[li]
    moff = l * 3 * D
    P.mod_norm(B, row, P.norm_g[l], moff, moff + D, 128)
    P.load_rep(B, B.gate_rep, B.r_gate, row, moff + 2 * D)
    scale = 128 ** -0.5
    for j in range(16):
        P.proj_T(B, W, 2048 + j * 128,
                 lambda pi, j=j: S.add("act", lambda e: e.activation(out=B.sgT[:, j, :], in_=P.psf[pi][:, 0:NT], func=AF.Silu),
                                       reads=[P.r_psf[pi]], writes=[B.r_sgT]))
    nkmax = (pidx + 1) * TW
    sm = A.sm
    for h in range(H_B):
        P.proj_T(B, W, h * 128,
                 lambda pi: S.add("act", lambda e: e.activation(out=B.qT[:, 0, :], in_=P.psf[pi][:, 0:NT], func=AF.Copy),
                                  reads=[P.r_psf[pi]], writes=[B.r_qT]))
        S.add("sp", lambda e, h=h: e.dma_start(out=A.KTh[:, 0:nkmax], in_=P.kt_scr[h, :, 0:nkmax]),
              reads=[P.r_kt_scr], writes=[A.r_KTh], dma=A.r_KTh)
        S.add("sp", lambda e, h=h: e.dma_start(out=A.Vh[:, 0:nkmax // 128, :],
                                               in_=P.v_scr[0:nkmax, h * 128:(h + 1) * 128].rearrange("(c p) d -> p c d", p=128)),
              reads=[P.r_v_scr], writes=[A.r_Vh], dma=A.r_Vh)
        for qc in range(4):
            _moba_chunk(P, B, A, h, qc, pidx, scale)
    out_proj_full(P, B, P.w_o_b[li], 128)


def _moba_chunk(P, B, A, h, qc, pidx, scale):
    S = P.S
    sm = A.sm
    if True:
        if True:
            ac = pidx * 4 + qc
            ob = ac // 2
            nkf = ob * 256
            own = 128 if ac % 2 == 0 else 256
            NK = nkf + own
            qs = slice(qc * 128, (qc + 1) * 128)
            nb = (NK + 511) // 512
            banks = [P.next_ps() for _ in range(nb)]
            for i, bk in enumerate(banks):
                cols = min(512, NK - i * 512)
                S.add("pe", lambda e, i=i, bk=bk, cols=cols: e.matmul(P.psf[bk][:, 0:cols], lhsT=B.qT[:, 0, qs], rhs=A.KTh[:, i * 512:i * 512 + cols], start=True, stop=True),
                      reads=[B.r_qT, A.r_KTh], writes=[P.r_psf[bk]])
                S.add("dve", lambda e, i=i, bk=bk, cols=cols: e.tensor_reduce(out=sm[:, i:i + 1], in_=P.psf[bk][:, 0:cols], axis=AX.X, op=ALU.max),
                      reads=[P.r_psf[bk]], writes=[A.r_sm])
            S.add("dve", lambda e: e.tensor_reduce(out=sm[:, 4:5], in_=sm[:, 0:nb], axis=AX.X, op=ALU.max), reads=[A.r_sm], writes=[A.r_sm])
            S.add("dve", lambda e: e.tensor_scalar(out=sm[:, 5:6], in0=sm[:, 4:5], scalar1=-scale, scalar2=None, op0=ALU.mult), reads=[A.r_sm], writes=[A.r_sm])
            if ob > 3:
                pg = P.next_ps()
                S.add("pe", lambda e, pg=pg, h=h: e.matmul(P.psf[pg][:, 0:ob], lhsT=B.qT[:, 0, qs], rhs=P.meansT_bf[:, h, 0:ob], start=True, stop=True),
                      reads=[B.r_qT, P.r_means_bf], writes=[P.r_psf[pg]])
                S.add("dve", lambda e: e.memset(sm[:, 16:24], -1e30), writes=[A.r_sm])
                S.add("dve", lambda e, pg=pg: e.tensor_copy(out=sm[:, 16:16 + ob], in_=P.psf[pg][:, 0:ob]), reads=[P.r_psf[pg]], writes=[A.r_sm])
                S.add("dve", lambda e: e.max(out=sm[:, 24:32], in_=sm[:, 16:24]), reads=[A.r_sm], writes=[A.r_sm])
                S.add("dve", lambda e: e.tensor_scalar(out=sm[:, 8:8 + ob], in0=sm[:, 16:16 + ob], scalar1=sm[:, 26:27], scalar2=None, op0=ALU.is_ge),
                      reads=[A.r_sm], writes=[A.r_sm])
                S.add("dve", lambda e: e.tensor_scalar(out=sm[:, 8:8 + ob], in0=sm[:, 8:8 + ob], scalar1=-1.0, scalar2=30000.0, op0=ALU.add, op1=ALU.mult),
                      reads=[A.r_sm], writes=[A.r_sm])
                S.add("dve", lambda e: e.tensor_scalar(out=sm[:, 8:8 + ob], in0=sm[:, 8:8 + ob], scalar1=sm[:, 5:6], scalar2=None, op0=ALU.add),
                      reads=[A.r_sm], writes=[A.r_sm])
            elif ob > 0:
                S.add("dve", lambda e: e.tensor_copy(out=sm[:, 8:8 + ob], in_=sm[:, 5:6].to_broadcast([128, ob])), reads=[A.r_sm], writes=[A.r_sm])
            S.add("dve", lambda e: e.memset(sm[:, 32:48], 0.0), writes=[A.r_sm])
            for n in range(ob):
                bk = banks[(n * 256) // 512]
                co = (n * 256) % 512
                S.add("act", lambda e, n=n, bk=bk, co=co: e.activation(out=A.Pm[:, n * 256:(n + 1) * 256], in_=P.psf[bk][:, co:co + 256], func=AF.Exp,
                                                                       scale=scale, bias=sm[:, 8 + n:9 + n], accum_out=sm[:, 32 + n:33 + n]),
                      reads=[P.r_psf[bk], A.r_sm], writes=[A.r_Pm, A.r_sm])
            bk = banks[nkf // 512]
            co = nkf % 512
            nr = ob
            if own == 256:
                S.add("act", lambda e, bk=bk, co=co, nr=nr: e.activation(out=A.Pm[:, nkf:nkf + 128], in_=P.psf[bk][:, co:co + 128], func=AF.Exp,
                                                                         scale=scale, bias=sm[:, 5:6], accum_out=sm[:, 32 + nr:33 + nr]),
                      reads=[P.r_psf[bk], A.r_sm], writes=[A.r_Pm, A.r_sm])
                nr += 1
                co += 128
            oc = nkf + own - 128
            S.add("act", lambda e, bk=bk, co=co: e.activation(out=A.Pown[:], in_=P.psf[bk][:, co:co + 128], func=AF.Exp, scale=scale, bias=sm[:, 5:6]),
                  reads=[P.r_psf[bk], A.r_sm], writes=[A.r_Pown])
            S.add("dve", lambda e, oc=oc: e.tensor_tensor(out=A.Pm[:, oc:oc + 128], in0=A.Pown[:], in1=P.tri_qk[:], op=ALU.mult),
                  reads=[A.r_Pown, P.r_const], writes=[A.r_Pm])
            S.add("dve", lambda e, oc=oc, nr=nr: e.tensor_reduce(out=sm[:, 32 + nr:33 + nr], in_=A.Pm[:, oc:oc + 128], axis=AX.X, op=ALU.add),
                  reads=[A.r_Pm], writes=[A.r_sm])
            nr += 1
            S.add("dve", lambda e, nr=nr: e.tensor_reduce(out=sm[:, 6:7], in_=sm[:, 32:32 + nr], axis=AX.X, op=ALU.add), reads=[A.r_sm], writes=[A.r_sm])
            S.add("dve", lambda e: e.reciprocal(out=sm[:, 7:8], in_=sm[:, 6:7]), reads=[A.r_sm], writes=[A.r_sm])
            nkc = NK // 128
            for g0 in range(0, nkc, 8):
                g1 = min(nkc, g0 + 8)
                pb = P.next_psb()
                for kc in range(g0, g1):
                    S.add("pe", lambda e, kc=kc, pb=pb, g0=g0: e.transpose(out=P.psb[pb][:, (kc - g0) * 128:(kc - g0 + 1) * 128], in_=A.Pm[:, kc * 128:(kc + 1) * 128],
                                                                           identity=P.ident_bf[:]),
                          reads=[A.r_Pm, P.r_const], writes=[P.r_psb[pb]])
                S.add("dve", lambda e, pb=pb, g0=g0, g1=g1: e.tensor_copy(out=A.PT[:, g0:g1, :],
                                                                         in_=P.psb[pb][:, 0:(g1 - g0) * 128].rearrange("p (c n) -> p c n", n=128)),
                      reads=[P.r_psb[pb]], writes=[A.r_PT])
            po = P.next_ps()
            for kc in range(nkc):
                S.add("pe", lambda e, kc=kc, po=po: e.matmul(P.psf[po][:, 0:128], lhsT=A.PT[:, kc, :], rhs=A.Vh[:, kc, :], start=(kc == 0), stop=(kc == nkc - 1)),
                      reads=[A.r_PT, A.r_Vh], writes=[P.r_psf[po]])
            S.add("dve", lambda e, po=po: e.tensor_scalar(out=B.onb[:, 0:128], in0=P.psf[po][:, 0:128], scalar1=sm[:, 7:8], scalar2=None, op0=ALU.mult),
                  reads=[P.r_psf[po], A.r_sm], writes=[B.r_onb])
            pb = P.next_psb()
            S.add("pe", lambda e, pb=pb: e.transpose(out=P.psb[pb][:, 0:128], in_=B.onb[:, 0:128], identity=P.ident_bf[:]),
                  reads=[B.r_onb, P.r_const], writes=[P.r_psb[pb]])
            S.add("dve", lambda e, pb=pb, h=h: e.tensor_tensor(out=B.gT[:, h, qs], in0=P.psb[pb][:, 0:128], in1=B.sgT[:, h, qs], op=ALU.mult),
                  reads=[P.r_psb[pb], B.r_sgT], writes=[B.r_gT])


def out_proj_full(P, B, Wo, rows_last):
    S = P.S
    TT = B.TT
    for cb in range(4):
        slots = [P.wload_R(Wo, kk * 512, cb * 512) for kk in range(4)]
        for t in range(TT):
            R = rows_last if t == TT - 1 else 128
            pi = P.next_ps()
            for kk in range(4):
                for j in range(4):
                    S.add("pe", lambda e, kk=kk, j=j, pi=pi, t=t, R=R, si=slots[kk]: e.matmul(
                        P.psf[pi][:R, :], lhsT=B.gT[:, kk * 4 + j, t * 128:t * 128 + R], rhs=P.slotR(si)[:, j, :],
                        start=(kk == 0 and j == 0), stop=(kk == 3 and j == 3)),
                        reads=[B.r_gT, P.r_w[slots[kk]]], writes=[P.r_psf[pi]])
            P.residual_add(B, t, R, cb, pi)


def final_norm(P, B, row, dst, rows_last):
    S = P.S
    off = 4 * 3 * D + 2 * D
    P.load_rep(B, B.sh_rep, B.r_sh, row, off)
    P.load_rep(B, B.a_rep, B.r_a, row, off + D)
    S.add("sp", lambda e: e.dma_start(out=B.ntmp[:], in_=P.final_g.partition_broadcast(128)), writes=[B.r_ntmp], dma=B.r_ntmp)
    S.add("dve", lambda e: e.scalar_tensor_tensor(out=B.a_rep[:], in0=B.a_rep[:], scalar=1.0, in1=B.ntmp[:], op0=ALU.add, op1=ALU.mult),
          reads=[B.r_a, B.r_ntmp], writes=[B.r_a])
    for t in range(B.TT):
        R = rows_last if t == B.TT - 1 else 128
        S.add("dve", lambda e, R=R: e.memset(B.ss[:R, 0:1], 0.0), writes=[B.r_ss])
        S.add("act", lambda e, t=t, R=R: e.activation(out=B.ntmp[:R, :], in_=B.x[:R, t, :], func=AF.Square, accum_out=B.ss[:R, 0:1]),
              reads=[B.r_x[t], B.r_ss], writes=[B.r_ntmp, B.r_ss])
        S.add("act", lambda e, R=R: e.activation(out=B.ss[:R, 1:2], in_=B.ss[:R, 0:1], func=AF.Sqrt, scale=1.0 / D, bias=P.eps_t[:R, 0:1]),
              reads=[B.r_ss, P.r_const], writes=[B.r_ss])
        S.add("dve", lambda e, R=R: e.reciprocal(out=B.ss[:R, 2:3], in_=B.ss[:R, 1:2]), reads=[B.r_ss], writes=[B.r_ss])
        S.add("dve", lambda e, t=t, R=R: e.scalar_tensor_tensor(out=B.ntmp[:R, :], in0=B.x[:R, t, :], scalar=B.ss[:R, 2:3], in1=B.a_rep[:R, :],
                                                                 op0=ALU.mult, op1=ALU.mult),
              reads=[B.r_x[t], B.r_ss, B.r_a], writes=[B.r_ntmp])
        S.add("dve", lambda e, t=t, R=R: e.tensor_tensor(out=B.x[:R, t, :], in0=B.ntmp[:R, :], in1=B.sh_rep[:R, :], op=ALU.add),
              reads=[B.r_ntmp, B.r_sh, B.r_x[t]], writes=[B.r_x[t]])
        S.add("sp", lambda e, t=t, R=R: e.dma_start(out=dst[t * 128:t * 128 + R, :], in_=B.x[:R, t, :]),
              reads=[B.r_x[t]], writes=[P.r_out], dma=B.r_x[t])


def build_prompt_only(nseg=NSEG, nlayers=4):
    P = Prog(0, do_sample=False)
    with P.es:
        P.alloc_common()
        P.meansT = P.sb("meansT", [128, H_B, 8], F32)
        P.meansT_bf = P.sb("meansT_bf", [128, H_B, 8], BF16)
        P.r_means, P.r_means_bf = Res("means"), Res("means_bf")
        P.prepass_mods()
        B = P.alloc_pass("p", 4, TW)
        A = _alloc_attn(P, "p")
        for s in range(nseg):
            P.load_x(B, P.xp[s * TW:(s + 1) * TW, :], 128)
            for l in range(min(2, nlayers)):
                P.retention_layer(B, l, s, 0, 128, s == 0, s == nseg - 1, False)
            import os
            if nlayers >= 2 and os.environ.get("NOKV") != "1":
                kv_proj(P, B, A, s, 0, 128, False)
            for l in range(2, nlayers):
                moba_layer_prompt(P, B, A, l, s, 0)
            if nlayers == 4:
                final_norm(P, B, 0, P.y_p[s * TW:(s + 1) * TW, :], 128)
            else:
                P.dump_x(B, P.y_p[s * TW:(s + 1) * TW, :], 128)
        P.S.emit()
    return P


def _consts_sample():
    c = {}
    oh = np.zeros((4, 4, 128), np.float32)
    for q in range(4):
        oh[q, q, :] = 1.0
    c["onehot"] = oh.astype(ml_dtypes.bfloat16)
    bd = np.zeros((64, D), np.float32)
    sq = np.zeros((64, 4), np.float32)
    for h in range(H_B):
        for q in range(4):
            bd[h * 4 + q, h * 128:(h + 1) * 128] = 1.0
            sq[h * 4 + q, q] = 1.0
    c["blockdiag"] = bd.astype(ml_dtypes.bfloat16)
    c["selq"] = sq.astype(ml_dtypes.bfloat16)
    nm = np.zeros((4, 64), np.float32)
    for key in range(4):
        for h in range(H_B):
            for q in range(4):
                if key > q:
                    nm[key, h * 4 + q] = -30000.0
    c["negmaskn"] = nm
    c["piota"] = np.arange(128, dtype=np.float32).reshape(128, 1)
    return c


def alloc_sample_attn(P):
    A = _alloc_attn(P, "s", sample=True)
    A.KTn = P.sb("KTn", [128, H_B, 4], BF16)
    A.Vn = P.sb("Vn", [128, D], BF16)
    A.r_KTn, A.r_Vn = Res("KTn"), Res("Vn")
    A.ptf = P.sb("ptf", [128, NPAGES], F32)
    A.pti = P.sb("pti", [128, NPAGES], I32)
    A.idx = P.sb("idx", [128, NPAGES], I32)
    A.r_idx = Res("idx")
    A.kpage = [P.sb(f"kpage{i}", [128, D], F32) for i in range(2)]
    A.r_kpage = [Res(f"kpage{i}") for i in range(2)]
    A.KTp = [P.sb(f"KTp{i}", [128, H_B, 128], BF16) for i in range(2)]
    A.r_KTp = [Res(f"KTp{i}") for i in range(2)]
    A.ST = P.sb("ST", [128, NPAGES, 64], F32)
    A.r_ST = Res("ST")
    A.PTs = P.sb("PTs", [128, NPAGES, 64], BF16)
    A.r_PTs = Res("PTs")
    A.vb = [P.sb("vb0", [128, D], BF16)] * 2
    A.r_vb = [Res("vb0")] * 2
    A.qTs = P.sb("qTs", [128, H_B, 4], BF16)
    A.r_qTs = Res("qTs")
    A.msumS = P.sb("msumS", [128, H_B, NPAGES], F32)
    A.meansS_bf = P.sb("meansS_bf", [128, H_B, 64], BF16)
    A.r_msumS, A.r_meansS = Res("msumS"), Res("meansS")
    A.selrep = P.sb("selrep", [128, 4, 1024], BF16)
    A.r_selrep = Res("selrep")
    A.sc = P.sb("sc", [128, 512], F32)
    A.r_sc = Res("sc")
    A.scb = P.sb("scb", [128, 256], BF16)
    A.r_scb = Res("scb")
    A.gt = P.sb("gt", [4, 1024], F32)
    A.selb = P.sb("selb", [4, 1024], BF16)
    A.r_gt = Res("gt")
    A.Om = P.sb("Om", [64, D], BF16)
    A.r_Om = Res("Om")
    A.onehot = P.sb("onehot_t", [4, 4, 128], BF16)
    A.blockdiag = P.sb("blockdiag_t", [64, D], BF16)
    A.selq = P.sb("selq_t", [64, 4], BF16)
    A.negmaskn = P.sb("negmaskn_t", [4, 64], F32)
    A.piota = P.sb("piota_t", [128, 1], F32)
    A.r_c = Res("sconst")
    S = P.S
    for dst, src in ((A.onehot, P.c_onehot), (A.blockdiag, P.c_blockdiag), (A.selq, P.c_selq), (A.negmaskn, P.c_negmaskn), (A.piota, P.c_piota)):
        S.add("sp", lambda e, dst=dst, src=src: e.dma_start(out=dst[:], in_=src), writes=[A.r_c], dma=A.r_c)
    S.add("dve", lambda e: e.memset(A.scb[:, 128:129], 1.0), writes=[A.r_c])
    return A


def sample_page_index(P, A, si):
    S = P.S
    S.add("sp", lambda e: e.dma_start(out=A.pti[:], in_=P.ptab[si].partition_broadcast(128)), writes=[A.r_idx], dma=A.r_idx)
    S.add("dve", lambda e: e.tensor_copy(out=A.ptf[:], in_=A.pti[:]), reads=[A.r_idx], writes=[A.r_idx])
    S.add("dve", lambda e: e.tensor_scalar(out=A.ptf[:], in0=A.ptf[:], scalar1=128.0, scalar2=A.piota[:, 0:1], op0=ALU.mult, op1=ALU.add),
          reads=[A.r_idx, A.r_c], writes=[A.r_idx])
    S.add("dve", lambda e: e.tensor_copy(out=A.idx[:], in_=A.ptf[:]), reads=[A.r_idx], writes=[A.r_idx])


def moba_layer_sample(P, B, A, l, row, si, do_means):
    S = P.S
    NT = B.NT
    li = l - 2
    W = P.w_q_b[li]
    moff = l * 3 * D
    scale = 128 ** -0.5
    P.mod_norm(B, row, P.norm_g[l], moff, moff + D, 4)
    P.load_rep(B, B.gate_rep, B.r_gate, row, moff + 2 * D)
    for j in range(16):
        P.proj_T(B, W, 2048 + j * 128,
                 lambda pi, j=j: S.add("act", lambda e: e.activation(out=B.sgT[:, j, :], in_=P.psf[pi][:, 0:NT], func=AF.Silu),
                                       reads=[P.r_psf[pi]], writes=[B.r_sgT]))
    for h in range(H_B):
        P.proj_T(B, W, h * 128,
                 lambda pi, h=h: S.add("act", lambda e: e.activation(out=A.qTs[:, h, :], in_=P.psf[pi][:, 0:NT], func=AF.Copy),
                                       reads=[P.r_psf[pi]], writes=[A.r_qTs]))
    sc, scb = A.sc, A.scb
    import os
    mst = int(os.environ.get("MSTAGE", "9"))
    if mst < 2:
        out_proj_full(P, B, P.w_o_b[li], 4)
        return
    for j in range(NPAGES):
        b = j % 2
        S.add("pool", lambda e, j=j, b=b: e.indirect_dma_start(out=A.kpage[b][:], out_offset=None, in_=P.cache_k,
                                                              in_offset=bass.IndirectOffsetOnAxis(ap=A.idx[:, j:j + 1], axis=0)),
              reads=[A.r_idx], writes=[A.r_kpage[b]], dma=A.r_kpage[b])
        if mst < 3:
            S.add("dve", lambda e, j=j, b=b: e.tensor_copy(out=A.ST[:, j, :], in_=A.kpage[b][:, 0:64]), reads=[A.r_kpage[b]], writes=[A.r_ST])
            continue
        for g in range(4):
            pi = P.next_ps()
            for hh in range(4):
                h = g * 4 + hh
                S.add("pe", lambda e, pi=pi, hh=hh, h=h, b=b: e.transpose(out=P.psf[pi][:, hh * 128:(hh + 1) * 128], in_=A.kpage[b][:, h * 128:(h + 1) * 128],
                                                                        identity=P.ident_f[:]),
                      reads=[A.r_kpage[b], P.r_const], writes=[P.r_psf[pi]])
            S.add("act", lambda e, pi=pi, g=g, b=b: e.activation(out=A.KTp[b][:, g * 4:(g + 1) * 4, :], in_=P.psf[pi][:].rearrange("p (h n) -> p h n", n=128), func=AF.Copy),
                  reads=[P.r_psf[pi]], writes=[A.r_KTp[b]])
        if do_means:
            S.add("dve", lambda e, j=j, b=b: e.tensor_reduce(out=A.msumS[:, :, j], in_=A.KTp[b][:], axis=AX.X, op=ALU.add),
                  reads=[A.r_KTp[b]], writes=[A.r_msumS])
        pst = P.next_ps()
        for h in range(H_B):
            S.add("pe", lambda e, pst=pst, h=h, b=b: e.matmul(P.psf[pst][:, h * 4:(h + 1) * 4], lhsT=A.KTp[b][:, h, :], rhs=A.qTs[:, h, :], start=True, stop=True),
                  reads=[A.r_KTp[b], A.r_qTs], writes=[P.r_psf[pst]])
        S.add("act", lambda e, pst=pst, j=j: e.activation(out=A.ST[:, j, :], in_=P.psf[pst][:, 0:64], func=AF.Copy),
              reads=[P.r_psf[pst]], writes=[A.r_ST])
    if do_means:
        S.add("dve", lambda e: e.tensor_tensor(out=A.meansS_bf[:], in0=A.msumS[:].rearrange("p h (b g) -> p h b g", g=2)[:, :, :, 0],
                                               in1=A.msumS[:].rearrange("p h (b g) -> p h b g", g=2)[:, :, :, 1], op=ALU.add),
              reads=[A.r_msumS], writes=[A.r_meansS])
    if mst < 4:
        out_proj_full(P, B, P.w_o_b[li], 4)
        return
    pn = P.next_ps()
    for h in range(H_B):
        S.add("pe", lambda e, h=h: e.matmul(P.psf[pn][0:4, h * 4:(h + 1) * 4], lhsT=A.KTn[:, h, 0:4], rhs=A.qTs[:, h, :], start=True, stop=True),
              reads=[A.r_KTn, A.r_qTs], writes=[P.r_psf[pn]])
    S.add("dve", lambda e: e.tensor_tensor(out=sc[0:4, 128:192], in0=P.psf[pn][0:4, 0:64], in1=A.negmaskn[:], op=ALU.add),
          reads=[P.r_psf[pn], A.r_c], writes=[A.r_sc])
    S.add("dve", lambda e: e.tensor_reduce(out=sc[:, 0:64], in_=A.ST[:].rearrange("p j c -> p c j"), axis=AX.X, op=ALU.max),
          reads=[A.r_ST], writes=[A.r_sc])
    S.add("dve", lambda e: e.tensor_tensor(out=sc[0:4, 0:64], in0=sc[0:4, 0:64], in1=sc[0:4, 128:192], op=ALU.max), reads=[A.r_sc], writes=[A.r_sc])
    pm = P.next_ps()
    S.add("pe", lambda e: e.transpose(out=P.psf[pm][0:64, 0:128], in_=sc[:, 0:64], identity=P.ident_f[:]),
          reads=[A.r_sc, P.r_const], writes=[P.r_psf[pm]])
    S.add("dve", lambda e: e.tensor_reduce(out=sc[0:64, 320:321], in_=P.psf[pm][0:64, 0:128], axis=AX.X, op=ALU.max), reads=[P.r_psf[pm]], writes=[A.r_sc])
    S.add("dve", lambda e: e.tensor_scalar(out=scb[0:64, 0:64], in0=P.ident_f[0:64, 0:64], scalar1=sc[0:64, 320:321], scalar2=None, op0=ALU.mult),
          reads=[A.r_sc, P.r_const], writes=[A.r_scb])
    pm2 = P.next_ps()
    S.add("pe", lambda e: e.matmul(P.psf[pm2][:, 0:64], lhsT=P.ones_bf[0:64, :], rhs=scb[0:64, 0:64], start=True, stop=True),
          reads=[A.r_scb, P.r_const], writes=[P.r_psf[pm2]])
    S.add("dve", lambda e: e.tensor_copy(out=sc[:, 64:128], in_=P.psf[pm2][:, 0:64]), reads=[P.r_psf[pm2]], writes=[A.r_sc])
    pgs = [P.next_ps(), P.next_ps()]
    for h in range(H_B):
        S.add("pe", lambda e, h=h: e.matmul(P.psf[pgs[h // 8]][0:4, (h % 8) * 64:(h % 8 + 1) * 64], lhsT=A.qTs[:, h, :], rhs=A.meansS_bf[:, h, :], start=True, stop=True),
              reads=[A.r_qTs, A.r_meansS], writes=[P.r_psf[pgs[h // 8]]])
    for g in range(2):
        S.add("dve", lambda e, g=g: e.tensor_copy(out=A.gt[0:4, g * 512:(g + 1) * 512], in_=P.psf[pgs[g]][0:4, :]), reads=[P.r_psf[pgs[g]]], writes=[A.r_gt])
    for h in range(H_B):
        S.add("dve", lambda e, h=h: e.max(out=sc[0:4, 192 + h * 8:200 + h * 8], in_=A.gt[0:4, h * 64:(h + 1) * 64]), reads=[A.r_gt], writes=[A.r_sc])
        S.add("dve", lambda e, h=h: e.tensor_scalar(out=A.selb[0:4, h * 64:(h + 1) * 64], in0=A.gt[0:4, h * 64:(h + 1) * 64],
                                                    scalar1=sc[0:4, 194 + h * 8:195 + h * 8], scalar2=None, op0=ALU.is_ge),
              reads=[A.r_gt, A.r_sc], writes=[A.r_gt])
    for q in range(4):
        for half in range(2):
            pr = P.next_ps()
            S.add("pe", lambda e, q=q, half=half, pr=pr: e.matmul(P.psf[pr][:, :], lhsT=A.onehot[0:4, q, :], rhs=A.selb[0:4, half * 512:(half + 1) * 512], start=True, stop=True),
                  reads=[A.r_gt, A.r_c], writes=[P.r_psf[pr]])
            S.add("act", lambda e, q=q, half=half, pr=pr: e.activation(out=A.selrep[:, q, half * 512:(half + 1) * 512], in_=P.psf[pr][:, :], func=AF.Copy),
                  reads=[P.r_psf[pr]], writes=[A.r_selrep])
    S.add("dve", lambda e: e.tensor_tensor(out=A.ST[:], in0=A.ST[:], in1=sc[:, 64:128].unsqueeze(1).to_broadcast([128, NPAGES, 64]), op=ALU.subtract),
          reads=[A.r_ST, A.r_sc], writes=[A.r_ST])
    S.add("act", lambda e: e.activation(out=A.PTs[:], in_=A.ST[:], func=AF.Exp, scale=scale), reads=[A.r_ST], writes=[A.r_PTs])
    for q in range(4):
        S.add("dve", lambda e, q=q: e.tensor_tensor(
            out=A.PTs[:].rearrange("p (b g) (h q) -> p b g h q", g=2, q=4)[:, :, :, :, q],
            in0=A.PTs[:].rearrange("p (b g) (h q) -> p b g h q", g=2, q=4)[:, :, :, :, q],
            in1=A.selrep[:, q, :].rearrange("p (h b) -> p b h", b=64).unsqueeze(2).to_broadcast([128, 64, 2, H_B]), op=ALU.mult),
            reads=[A.r_PTs, A.r_selrep], writes=[A.r_PTs])
    S.add("dve", lambda e: e.tensor_tensor(out=sc[0:4, 128:192], in0=sc[0:4, 128:192], in1=sc[0:4, 64:128], op=ALU.subtract), reads=[A.r_sc], writes=[A.r_sc])
    S.add("act", lambda e: e.activation(out=scb[0:4, 64:128], in_=sc[0:4, 128:192], func=AF.Exp, scale=scale), reads=[A.r_sc], writes=[A.r_scb])
    if mst < 5:
        out_proj_full(P, B, P.w_o_b[li], 4)
        return
    po = [P.next_ps() for _ in range(4)]
    prs = P.next_ps()
    for j in range(NPAGES):
        b = j % 2
        S.add("pool", lambda e, j=j, b=b: e.indirect_dma_start(out=A.kpage[b][:], out_offset=None, in_=P.cache_v,
                                                              in_offset=bass.IndirectOffsetOnAxis(ap=A.idx[:, j:j + 1], axis=0)),
              reads=[A.r_idx], writes=[A.r_kpage[b]], dma=A.r_kpage[b])
        if j % 2 == 0:
            S.add("act", lambda e, b=b: e.activation(out=A.vb[b][:], in_=A.kpage[b][:], func=AF.Copy), reads=[A.r_kpage[b]], writes=[A.r_vb[b]])
        else:
            S.add("dve", lambda e, b=b: e.tensor_copy(out=A.vb[b][:], in_=A.kpage[b][:]), reads=[A.r_kpage[b]], writes=[A.r_vb[b]])
        for cbk in range(4):
            S.add("pe", lambda e, j=j, b=b, cbk=cbk: e.matmul(P.psf[po[cbk]][0:64, :], lhsT=A.PTs[:, j, :], rhs=A.vb[b][:, cbk * 512:(cbk + 1) * 512], start=(j == 0), stop=False),
                  reads=[A.r_PTs, A.r_vb[b]], writes=[P.r_psf[po[cbk]]])
        S.add("pe", lambda e, j=j: e.matmul(P.psf[prs][0:64, 0:8], lhsT=A.PTs[:, j, :], rhs=P.ones_bf[:, 0:8], start=(j == 0), stop=False),
              reads=[A.r_PTs, P.r_const], writes=[P.r_psf[prs]])
    for cbk in range(4):
        S.add("pe", lambda e, cbk=cbk: e.matmul(P.psf[po[cbk]][0:64, :], lhsT=scb[0:4, 64:128], rhs=A.Vn[0:4, cbk * 512:(cbk + 1) * 512], start=False, stop=True),
              reads=[A.r_scb, A.r_Vn], writes=[P.r_psf[po[cbk]]])
    S.add("pe", lambda e: e.matmul(P.psf[prs][0:64, 0:8], lhsT=scb[0:4, 64:128], rhs=P.ones_bf[0:4, 0:8], start=False, stop=True),
          reads=[A.r_scb, P.r_const], writes=[P.r_psf[prs]])
    if mst < 6:
        out_proj_full(P, B, P.w_o_b[li], 4)
        return
    S.add("dve", lambda e: e.reciprocal(out=sc[0:64, 321:322], in_=P.psf[prs][0:64, 0:1]), reads=[P.r_psf[prs]], writes=[A.r_sc])
    for cbk in range(4):
        S.add("dve", lambda e, cbk=cbk: e.scalar_tensor_tensor(out=A.Om[:, cbk * 512:(cbk + 1) * 512], in0=P.psf[po[cbk]][0:64, :], scalar=sc[0:64, 321:322],
                                                              in1=A.blockdiag[:, cbk * 512:(cbk + 1) * 512], op0=ALU.mult, op1=ALU.mult),
              reads=[P.r_psf[po[cbk]], A.r_sc, A.r_c], writes=[A.r_Om])
    pb = P.next_psb()
    for cbk in range(4):
        pt = P.next_ps()
        S.add("pe", lambda e, cbk=cbk, pt=pt: e.matmul(P.psf[pt][0:4, :], lhsT=A.selq[:, :], rhs=A.Om[:, cbk * 512:(cbk + 1) * 512], start=True, stop=True),
              reads=[A.r_Om, A.r_c], writes=[P.r_psf[pt]])
        S.add("act", lambda e, cbk=cbk, pt=pt: e.activation(out=B.hb[0:4, cbk * 512:(cbk + 1) * 512], in_=P.psf[pt][0:4, :], func=AF.Copy),
              reads=[P.r_psf[pt]], writes=[B.r_hb])
    for h in range(H_B):
        S.add("pe", lambda e, h=h: e.transpose(out=P.psb[pb][:, h * 4:(h + 1) * 4], in_=B.hb[0:4, h * 128:(h + 1) * 128], identity=P.ident_bf[0:4, 0:4]),
              reads=[B.r_hb, P.r_const], writes=[P.r_psb[pb]])
    S.add("dve", lambda e: e.tensor_tensor(out=B.gT[:, :, 0:4], in0=P.psb[pb][:, 0:64].rearrange("p (h q) -> p h q", q=4), in1=B.sgT[:, :, 0:4], op=ALU.mult),
          reads=[P.r_psb[pb], B.r_sgT], writes=[B.r_gT])
    out_proj_full(P, B, P.w_o_b[li], 4)


def build_full(n_pool, ns=2, nseg=NSEG, prompt=True):
    P = Prog(n_pool, do_sample=True, ns=ns)
    with P.es:
        P.alloc_common()
        P.ones_bf = P.sb("ones_bf", [128, 128], BF16)
        P.S.add("dve", lambda e: e.memset(P.ones_bf[:], 1.0), writes=[P.r_const])
        P.prepass_mods()
        P.barrier()
        if prompt:
            P.cur_es = contextlib.ExitStack()
            P.meansT = P.sb("meansT", [128, H_B, 8], F32)
            P.meansT_bf = P.sb("meansT_bf", [128, H_B, 8], BF16)
            P.r_means, P.r_means_bf = Res("means"), Res("means_bf")
            B = P.alloc_pass("p", 4, TW)
            A = _alloc_attn(P, "p")
            for s in range(nseg):
                P.load_x(B, P.xp[s * TW:(s + 1) * TW, :], 128)
                for l in range(2):
                    P.retention_layer(B, l, s, 0, 128, s == 0, s == nseg - 1, False)
                kv_proj(P, B, A, s, 0, 128, False)
                for l in range(2, 4):
                    moba_layer_prompt(P, B, A, l, s, 0)
                final_norm(P, B, 0, P.y_p[s * TW:(s + 1) * TW, :], 128)
            P.barrier()
            P.cur_es.close()
        P.cur_es = contextlib.ExitStack()
        Bs = P.alloc_pass("s", 1, 4)
        As = alloc_sample_attn(P)
        for si in range(ns):
            row = 1 + si
            sample_page_index(P, As, si)
            P.load_x(Bs, P.xs[si], 4)
            import os
            sst = int(os.environ.get("SSTAGE", "9"))
            for l in range(2):
                if sst >= 1:
                    P.retention_layer(Bs, l, NSEG, row, 4, False, False, True, st_in=P.st_in[si], sts=P.sts[si])
            if sst >= 2:
                kv_proj(P, Bs, As, NSEG, row, 4, True, samp_i=si)
            for l in range(2, 4):
                if sst >= 3:
                    moba_layer_sample(P, Bs, As, l, row, si, l == 2)
            final_norm(P, Bs, row, P.y_s[si], 4)
        P.S.emit()
        P.cur_es.close()
    return P


NS_PER_CORE = 2


def kernel(x_prompt, x_sample, state_ret, cache_k, cache_v, page_table, c_prompt, c_sample,
           norm_g, w_mod, b_mod, w_in_a, w_out_a, w_q_b, w_o_b,
           kv_norm_g, w_mod_kv, b_mod_kv, w_kv, final_g, w_mod_f, b_mod_f):
    ns = NS_PER_CORE
    ncores = 8 // ns
    f32 = lambda a: np.ascontiguousarray(np.asarray(a), dtype=np.float32)
    x_prompt, x_sample, state_ret = f32(x_prompt), f32(x_sample), f32(state_ret)
    cache_k, cache_v = f32(cache_k), f32(cache_v)
    page_table = np.ascontiguousarray(np.asarray(page_table), dtype=np.int32)
    n_pool = cache_k.shape[0]
    P = build_full(n_pool, ns=ns)
    shared = {"norm_g": f32(norm_g), "w_mod": f32(w_mod), "b_mod": f32(b_mod), "w_in_a": f32(w_in_a), "w_out_a": f32(w_out_a),
              "w_q_b": f32(w_q_b), "w_o_b": f32(w_o_b), "kv_norm_g": f32(kv_norm_g), "w_mod_kv": f32(w_mod_kv),
              "b_mod_kv": f32(b_mod_kv), "w_kv": f32(w_kv), "final_g": f32(final_g), "w_mod_f": f32(w_mod_f), "b_mod_f": f32(b_mod_f),
              "cache_k": cache_k.reshape(n_pool * 128, D), "cache_v": cache_v.reshape(n_pool * 128, D)}
    shared.update(_consts())
    shared.update(_consts_sample())
    c_prompt, c_sample = f32(c_prompt), f32(c_sample)
    in_maps = []
    for c in range(ncores):
        b = (c * ns) // 2
        s0 = c * ns
        m = dict(shared)
        m["xp"] = x_prompt[b]
        m["cp"] = c_prompt[b]
        m["xs"] = x_sample[s0:s0 + ns]
        m["cs"] = c_sample[s0:s0 + ns]
        m["st_in"] = np.ascontiguousarray(state_ret[:, s0:s0 + ns].transpose(1, 0, 2, 3, 4))
        m["ptab"] = page_table[s0:s0 + ns]
        in_maps.append(m)
    res = run_bass_kernel_spmd(P.nc, in_maps, core_ids=list(range(ncores)))
    r = res.results
    nb, nsamp = x_prompt.shape[0], x_sample.shape[0]
    y_prompt = np.zeros((nb, SEQ, D), np.float32)
    y_sample = np.zeros((nsamp, 4, D), np.float32)
    st_p = np.zeros((2, nb, H_A, 256, 512), np.float32)
    st_s = np.zeros((2, nsamp, H_A, 256, 512), np.float32)
    k_p = np.zeros((nb, SEQ, H_B, 128), np.float32)
    v_p = np.zeros((nb, SEQ, H_B, 128), np.float32)
    k_s = np.zeros((nsamp, 4, H_B, 128), np.float32)
    v_s = np.zeros((nsamp, 4, H_B, 128), np.float32)
    for c in range(ncores):
        b = (c * ns) // 2
        if (c * ns) % 2 == 0:
            y_prompt[b] = r[c]["y_p"]
            st_p[:, b] = r[c]["stp"]
            k_p[b] = r[c]["k_p"].reshape(SEQ, H_B, 128)
            v_p[b] = r[c]["v_p"].reshape(SEQ, H_B, 128)
        for i in range(ns):
            s = c * ns + i
            y_sample[s] = r[c]["y_s"][i]
            st_s[:, s] = r[c]["sts"][i]
            k_s[s] = r[c]["k_s"][i].reshape(4, H_B, 128)
            v_s[s] = r[c]["v_s"][i].reshape(4, H_B, 128)
    return (y_prompt, y_sample, st_p, st_s, k_p, v_p, k_s, v_s)
```

```python
import contextlib
import math
import numpy as np
import ml_dtypes
import concourse.bass as bass
import concourse.mybir as mybir
from concourse.bass_utils import run_bass_kernel_spmd

F32 = mybir.dt.float32
BF16 = mybir.dt.bfloat16
I32 = mybir.dt.int32
AF = mybir.ActivationFunctionType
ALU = mybir.AluOpType
AX = mybir.AxisListType

D = 2048
SEQ = 2048
TW = 512
NSEG = 4
H_A = 8
H_B = 16
EPS = 1e-6
NPAGES = 128
SAME_ENGINE_RAW_SYNC = True


class Res:
    __slots__ = ("name", "last_w", "readers", "dsem", "dcnt", "last_dma")

    def __init__(self, name):
        self.name = name
        self.last_w = None
        self.readers = []
        self.dsem = None
        self.dcnt = 0
        self.last_dma = None


class Op:
    __slots__ = ("eng", "fn", "is_dma", "deps", "marked", "seq", "dres", "dval")

    def __init__(self, eng, fn, is_dma):
        self.eng = eng
        self.fn = fn
        self.is_dma = is_dma
        self.deps = []
        self.marked = False
        self.seq = 0
        self.dres = None
        self.dval = 0


class Sched:
    ENGS = ("pe", "act", "dve", "pool", "sp")

    def __init__(self, nc):
        self.nc = nc
        self.streams = {e: [] for e in self.ENGS}
        self.dma_res = []

    def add(self, eng, fn, reads=(), writes=(), dma=None, extra_deps=()):
        op = Op(eng, fn, dma is not None)
        deps = {}
        for r in reads:
            lw = r.last_w
            if lw is not None:
                deps[id(lw)] = (lw, True)
        for w in writes:
            lw = w.last_w
            if lw is not None and id(lw) not in deps:
                deps[id(lw)] = (lw, False)
            for rd in w.readers:
                if id(rd) not in deps:
                    deps[id(rd)] = (rd, False)
        for d in extra_deps:
            deps[id(d)] = (d, True)
        for d, raw in deps.values():
            if d is op:
                continue
            if (not d.is_dma) and d.eng == eng:
                if eng == "pe" or eng == "sp":
                    continue
                if not (raw and SAME_ENGINE_RAW_SYNC):
                    continue
            op.deps.append(d)
            d.marked = True
        for r in reads:
            r.readers.append(op)
        for w in writes:
            w.last_w = op
            w.readers = []
        if dma is not None:
            if dma.dsem is None:
                self.dma_res.append(dma)
                dma.dsem = True
            dma.dcnt += 16
            dma.last_dma = op
            op.dres = dma
            op.dval = dma.dcnt
            op.marked = True
        self.streams[eng].append(op)
        return op

    def emit(self):
        nc = self.nc
        with contextlib.ExitStack() as es:
            esem = {e: es.enter_context(nc.semaphore("sem_" + e)) for e in self.ENGS}
            for r in self.dma_res:
                r.dsem = es.enter_context(nc.semaphore("d_" + r.name))
            for e in self.ENGS:
                n = 0
                for op in self.streams[e]:
                    if op.marked and not op.is_dma:
                        n += 1
                        op.seq = n
            block = es.enter_context(nc.Block())
            streams = self.streams
            dma_res = self.dma_res

            def run(e, eng):
                waited = {}
                for op in streams[e]:
                    need = {}
                    for d in op.deps:
                        if d.is_dma:
                            key = ("d", id(d.dres))
                            sem, val = d.dres.dsem, d.dval
                        else:
                            key = ("e", d.eng)
                            sem, val = esem[d.eng], d.seq
                        if waited.get(key, 0) >= val:
                            continue
                        if key not in need or need[key][1] < val:
                            need[key] = (sem, val)
                    for key, (sem, val) in need.items():
                        eng.wait_ge(sem, val)
                        waited[key] = val
                    inst = op.fn(eng)
                    if op.is_dma:
                        inst.then_inc(op.dres.dsem, 16)
                    elif op.marked:
                        inst.then_inc(esem[e], 1)
                if e == "sp":
                    for r in dma_res:
                        if waited.get(("d", id(r)), 0) < r.dcnt:
                            eng.wait_ge(r.dsem, r.dcnt)

            @block.tensor
            def _(eng):
                run("pe", eng)

            @block.scalar
            def _(eng):
                run("act", eng)

            @block.vector
            def _(eng):
                run("dve", eng)

            @block.gpsimd
            def _(eng):
                run("pool", eng)

            @block.sync
            def _(eng):
                run("sp", eng)


def _gammas():
    return [1.0 - 2.0 ** (-5.0 - h) for h in range(H_A)]


def _rot_tables(pos, L):
    n = len(pos)
    inv = (1.0 / (10000.0 ** np.linspace(0.0, 1.0, 128, dtype=np.float32))).astype(np.float32)
    ang = (pos.astype(np.float32)[None, :] * inv[:, None]).astype(np.float32)
    cos = np.cos(ang).astype(np.float64)
    sin = np.sin(ang).astype(np.float64)
    out = np.zeros((H_A, 4, 128, n), np.float32)
    i = (np.arange(n) % L).astype(np.float64)
    for h, g in enumerate(_gammas()):
        fq = g ** (i + 1.0)
        fk = g ** (-(i + 1.0)) / 16.0
        out[h, 0] = cos * fq
        out[h, 1] = sin * fq
        out[h, 2] = cos * fk
        out[h, 3] = sin * fk
    return out


def _consts():
    c = {}
    c["ident_bf"] = np.eye(128, dtype=np.float32).astype(ml_dtypes.bfloat16)
    c["ident_f"] = np.eye(128, dtype=np.float32)
    m = np.arange(128)
    c["caus_ml"] = (m[:, None] <= m[None, :]).astype(np.float32)
    c["tri_qk"] = (m[None, :] <= m[:, None]).astype(np.float32)
    rot = np.zeros((NSEG + 1, H_A, 4, 128, TW), np.float32)
    for s in range(NSEG):
        rot[s] = _rot_tables(np.arange(s * TW, (s + 1) * TW), 128)
    rot[NSEG, :, :, :, :4] = _rot_tables(np.arange(16384, 16388), 4)
    c["rot"] = rot
    wk = np.zeros((2, 128, H_A), np.float32)
    for h, g in enumerate(_gammas()):
        wk[0, :, h] = g ** 128.0
        wk[1, :4, h] = g ** 4.0
    c["wk"] = wk
    return c


class Prog:
    def __init__(self, n_pool, do_sample=True, nseg=NSEG, debug=False, ns=2):
        self.ns = ns
        self.n_pool = n_pool
        self.cur_es = None
        self.do_sample = do_sample
        self.nseg = nseg
        self.nc = nc = bass.Bass("TRN2", target_bir_lowering=False)
        self.es = contextlib.ExitStack()
        self.S = Sched(nc)
        self.wctr = 0
        self.pctr = 0
        self.tctr = 0

        def din(name, shape, dt=F32):
            return nc.dram_tensor(name, list(shape), dt, kind="ExternalInput").ap()

        def dout(name, shape, dt=F32):
            return nc.dram_tensor(name, list(shape), dt, kind="ExternalOutput").ap()

        def dscr(name, shape, dt=F32):
            return nc.dram_tensor(name, list(shape), dt, kind="Internal").ap()

        self.xp = din("xp", [SEQ, D])
        self.cp = din("cp", [D])
        self.norm_g = din("norm_g", [4, D])
        self.w_mod = din("w_mod", [4, D, 3 * D])
        self.b_mod = din("b_mod", [4, 3 * D])
        self.w_in_a = din("w_in_a", [2, D, 12288])
        self.w_out_a = din("w_out_a", [2, 4096, D])
        self.w_q_b = din("w_q_b", [2, D, 2 * D])
        self.w_o_b = din("w_o_b", [2, D, D])
        self.kv_norm_g = din("kv_norm_g", [D])
        self.w_mod_kv = din("w_mod_kv", [D, 2 * D])
        self.b_mod_kv = din("b_mod_kv", [2 * D])
        self.w_kv = din("w_kv", [D, 2 * D])
        self.final_g = din("final_g", [D])
        self.w_mod_f = din("w_mod_f", [D, 2 * D])
        self.b_mod_f = din("b_mod_f", [2 * D])
        self.c_ident_bf = din("ident_bf", [128, 128], BF16)
        self.c_ident_f = din("ident_f", [128, 128])
        self.c_caus_ml = din("caus_ml", [128, 128])
        self.c_tri_qk = din("tri_qk", [128, 128])
        self.c_rot = din("rot", [NSEG + 1, H_A, 4, 128, TW])
        self.c_wk = din("wk", [2, 128, H_A])
        if do_sample:
            self.xs = din("xs", [ns, 4, D])
            self.cs = din("cs", [ns, D])
            self.st_in = din("st_in", [ns, 2, H_A, 256, 512])
            self.c_onehot = din("onehot", [4, 4, 128], BF16)
            self.c_blockdiag = din("blockdiag", [64, D], BF16)
            self.c_selq = din("selq", [64, 4], BF16)
            self.c_negmaskn = din("negmaskn", [4, 64])
            self.c_piota = din("piota", [128, 1])
            self.cache_k = din("cache_k", [n_pool * 128, D])
            self.cache_v = din("cache_v", [n_pool * 128, D])
            self.ptab = din("ptab", [ns, NPAGES], I32)

        self.y_p = dout("y_p", [SEQ, D])
        self.stp = dout("stp", [2, H_A, 256, 512])
        self.k_p = dout("k_p", [SEQ, D])
        self.v_p = dout("v_p", [SEQ, D])
        if do_sample:
            self.y_s = dout("y_s", [ns, 4, D])
            self.sts = dout("sts", [ns, 2, H_A, 256, 512])
            self.k_s = dout("k_s", [ns, 4, D])
            self.v_s = dout("v_s", [ns, 4, D])

        self.NMOD = 4 * 3 * D + 2 * 2 * D
        self.mod_scr = dscr("mod_scr", [3, self.NMOD])
        self.bar_scr = dscr("bar_scr", [128, 1])
        self.r_mod_scr = Res("mod_scr")
        self.st_scr = dscr("st_scr", [2, H_A, 256, 512])
        self.r_st_scr = [[Res(f"st_scr{l}_{h}") for h in range(H_A)] for l in range(2)]
        self.kt_scr = dscr("kt_scr", [H_B, 128, SEQ], BF16)
        self.v_scr = dscr("v_scr", [SEQ, D], BF16)
        self.r_kt_scr = Res("kt_scr")
        self.r_v_scr = Res("v_scr")
        self.r_out = Res("outputs")

    def sb(self, name, shape, dt):
        es = self.cur_es if self.cur_es is not None else self.es
        return es.enter_context(self.nc.sbuf_tensor("sb_" + name, list(shape), dt))

    def ps(self, name, shape, dt):
        return self.es.enter_context(self.nc.psum_tensor(name, list(shape), dt))

    def alloc_common(self):
        self.ident_bf = self.sb("ident_bf_t", [128, 128], BF16)
        self.ident_f = self.sb("ident_f_t", [128, 128], F32)
        self.caus_ml = self.sb("caus_ml_t", [128, 128], F32)
        self.tri_qk = self.sb("tri_qk_t", [128, 128], F32)
        self.wk_t = self.sb("wk_t", [128, 2, H_A], F32)
        self.r_const = Res("const")
        S = self.S
        self.bar_t = self.sb("bar_t", [128, 4], F32)
        self.r_bar = [Res(f"bar{i}") for i in range(4)]
        self.eps_t = self.sb("eps_t", [128, 1], F32)
        S.add("dve", lambda e: e.memset(self.eps_t[:], EPS), writes=[self.r_const])
        S.add("sp", lambda e: e.dma_start(out=self.ident_bf[:], in_=self.c_ident_bf), writes=[self.r_const], dma=self.r_const)
        S.add("sp", lambda e: e.dma_start(out=self.ident_f[:], in_=self.c_ident_f), writes=[self.r_const], dma=self.r_const)
        S.add("sp", lambda e: e.dma_start(out=self.caus_ml[:], in_=self.c_caus_ml), writes=[self.r_const], dma=self.r_const)
        S.add("sp", lambda e: e.dma_start(out=self.tri_qk[:], in_=self.c_tri_qk), writes=[self.r_const], dma=self.r_const)
        S.add("sp", lambda e: e.dma_start(out=self.wk_t[:], in_=self.c_wk.rearrange("a p h -> p a h")), writes=[self.r_const], dma=self.r_const)
        self.NSLOT = 6
        self.wslot = [self.sb(f"wslot{i}", [128, 2048], BF16) for i in range(self.NSLOT)]
        self.r_w = [Res(f"wslot{i}") for i in range(self.NSLOT)]
        self.NPS = 6
        self.psf = [self.ps(f"psf{i}", [128, 512], F32) for i in range(self.NPS)]
        self.r_psf = [Res(f"psf{i}") for i in range(self.NPS)]
        self.psb = [self.ps(f"psb{i}", [128, 1024], BF16) for i in range(2)]
        self.r_psb = [Res(f"psb{i}") for i in range(2)]

    def barrier(self):
        S = self.S
        lasts = [S.streams[e][-1] for e in S.ENGS if S.streams[e]]
        dmas = [r.last_dma for r in S.dma_res if r.last_dma is not None]
        deps = lasts + dmas
        bt = self.bar_t
        S.add("dve", lambda e: e.memset(bt[:, 0:1], 0.0), writes=[self.r_bar[0]], extra_deps=deps)
        S.add("act", lambda e: e.activation(out=bt[:, 1:2], in_=self.eps_t[:, 0:1], func=AF.Copy), writes=[self.r_bar[1]], extra_deps=deps)
        S.add("pool", lambda e: e.memset(bt[:, 2:3], 0.0), writes=[self.r_bar[2]], extra_deps=deps)
        S.add("pe", lambda e: e.matmul(self.psf[0][0:1, 0:1], lhsT=self.ident_bf[0:1, 0:1], rhs=self.ident_bf[0:1, 0:1], start=True, stop=True),
              writes=[self.r_psf[0]], extra_deps=deps)
        S.add("sp", lambda e: e.dma_start(out=self.bar_scr, in_=self.eps_t[:, 0:1]), writes=[self.r_bar[3]], dma=self.r_bar[3], extra_deps=deps)

    def next_slot(self):
        i = self.wctr % self.NSLOT
        self.wctr += 1
        return i

    def next_ps(self):
        i = self.pctr % self.NPS
        self.pctr += 1
        return i

    def next_psb(self):
        i = self.tctr % 2
        self.tctr += 1
        return i

    def wload_T(self, W, c0):
        i = self.next_slot()
        src = W.rearrange("(k p) n -> p k n", p=128)[:, :, c0:c0 + 128]
        dst = self.wslot[i][:].rearrange("p (k n) -> p k n", n=128)
        self.S.add("pool", lambda e: e.dma_start(out=dst, in_=src), writes=[self.r_w[i]], dma=self.r_w[i])
        return i

    def wload_R(self, W, r0, c0):
        i = self.next_slot()
        src = W[r0:r0 + 512, c0:c0 + 512].rearrange("(j p) n -> p j n", p=128)
        dst = self.wslot[i][:].rearrange("p (j n) -> p j n", n=512)
        self.S.add("pool", lambda e: e.dma_start(out=dst, in_=src), writes=[self.r_w[i]], dma=self.r_w[i])
        return i

    def slotT(self, i):
        return self.wslot[i][:].rearrange("p (k n) -> p k n", n=128)

    def slotR(self, i):
        return self.wslot[i][:].rearrange("p (j n) -> p j n", n=512)

    def prepass_mods(self):
        S = self.S
        with contextlib.ExitStack() as es:
            nc = self.nc
            ccol = es.enter_context(nc.sbuf_tensor("ccol", [128, 3, 16], F32))
            cT = es.enter_context(nc.sbuf_tensor("cT", [128, 16, 128], BF16))
            r_ccol, r_cT = Res("ccol"), Res("cT")
            S.add("sp", lambda e: e.dma_start(out=ccol[:, 0, :], in_=self.cp.rearrange("(k p) -> p k", p=128),
                                              allow_slow_non_contiguous=True),
                  writes=[r_ccol], dma=r_ccol)
            S.add("dve", lambda e: e.memset(ccol[:, 1:3, :], 0.0), writes=[r_ccol])
            if self.do_sample:
                for si in range(self.ns):
                    S.add("sp", lambda e, si=si: e.dma_start(out=ccol[:, 1 + si, :], in_=self.cs[si].rearrange("(k p) -> p k", p=128),
                                                             allow_slow_non_contiguous=True),
                          writes=[r_ccol], dma=r_ccol)
            for g in range(4):
                S.add("dve", lambda e, g=g: e.tensor_copy(out=cT[:, :, g * 32:(g + 1) * 32],
                                                          in_=ccol[:, min(g, 2), :].unsqueeze(2).to_broadcast([128, 16, 32])),
                      reads=[r_ccol], writes=[r_cT])
            mats = [(self.w_mod[l], self.b_mod[l], 3 * D) for l in range(4)] + \
                   [(self.w_mod_kv, self.b_mod_kv, 2 * D), (self.w_mod_f, self.b_mod_f, 2 * D)]
            GW = 1024
            bwide = [es.enter_context(nc.sbuf_tensor(f"bwide{i}", [128, GW], F32)) for i in range(2)]
            mwide = [es.enter_context(nc.sbuf_tensor(f"mwide{i}", [128, GW], F32)) for i in range(2)]
            r_bw = [Res("bwide0"), Res("bwide1")]
            r_mw = [Res("mwide0"), Res("mwide1")]
            off = 0
            it = 0
            for W, b, n in mats:
                for g0 in range(0, n, GW):
                    gi = it % 2
                    it += 1
                    S.add("sp", lambda e, b=b, g0=g0, gi=gi: e.dma_start(out=bwide[gi][:], in_=b[g0:g0 + GW].partition_broadcast(128)),
                          writes=[r_bw[gi]], dma=r_bw[gi])
                    for c0 in range(g0, g0 + GW, 128):
                        si = self.wload_T(W, c0)
                        pi = self.next_ps()
                        for k in range(16):
                            S.add("pe", lambda e, k=k, si=si, pi=pi: e.matmul(self.psf[pi][:, 0:128], lhsT=cT[:, k, :], rhs=self.slotT(si)[:, k, :],
                                                                                start=(k == 0), stop=(k == 15)),
                                  reads=[r_cT, self.r_w[si]], writes=[self.r_psf[pi]])
                        cc = c0 - g0
                        S.add("dve", lambda e, pi=pi, gi=gi, cc=cc: e.tensor_tensor(out=mwide[gi][:, cc:cc + 128], in0=self.psf[pi][:, 0:128],
                                                                                    in1=bwide[gi][:, cc:cc + 128], op=ALU.add),
                              reads=[self.r_psf[pi], r_bw[gi]], writes=[r_mw[gi]])
                    o = off + g0
                    for ri, prow in enumerate((0, 32, 64)):
                        S.add("sp", lambda e, gi=gi, o=o, ri=ri, prow=prow: e.dma_start(out=self.mod_scr[ri:ri + 1, o:o + GW], in_=mwide[gi][prow:prow + 1, :]),
                              reads=[r_mw[gi]], writes=[self.r_mod_scr], dma=r_mw[gi])
                off += n

    def alloc_pass(self, tag, TT, NT):
        B = type("B", (), {})()
        B.TT, B.NT = TT, NT
        B.x = self.sb(f"x_{tag}", [128, TT, D], F32)
        B.r_x = [Res(f"x_{tag}{t}") for t in range(TT)]
        B.hT = self.sb(f"hT_{tag}", [128, 16, NT], BF16)
        B.r_hT = Res(f"hT_{tag}")
        B.a_rep = self.sb(f"a_rep_{tag}", [128, D], F32)
        B.sh_rep = self.sb(f"sh_rep_{tag}", [128, D], F32)
        B.gate_rep = self.sb(f"gate_rep_{tag}", [128, D], F32)
        B.r_a, B.r_sh, B.r_gate = Res("a_rep" + tag), Res("sh_rep" + tag), Res("gate_rep" + tag)
        B.ntmp = self.sb(f"ntmp_{tag}", [128, D], F32)
        B.r_ntmp = Res("ntmp" + tag)
        B.hb = self.sb(f"hb_{tag}", [128, D], BF16)
        B.r_hb = Res("hb" + tag)
        B.ss = self.sb(f"ss_{tag}", [128, 8], F32)
        B.r_ss = Res("ss" + tag)
        B.qT = self.sb(f"qT_{tag}", [128, 2, NT], BF16)
        B.kT = self.sb(f"kT_{tag}", [128, 2, NT], BF16)
        B.r_qT, B.r_kT = Res("qT" + tag), Res("kT" + tag)
        B.kw = self.sb(f"kw_{tag}", [128, TT, 256], BF16)
        B.r_kw = Res("kw" + tag)
        B.vT = self.sb(f"vT_{tag}", [128, 4, NT], BF16)
        B.r_vT = Res("vT" + tag)
        B.v = self.sb(f"v_{tag}", [128, TT, 512], BF16)
        B.r_v = Res("v" + tag)
        B.sgT = self.sb(f"sgT_{tag}", [128, 16, NT], BF16)
        B.r_sgT = Res("sgT" + tag)
        B.gT = self.sb(f"gT_{tag}", [128, 16, NT], BF16)
        B.r_gT = Res("gT" + tag)
        B.St = self.sb(f"St_{tag}", [128, 2, 512], F32)
        B.Sb = self.sb(f"Sb_{tag}", [128, 2, 512], BF16)
        B.r_St, B.r_Sb = Res("St" + tag), Res("Sb" + tag)
        B.rot = self.sb(f"rot_{tag}", [128, 4, NT], F32)
        B.r_rot = Res("rot" + tag)
        B.rt = self.sb(f"rt_{tag}", [128, 2, NT], F32)
        B.r_rt = Res("rt" + tag)
        B.AT = self.sb(f"AT_{tag}", [128, 128], BF16)
        B.r_AT = Res("AT" + tag)
        B.onb = self.sb(f"onb_{tag}", [128, 512], BF16)
        B.r_onb = Res("onb" + tag)
        return B

    def load_rep(self, B, dst, r_dst, row, off, n=D):
        self.S.add("sp", lambda e: e.dma_start(out=dst[:, 0:n], in_=self.mod_scr[row, off:off + n].partition_broadcast(128)),
                   reads=[self.r_mod_scr], writes=[r_dst], dma=r_dst)

    def mod_norm(self, B, row, g_dram, off_shift, off_scale, rows_last):
        S = self.S
        TT = B.TT
        self.load_rep(B, B.sh_rep, B.r_sh, row, off_shift)
        self.load_rep(B, B.a_rep, B.r_a, row, off_scale)
        S.add("sp", lambda e: e.dma_start(out=B.ntmp[:], in_=g_dram.partition_broadcast(128)), writes=[B.r_ntmp], dma=B.r_ntmp)
        S.add("dve", lambda e: e.scalar_tensor_tensor(out=B.a_rep[:], in0=B.a_rep[:], scalar=1.0, in1=B.ntmp[:], op0=ALU.add, op1=ALU.mult),
              reads=[B.r_a, B.r_ntmp], writes=[B.r_a])
        for t in range(TT):
            R = rows_last if t == TT - 1 else 128
            S.add("dve", lambda e, R=R: e.memset(B.ss[:R, 0:1], 0.0), writes=[B.r_ss])
            S.add("act", lambda e, t=t, R=R: e.activation(out=B.ntmp[:R, :], in_=B.x[:R, t, :], func=AF.Square, accum_out=B.ss[:R, 0:1]),
                  reads=[B.r_x[t], B.r_ss], writes=[B.r_ntmp, B.r_ss])
            S.add("act", lambda e, R=R: e.activation(out=B.ss[:R, 1:2], in_=B.ss[:R, 0:1], func=AF.Sqrt, scale=1.0 / D, bias=self.eps_t[:R, 0:1]),
                  reads=[B.r_ss, self.r_const], writes=[B.r_ss])
            S.add("dve", lambda e, R=R: e.reciprocal(out=B.ss[:R, 2:3], in_=B.ss[:R, 1:2]),
                  reads=[B.r_ss], writes=[B.r_ss])
            S.add("dve", lambda e, t=t, R=R: e.scalar_tensor_tensor(out=B.ntmp[:R, :], in0=B.x[:R, t, :], scalar=B.ss[:R, 2:3], in1=B.a_rep[:R, :],
                                                                     op0=ALU.mult, op1=ALU.mult),
                  reads=[B.r_x[t], B.r_ss, B.r_a], writes=[B.r_ntmp])
            S.add("dve", lambda e, R=R: e.tensor_tensor(out=B.hb[:R, :], in0=B.ntmp[:R, :], in1=B.sh_rep[:R, :], op=ALU.add),
                  reads=[B.r_ntmp, B.r_sh], writes=[B.r_hb])
            for g in range(2):
                pb = self.next_psb()
                for j in range(8):
                    k = g * 8 + j
                    S.add("pe", lambda e, k=k, j=j, pb=pb, R=R: e.transpose(out=self.psb[pb][:, j * 128:j * 128 + R], in_=B.hb[:R, k * 128:(k + 1) * 128],
                                                                            identity=self.ident_bf[:R, :R]),
                          reads=[B.r_hb, self.r_const], writes=[self.r_psb[pb]])
                S.add("act", lambda e, g=g, pb=pb, t=t, R=R: e.activation(
                    out=B.hT[:, g * 8:(g + 1) * 8, t * 128:t * 128 + R],
                    in_=self.psb[pb][:].rearrange("p (j n) -> p j n", n=128)[:, :, 0:R], func=AF.Copy),
                    reads=[self.r_psb[pb]], writes=[B.r_hT])

    def proj_T(self, B, W, c0, evac):
        S = self.S
        si = self.wload_T(W, c0)
        pi = self.next_ps()
        for k in range(16):
            S.add("pe", lambda e, k=k, si=si, pi=pi: e.matmul(self.psf[pi][:, 0:B.NT], lhsT=self.slotT(si)[:, k, :], rhs=B.hT[:, k, :],
                                                                start=(k == 0), stop=(k == 15)),
                  reads=[self.r_w[si], B.r_hT], writes=[self.r_psf[pi]])
        evac(pi)

    def residual_add(self, B, t, R, cb, pi):
        S = self.S
        S.add("dve", lambda e: e.tensor_tensor(out=B.ntmp[:R, cb * 512:(cb + 1) * 512], in0=self.psf[pi][:R, :], in1=B.gate_rep[:R, cb * 512:(cb + 1) * 512],
                                               op=ALU.mult),
              reads=[self.r_psf[pi], B.r_gate], writes=[B.r_ntmp])
        S.add("dve", lambda e: e.tensor_tensor(out=B.x[:R, t, cb * 512:(cb + 1) * 512], in0=B.x[:R, t, cb * 512:(cb + 1) * 512],
                                               in1=B.ntmp[:R, cb * 512:(cb + 1) * 512], op=ALU.add),
              reads=[B.r_ntmp, B.r_x[t]], writes=[B.r_x[t]])

    def retention_layer(self, B, l, pidx, row, rows_last, first, last, sample, st_in=None, sts=None):
        S = self.S
        TT, NT = B.TT, B.NT
        W = self.w_in_a[l]
        moff = l * 3 * D
        self.mod_norm(B, row, self.norm_g[l], moff, moff + D, rows_last)
        self.load_rep(B, B.gate_rep, B.r_gate, row, moff + 2 * D)
        gam = _gammas()
        L = rows_last if TT == 1 else 128
        for h in range(H_A):
            S.add("sp", lambda e, h=h: e.dma_start(out=B.rot[:], in_=self.c_rot[pidx, h, :, :, 0:NT].rearrange("a p n -> p a n")),
                  writes=[B.r_rot], dma=B.r_rot)
            for which, dstT, r_dst in ((0, B.qT, B.r_qT), (1, B.kT, B.r_kT)):
                pis = []
                for c in range(2):
                    self.proj_T(B, W, which * 2048 + h * 256 + c * 128, lambda pi: pis.append(pi))
                p1, p2 = pis
                cs_, sn_ = 2 * which, 2 * which + 1
                S.add("dve", lambda e, p1=p1, cs_=cs_: e.tensor_tensor(out=B.rt[:, 0, :], in0=self.psf[p1][:, 0:NT], in1=B.rot[:, cs_, :], op=ALU.mult),
                      reads=[self.r_psf[p1], B.r_rot], writes=[B.r_rt])
                S.add("dve", lambda e, p2=p2, sn_=sn_: e.tensor_tensor(out=B.rt[:, 1, :], in0=self.psf[p2][:, 0:NT], in1=B.rot[:, sn_, :], op=ALU.mult),
                      reads=[self.r_psf[p2], B.r_rot], writes=[B.r_rt])
                S.add("dve", lambda e, dstT=dstT: e.tensor_tensor(out=dstT[:, 0, :], in0=B.rt[:, 0, :], in1=B.rt[:, 1, :], op=ALU.subtract),
                      reads=[B.r_rt], writes=[r_dst])
                S.add("dve", lambda e, p1=p1, sn_=sn_: e.tensor_tensor(out=B.rt[:, 0, :], in0=self.psf[p1][:, 0:NT], in1=B.rot[:, sn_, :], op=ALU.mult),
                      reads=[self.r_psf[p1], B.r_rot, r_dst], writes=[B.r_rt])
                S.add("dve", lambda e, p2=p2, cs_=cs_: e.tensor_tensor(out=B.rt[:, 1, :], in0=self.psf[p2][:, 0:NT], in1=B.rot[:, cs_, :], op=ALU.mult),
                      reads=[self.r_psf[p2], B.r_rot], writes=[B.r_rt])
                S.add("dve", lambda e, dstT=dstT: e.tensor_tensor(out=dstT[:, 1, :], in0=B.rt[:, 0, :], in1=B.rt[:, 1, :], op=ALU.add),
                      reads=[B.r_rt], writes=[r_dst])
            for t in range(TT):
                R = rows_last if t == TT - 1 else 128
                pb = self.next_psb()
                for c in range(2):
                    S.add("pe", lambda e, c=c, pb=pb, t=t, R=R: e.transpose(out=self.psb[pb][:R, c * 128:(c + 1) * 128], in_=B.kT[:, c, t * 128:t * 128 + R],
                                                                            identity=self.ident_bf[:]),
                          reads=[B.r_kT, self.r_const], writes=[self.r_psb[pb]])
                S.add("act", lambda e, pb=pb, t=t, R=R, h=h: e.activation(out=B.kw[:R, t, :], in_=self.psb[pb][:R, 0:256], func=AF.Copy,
                                                                          scale=self.wk_t[:R, 1 if sample else 0, h:h + 1]),
                      reads=[self.r_psb[pb], self.r_const], writes=[B.r_kw])
            for j in range(4):
                self.proj_T(B, W, 4096 + h * 512 + j * 128,
                            lambda pi, j=j: S.add("act", lambda e: e.activation(out=B.vT[:, j, :], in_=self.psf[pi][:, 0:NT], func=AF.Copy),
                                                  reads=[self.r_psf[pi]], writes=[B.r_vT]))
            for t in range(TT):
                R = rows_last if t == TT - 1 else 128
                pb = self.next_psb()
                for j in range(4):
                    S.add("pe", lambda e, j=j, pb=pb, t=t, R=R: e.transpose(out=self.psb[pb][:R, j * 128:(j + 1) * 128], in_=B.vT[:, j, t * 128:t * 128 + R],
                                                                            identity=self.ident_bf[:]),
                          reads=[B.r_vT, self.r_const], writes=[self.r_psb[pb]])
                S.add("act", lambda e, pb=pb, t=t, R=R: e.activation(out=B.v[:R, t, :], in_=self.psb[pb][:R, 0:512], func=AF.Copy),
                      reads=[self.r_psb[pb]], writes=[B.r_v])
            for j in range(4):
                self.proj_T(B, W, 8192 + h * 512 + j * 128,
                            lambda pi, j=j: S.add("act", lambda e: e.activation(out=B.sgT[:, j, :], in_=self.psf[pi][:, 0:NT], func=AF.Silu),
                                                  reads=[self.r_psf[pi]], writes=[B.r_sgT]))
            r_scr = self.r_st_scr[l][h]
            if sample:
                S.add("sp", lambda e, h=h: e.dma_start(out=B.St[:], in_=st_in[l, h].rearrange("(c p) e -> p c e", p=128)),
                      writes=[B.r_St], dma=B.r_St)
            elif first:
                S.add("dve", lambda e: e.memset(B.St[:], 0.0), writes=[B.r_St])
            else:
                S.add("sp", lambda e, h=h: e.dma_start(out=B.St[:], in_=self.st_scr[l, h].rearrange("(c p) e -> p c e", p=128)),
                      reads=[r_scr], writes=[B.r_St], dma=B.r_St)
            S.add("act", lambda e: e.activation(out=B.Sb[:], in_=B.St[:], func=AF.Copy), reads=[B.r_St], writes=[B.r_Sb])
            gL = gam[h] ** L
            for t in range(TT):
                R = rows_last if t == TT - 1 else 128
                ts = slice(t * 128, t * 128 + R)
                pa = self.next_ps()
                for c in range(2):
                    S.add("pe", lambda e, c=c, pa=pa, ts=ts, R=R: e.matmul(self.psf[pa][:R, 0:R], lhsT=B.kT[:, c, ts], rhs=B.qT[:, c, ts], start=(c == 0), stop=(c == 1)),
                          reads=[B.r_kT, B.r_qT], writes=[self.r_psf[pa]])
                S.add("dve", lambda e, pa=pa, R=R: e.tensor_tensor(out=B.AT[:R, :R], in0=self.psf[pa][:R, 0:R], in1=self.caus_ml[:R, :R], op=ALU.mult),
                      reads=[self.r_psf[pa], self.r_const], writes=[B.r_AT])
                po = self.next_ps()
                for c in range(2):
                    S.add("pe", lambda e, c=c, po=po, ts=ts, R=R: e.matmul(self.psf[po][:R, :], lhsT=B.qT[:, c, ts], rhs=B.Sb[:, c, :], start=(c == 0), stop=False),
                          reads=[B.r_qT, B.r_Sb], writes=[self.r_psf[po]])
                S.add("pe", lambda e, po=po, t=t, R=R: e.matmul(self.psf[po][:R, :], lhsT=B.AT[:R, :R], rhs=B.v[:R, t, :], start=False, stop=True),
                      reads=[B.r_AT, B.r_v], writes=[self.r_psf[po]])
                for c in range(2):
                    pss = self.next_ps()
                    S.add("pe", lambda e, c=c, pss=pss, t=t, R=R: e.matmul(self.psf[pss][:, :], lhsT=B.kw[:R, t, c * 128:(c + 1) * 128], rhs=B.v[:R, t, :], start=True, stop=True),
                          reads=[B.r_kw, B.r_v], writes=[self.r_psf[pss]])
                    S.add("dve", lambda e, c=c, pss=pss, gL=gL: e.scalar_tensor_tensor(out=B.St[:, c, :], in0=B.St[:, c, :], scalar=float(gL), in1=self.psf[pss][:, :],
                                                                               op0=ALU.mult, op1=ALU.add),
                          reads=[B.r_St, self.r_psf[pss]], writes=[B.r_St])
                if t < TT - 1:
                    S.add("act", lambda e: e.activation(out=B.Sb[:], in_=B.St[:], func=AF.Copy), reads=[B.r_St], writes=[B.r_Sb])
                S.add("dve", lambda e, R=R: e.memset(B.ss[:R, 4:5], 0.0), writes=[B.r_ss])
                S.add("act", lambda e, po=po, R=R: e.activation(out=B.ntmp[:R, 0:512], in_=self.psf[po][:R, :], func=AF.Square, accum_out=B.ss[:R, 4:5]),
                      reads=[self.r_psf[po], B.r_ss], writes=[B.r_ntmp, B.r_ss])
                S.add("act", lambda e, R=R: e.activation(out=B.ss[:R, 5:6], in_=B.ss[:R, 4:5], func=AF.Sqrt, scale=1.0 / 512, bias=self.eps_t[:R, 0:1]),
                      reads=[B.r_ss, self.r_const], writes=[B.r_ss])
                S.add("dve", lambda e, R=R: e.reciprocal(out=B.ss[:R, 6:7], in_=B.ss[:R, 5:6]),
                      reads=[B.r_ss], writes=[B.r_ss])
                S.add("dve", lambda e, po=po, R=R: e.tensor_scalar(out=B.onb[:R, :], in0=self.psf[po][:R, :], scalar1=B.ss[:R, 6:7], scalar2=None, op0=ALU.mult),
                      reads=[self.r_psf[po], B.r_ss], writes=[B.r_onb])
                pb = self.next_psb()
                for j in range(4):
                    S.add("pe", lambda e, j=j, pb=pb, R=R: e.transpose(out=self.psb[pb][:, j * 128:j * 128 + R], in_=B.onb[:R, j * 128:(j + 1) * 128],
                                                                       identity=self.ident_bf[:R, :R]),
                          reads=[B.r_onb, self.r_const], writes=[self.r_psb[pb]])
                S.add("dve", lambda e, pb=pb, ts=ts, R=R: e.tensor_tensor(out=B.gT[:, 0:4, ts],
                                                                          in0=self.psb[pb][:, 0:512].rearrange("p (j n) -> p j n", n=128)[:, :, 0:R],
                                                                          in1=B.sgT[:, 0:4, ts], op=ALU.mult),
                      reads=[self.r_psb[pb], B.r_sgT], writes=[B.r_gT])
            if sample:
                S.add("sp", lambda e, h=h: e.dma_start(out=sts[l, h].rearrange("(c p) e -> p c e", p=128), in_=B.St[:]),
                      reads=[B.r_St], dma=B.r_St)
            elif last:
                S.add("sp", lambda e, h=h: e.dma_start(out=self.stp[l, h].rearrange("(c p) e -> p c e", p=128), in_=B.St[:]),
                      reads=[B.r_St], dma=B.r_St)
            else:
                S.add("sp", lambda e, h=h: e.dma_start(out=self.st_scr[l, h].rearrange("(c p) e -> p c e", p=128), in_=B.St[:]),
                      reads=[B.r_St], writes=[r_scr], dma=B.r_St)
            for cb in range(4):
                si = self.wload_R(self.w_out_a[l], h * 512, cb * 512)
                for t in range(TT):
                    R = rows_last if t == TT - 1 else 128
                    pi = self.next_ps()
                    for j in range(4):
                        S.add("pe", lambda e, j=j, si=si, pi=pi, t=t, R=R: e.matmul(self.psf[pi][:R, :], lhsT=B.gT[:, j, t * 128:t * 128 + R], rhs=self.slotR(si)[:, j, :],
                                                                                   start=(j == 0), stop=(j == 3)),
                              reads=[B.r_gT, self.r_w[si]], writes=[self.r_psf[pi]])
                    self.residual_add(B, t, R, cb, pi)

    def load_x(self, B, src, rows_last):
        for t in range(B.TT):
            R = rows_last if t == B.TT - 1 else 128
            self.S.add("sp", lambda e, t=t, R=R: e.dma_start(out=B.x[:R, t, :], in_=src[t * 128:t * 128 + R, :]),
                       writes=[B.r_x[t]], dma=B.r_x[t])

    def dump_x(self, B, dst, rows_last):
        for t in range(B.TT):
            R = rows_last if t == B.TT - 1 else 128
            self.S.add("sp", lambda e, t=t, R=R: e.dma_start(out=dst[t * 128:t * 128 + R, :], in_=B.x[:R, t, :]),
                       reads=[B.r_x[t]], dma=B.r_x[t])


def build_debug(stage):
    P = Prog(0, do_sample=False)
    P.dbg = P.nc.dram_tensor("dbg", [TW, D], F32, kind="ExternalOutput").ap()
    with P.es:
        P.alloc_common()
        P.prepass_mods()
        B = P.alloc_pass("p", 4, TW)
        P.load_x(B, P.xp[0:TW, :], 128)
        if stage >= 1:
            P.retention_layer(B, 0, 0, 0, 128, True, False, False)
        P.dump_x(B, P.dbg, 128)
        P.S.emit()
    return P


def _alloc_attn(P, tag, sample=False):
    A = type("A", (), {})()
    nb_ = 1 if sample else 2
    A.nb = nb_
    A.kvf = [P.sb(f"kvf{i}{tag}", [128, 512], F32) for i in range(nb_)]
    A.r_kvf = [Res(f"kvf{i}{tag}") for i in range(nb_)]
    A.kvb = [P.sb(f"kvb{i}{tag}", [128, 512], BF16) for i in range(nb_)]
    A.r_kvb = [Res(f"kvb{i}{tag}") for i in range(nb_)]
    if not sample:
        A.ktt = [P.sb(f"ktt{i}{tag}", [128, 4, 128], BF16) for i in range(2)]
        A.r_ktt = [Res(f"ktt{i}{tag}") for i in range(2)]
    A.msum = P.sb(f"msum{tag}", [128, H_B, 4], F32)
    A.r_msum = Res("msum" + tag)
    A.ctr = 0
    if not sample:
        A.KTh = P.sb("KTh", [128, SEQ], BF16)
        A.Vh = P.sb("Vh", [128, 16, 128], BF16)
        A.r_KTh, A.r_Vh = Res("KTh"), Res("Vh")
        A.Pm = P.sb("Pm", [128, SEQ], BF16)
        A.r_Pm = Res("Pm")
        A.PT = P.sb("PT", [128, 16, 128], BF16)
        A.r_PT = Res("PT")
        A.Pown = P.sb("Pown", [128, 128], F32)
        A.r_Pown = Res("Pown")
        A.sm = P.sb("sm", [128, 64], F32)
        A.r_sm = Res("sm")
    return A


def kv_proj(P, B, A, pidx, row, rows_last, sample, samp_i=0):
    S = P.S
    TT = B.TT
    off = 4 * 3 * D
    P.mod_norm(B, row, P.kv_norm_g, off, off + D, rows_last)
    import os
    kvmode = os.environ.get("KVMODE", "kv")
    for cb in range(8):
        isK = cb < 4
        if (isK and "k" not in kvmode) or ((not isK) and "v" not in kvmode):
            continue
        slots = [P.wload_R(P.w_kv, kk * 512, cb * 512) for kk in range(4)]
        for t in range(TT):
            R = rows_last if t == TT - 1 else 128
            pi = P.next_ps()
            for kk in range(4):
                for j in range(4):
                    S.add("pe", lambda e, kk=kk, j=j, pi=pi, t=t, R=R, si=slots[kk]: e.matmul(
                        P.psf[pi][:R, :], lhsT=B.hT[:, kk * 4 + j, t * 128:t * 128 + R], rhs=P.slotR(si)[:, j, :],
                        start=(kk == 0 and j == 0), stop=(kk == 3 and j == 3)),
                        reads=[B.r_hT, P.r_w[slots[kk]]], writes=[P.r_psf[pi]])
            bi = A.ctr % A.nb
            A.ctr += 1
            kvstep = int(os.environ.get("KVSTEP", "9"))
            if kvstep < 1:
                continue
            S.add("act", lambda e, pi=pi, bi=bi, R=R: e.activation(out=A.kvf[bi][:R, :], in_=P.psf[pi][:R, :], func=AF.Copy),
                  reads=[P.r_psf[pi]], writes=[A.r_kvf[bi]])
            if kvstep < 2:
                continue
            S.add("dve", lambda e, pi=pi, bi=bi, R=R: e.tensor_copy(out=A.kvb[bi][:R, :], in_=A.kvf[bi][:R, :]),
                  reads=[A.r_kvf[bi]], writes=[A.r_kvb[bi]])
            if kvstep < 3:
                continue
            cc = (cb % 4) * 512
            if sample:
                dst = (P.k_s if isK else P.v_s)[samp_i, 0:R, cc:cc + 512]
            else:
                r0 = pidx * TW + t * 128
                dst = (P.k_p if isK else P.v_p)[r0:r0 + R, cc:cc + 512]
            if os.environ.get("NOKVOUT") != "1":
                S.add("sp", lambda e, dst=dst, bi=bi, R=R: e.dma_start(out=dst, in_=A.kvf[bi][:R, :]),
                      reads=[A.r_kvf[bi]], dma=A.r_kvf[bi])
            if sample:
                if isK:
                    pb = P.next_psb()
                    for hh in range(4):
                        S.add("pe", lambda e, hh=hh, pb=pb, bi=bi, R=R: e.transpose(out=P.psb[pb][:, hh * 128:hh * 128 + R], in_=A.kvb[bi][:R, hh * 128:(hh + 1) * 128],
                                                                                  identity=P.ident_bf[:R, :R]),
                              reads=[A.r_kvb[bi], P.r_const], writes=[P.r_psb[pb]])
                    S.add("act", lambda e, pb=pb, cb=cb, R=R: e.activation(out=A.KTn[:, cb * 4:(cb + 1) * 4, 0:R],
                                                                           in_=P.psb[pb][:, 0:512].rearrange("p (h n) -> p h n", n=128)[:, :, 0:R], func=AF.Copy),
                          reads=[P.r_psb[pb]], writes=[A.r_KTn])
                else:
                    S.add("dve", lambda e, bi=bi, cc=cc, R=R: e.tensor_copy(out=A.Vn[:R, cc:cc + 512], in_=A.kvb[bi][:R, :]),
                          reads=[A.r_kvb[bi]], writes=[A.r_Vn])
                continue
            r0 = pidx * TW + t * 128
            if not isK:
                if os.environ.get("NOVSCR") != "1":
                    S.add("sp", lambda e, bi=bi, r0=r0, cc=cc: e.dma_start(out=P.v_scr[r0:r0 + 128, cc:cc + 512], in_=A.kvb[bi][:, :]),
                          reads=[A.r_kvb[bi]], writes=[P.r_v_scr], dma=A.r_kvb[bi])
            else:
                pb = P.next_psb()
                for hh in range(4):
                    S.add("pe", lambda e, hh=hh, pb=pb, bi=bi: e.transpose(out=P.psb[pb][:, hh * 128:(hh + 1) * 128], in_=A.kvb[bi][:, hh * 128:(hh + 1) * 128],
                                                                        identity=P.ident_bf[:]),
                          reads=[A.r_kvb[bi], P.r_const], writes=[P.r_psb[pb]])
                S.add("act", lambda e, pb=pb, bi=bi: e.activation(out=A.ktt[bi][:], in_=P.psb[pb][:, 0:512].rearrange("p (h n) -> p h n", n=128), func=AF.Copy),
                      reads=[P.r_psb[pb]], writes=[A.r_ktt[bi]])
                S.add("dve", lambda e, bi=bi, cb=cb, t=t: e.tensor_reduce(out=A.msum[:, cb * 4:(cb + 1) * 4, t], in_=A.ktt[bi][:], axis=AX.X, op=ALU.add),
                      reads=[A.r_ktt[bi]], writes=[A.r_msum])
                S.add("sp", lambda e, bi=bi, cb=cb, r0=r0: e.dma_start(out=P.kt_scr[cb * 4:(cb + 1) * 4, :, r0:r0 + 128].rearrange("h p n -> p h n"), in_=A.ktt[bi][:]),
                      reads=[A.r_ktt[bi]], writes=[P.r_kt_scr], dma=A.r_ktt[bi])
    if not sample:
        for bb in range(2):
            blk = pidx * 2 + bb
            S.add("dve", lambda e, bb=bb, blk=blk: e.tensor_tensor(out=P.meansT[:, :, blk], in0=A.msum[:, :, 2 * bb], in1=A.msum[:, :, 2 * bb + 1], op=ALU.add),
                  reads=[A.r_msum], writes=[P.r_means])
            S.add("dve", lambda e, blk=blk: e.tensor_scalar(out=P.meansT_bf[:, :, blk], in0=P.meansT[:, :, blk], scalar1=1.0 / 256, scalar2=None, op0=ALU.mult),
                  reads=[P.r_means], writes=[P.r_means_bf])


def moba_layer_prompt(P, B, A, l, pidx, row):
    S = P.S
    TT, NT = B.TT, B.NT
    li = l - 2
    W = P.w_q_b[li]
    moff = l * 3 * D
    P.mod_norm(B, row, P.norm_g[l], moff, moff + D, 128)
    P.load_rep(B, B.gate_rep, B.r_gate, row, moff + 2 * D)
    scale = 128 ** -0.5
    for j in range(16):
        P.proj_T(B, W, 2048 + j * 128,
                 lambda pi, j=j: S.add("act", lambda e: e.activation(out=B.sgT[:, j, :], in_=P.psf[pi][:, 0:NT], func=AF.Silu),
                                       reads=[P.r_psf[pi]], writes=[B.r_sgT]))
    nkmax = (pidx + 1) * TW
    sm = A.sm
    for h in range(H_B):
        P.proj_T(B, W, h * 128,
                 lambda pi: S.add("act", lambda e: e.activation(out=B.qT[:, 0, :], in_=P.psf[pi][:, 0:NT], func=AF.Copy),
                                  reads=[P.r_psf[pi]], writes=[B.r_qT]))
        S.add("sp", lambda e, h=h: e.dma_start(out=A.KTh[:, 0:nkmax], in_=P.kt_scr[h, :, 0:nkmax]),
              reads=[P.r_kt_scr], writes=[A.r_KTh], dma=A.r_KTh)
        S.add("sp", lambda e, h=h: e.dma_start(out=A.Vh[:, 0:nkmax // 128, :],
                                               in_=P.v_scr[0:nkmax, h * 128:(h + 1) * 128].rearrange("(c p) d -> p c d", p=128)),
              reads=[P.r_v_scr], writes=[A.r_Vh], dma=A.r_Vh)
        for qc in range(4):
            _moba_chunk(P, B, A, h, qc, pidx, scale)
    out_proj_full(P, B, P.w_o_b[li], 128)


def _moba_chunk(P, B, A, h, qc, pidx, scale):
    S = P.S
    sm = A.sm
    if True:
        if True:
            ac = pidx * 4 + qc
            ob = ac // 2
            nkf = ob * 256
            own = 128 if ac % 2 == 0 else 256
            NK = nkf + own
            qs = slice(qc * 128, (qc + 1) * 128)
            nb = (NK + 511) // 512
            banks = [P.next_ps() for _ in range(nb)]
            for i, bk in enumerate(banks):
                cols = min(512, NK - i * 512)
                S.add("pe", lambda e, i=i, bk=bk, cols=cols: e.matmul(P.psf[bk][:, 0:cols], lhsT=B.qT[:, 0, qs], rhs=A.KTh[:, i * 512:i * 512 + cols], start=True, stop=True),
                      reads=[B.r_qT, A.r_KTh], writes=[P.r_psf[bk]])
                S.add("dve", lambda e, i=i, bk=bk, cols=cols: e.tensor_reduce(out=sm[:, i:i + 1], in_=P.psf[bk][:, 0:cols], axis=AX.X, op=ALU.max),
                      reads=[P.r_psf[bk]], writes=[A.r_sm])
            S.add("dve", lambda e: e.tensor_reduce(out=sm[:, 4:5], in_=sm[:, 0:nb], axis=AX.X, op=ALU.max), reads=[A.r_sm], writes=[A.r_sm])
            S.add("dve", lambda e: e.tensor_scalar(out=sm[:, 5:6], in0=sm[:, 4:5], scalar1=-scale, scalar2=None, op0=ALU.mult), reads=[A.r_sm], writes=[A.r_sm])
            if ob > 3:
                pg = P.next_ps()
                S.add("pe", lambda e, pg=pg, h=h: e.matmul(P.psf[pg][:, 0:ob], lhsT=B.qT[:, 0, qs], rhs=P.meansT_bf[:, h, 0:ob], start=True, stop=True),
                      reads=[B.r_qT, P.r_means_bf], writes=[P.r_psf[pg]])
                S.add("dve", lambda e: e.memset(sm[:, 16:24], -1e30), writes=[A.r_sm])
                S.add("dve", lambda e, pg=pg: e.tensor_copy(out=sm[:, 16:16 + ob], in_=P.psf[pg][:, 0:ob]), reads=[P.r_psf[pg]], writes=[A.r_sm])
                S.add("dve", lambda e: e.max(out=sm[:, 24:32], in_=sm[:, 16:24]), reads=[A.r_sm], writes=[A.r_sm])
                S.add("dve", lambda e: e.tensor_scalar(out=sm[:, 8:8 + ob], in0=sm[:, 16:16 + ob], scalar1=sm[:, 26:27], scalar2=None, op0=ALU.is_ge),
                      reads=[A.r_sm], writes=[A.r_sm])
                S.add("dve", lambda e: e.tensor_scalar(out=sm[:, 8:8 + ob], in0=sm[:, 8:8 + ob], scalar1=-1.0, scalar2=30000.0, op0=ALU.add, op1=ALU.mult),
                      reads=[A.r_sm], writes=[A.r_sm])
                S.add("dve", lambda e: e.tensor_scalar(out=sm[:, 8:8 + ob], in0=sm[:, 8:8 + ob], scalar1=sm[:, 5:6], scalar2=None, op0=ALU.add),
                      reads=[A.r_sm], writes=[A.r_sm])
            elif ob > 0:
                S.add("dve", lambda e: e.tensor_copy(out=sm[:, 8:8 + ob], in_=sm[:, 5:6].to_broadcast([128, ob])), reads=[A.r_sm], writes=[A.r_sm])
            S.add("dve", lambda e: e.memset(sm[:, 32:48], 0.0), writes=[A.r_sm])
            for n in range(ob):
                bk = banks[(n * 256) // 512]
                co = (n * 256) % 512
                S.add("act", lambda e, n=n, bk=bk, co=co: e.activation(out=A.Pm[:, n * 256:(n + 1) * 256], in_=P.psf[bk][:, co:co + 256], func=AF.Exp,
                                                                       scale=scale, bias=sm[:, 8 + n:9 + n], accum_out=sm[:, 32 + n:33 + n]),
                      reads=[P.r_psf[bk], A.r_sm], writes=[A.r_Pm, A.r_sm])
            bk = banks[nkf // 512]
            co = nkf % 512
            nr = ob
            if own == 256:
                S.add("act", lambda e, bk=bk, co=co, nr=nr: e.activation(out=A.Pm[:, nkf:nkf + 128], in_=P.psf[bk][:, co:co + 128], func=AF.Exp,
                                                                         scale=scale, bias=sm[:, 5:6], accum_out=sm[:, 32 + nr:33 + nr]),
                      reads=[P.r_psf[bk], A.r_sm], writes=[A.r_Pm, A.r_sm])
                nr += 1
                co += 128
            oc = nkf + own - 128
            S.add("act", lambda e, bk=bk, co=co: e.activation(out=A.Pown[:], in_=P.psf[bk][:, co:co + 128], func=AF.Exp, scale=scale, bias=sm[:, 5:6]),
                  reads=[P.r_psf[bk], A.r_sm], writes=[A.r_Pown])
            S.add("dve", lambda e, oc=oc: e.tensor_tensor(out=A.Pm[:, oc:oc + 128], in0=A.Pown[:], in1=P.tri_qk[:], op=ALU.mult),
                  reads=[A.r_Pown, P.r_const], writes=[A.r_Pm])
            S.add("dve", lambda e, oc=oc, nr=nr: e.tensor_reduce(out=sm[:, 32 + nr:33 + nr], in_=A.Pm[:, oc:oc + 128], axis=AX.X, op=ALU.add),
                  reads=[A.r_Pm], writes=[A.r_sm])
            nr += 1
            S.add("dve", lambda e, nr=nr: e.tensor_reduce(out=sm[:, 6:7], in_=sm[:, 32:32 + nr], axis=AX.X, op=ALU.add), reads=[A.r_sm], writes=[A.r_sm])
            S.add("dve", lambda e: e.reciprocal(out=sm[:, 7:8], in_=sm[:, 6:7]), reads=[A.r_sm], writes=[A.r_sm])
            nkc = NK // 128
            for g0 in range(0, nkc, 8):
                g1 = min(nkc, g0 + 8)
                pb = P.next_psb()
                for kc in range(g0, g1):
                    S.add("pe", lambda e, kc=kc, pb=pb, g0=g0: e.transpose(out=P.psb[pb][:, (kc - g0) * 128:(kc - g0 + 1) * 128], in_=A.Pm[:, kc * 128:(kc + 1) * 128],
                                                                           identity=P.ident_bf[:]),
                          reads=[A.r_Pm, P.r_const], writes=[P.r_psb[pb]])
                S.add("dve", lambda e, pb=pb, g0=g0, g1=g1: e.tensor_copy(out=A.PT[:, g0:g1, :],
                                                                         in_=P.psb[pb][:, 0:(g1 - g0) * 128].rearrange("p (c n) -> p c n", n=128)),
                      reads=[P.r_psb[pb]], writes=[A.r_PT])
            po = P.next_ps()
            for kc in range(nkc):
                S.add("pe", lambda e, kc=kc, po=po: e.matmul(P.psf[po][:, 0:128], lhsT=A.PT[:, kc, :], rhs=A.Vh[:, kc, :], start=(kc == 0), stop=(kc == nkc - 1)),
                      reads=[A.r_PT, A.r_Vh], writes=[P.r_psf[po]])
            S.add("dve", lambda e, po=po: e.tensor_scalar(out=B.onb[:, 0:128], in0=P.psf[po][:, 0:128], scalar1=sm[:, 7:8], scalar2=None, op0=ALU.mult),
                  reads=[P.r_psf[po], A.r_sm], writes=[B.r_onb])
            pb = P.next_psb()
            S.add("pe", lambda e, pb=pb: e.transpose(out=P.psb[pb][:, 0:128], in_=B.onb[:, 0:128], identity=P.ident_bf[:]),
                  reads=[B.r_onb, P.r_const], writes=[P.r_psb[pb]])
            S.add("dve", lambda e, pb=pb, h=h: e.tensor_tensor(out=B.gT[:, h, qs], in0=P.psb[pb][:, 0:128], in1=B.sgT[:, h, qs], op=ALU.mult),
                  reads=[P.r_psb[pb], B.r_sgT], writes=[B.r_gT])


def out_proj_full(P, B, Wo, rows_last):
    S = P.S
    TT = B.TT
    for cb in range(4):
        slots = [P.wload_R(Wo, kk * 512, cb * 512) for kk in range(4)]
        for t in range(TT):
            R = rows_last if t == TT - 1 else 128
            pi = P.next_ps()
            for kk in range(4):
                for j in range(4):
                    S.add("pe", lambda e, kk=kk, j=j, pi=pi, t=t, R=R, si=slots[kk]: e.matmul(
                        P.psf[pi][:R, :], lhsT=B.gT[:, kk * 4 + j, t * 128:t * 128 + R], rhs=P.slotR(si)[:, j, :],
                        start=(kk == 0 and j == 0), stop=(kk == 3 and j == 3)),
                        reads=[B.r_gT, P.r_w[slots[kk]]], writes=[P.r_psf[pi]])
            P.residual_add(B, t, R, cb, pi)


def final_norm(P, B, row, dst, rows_last):
    S = P.S
    off = 4 * 3 * D + 2 * D
    P.load_rep(B, B.sh_rep, B.r_sh, row, off)
    P.load_rep(B, B.a_rep, B.r_a, row, off + D)
    S.add("sp", lambda e: e.dma_start(out=B.ntmp[:], in_=P.final_g.partition_broadcast(128)), writes=[B.r_ntmp], dma=B.r_ntmp)
    S.add("dve", lambda e: e.scalar_tensor_tensor(out=B.a_rep[:], in0=B.a_rep[:], scalar=1.0, in1=B.ntmp[:], op0=ALU.add, op1=ALU.mult),
          reads=[B.r_a, B.r_ntmp], writes=[B.r_a])
    for t in range(B.TT):
        R = rows_last if t == B.TT - 1 else 128
        S.add("dve", lambda e, R=R: e.memset(B.ss[:R, 0:1], 0.0), writes=[B.r_ss])
        S.add("act", lambda e, t=t, R=R: e.activation(out=B.ntmp[:R, :], in_=B.x[:R, t, :], func=AF.Square, accum_out=B.ss[:R, 0:1]),
              reads=[B.r_x[t], B.r_ss], writes=[B.r_ntmp, B.r_ss])
        S.add("act", lambda e, R=R: e.activation(out=B.ss[:R, 1:2], in_=B.ss[:R, 0:1], func=AF.Sqrt, scale=1.0 / D, bias=P.eps_t[:R, 0:1]),
              reads=[B.r_ss, P.r_const], writes=[B.r_ss])
        S.add("dve", lambda e, R=R: e.reciprocal(out=B.ss[:R, 2:3], in_=B.ss[:R, 1:2]), reads=[B.r_ss], writes=[B.r_ss])
        S.add("dve", lambda e, t=t, R=R: e.scalar_tensor_tensor(out=B.ntmp[:R, :], in0=B.x[:R, t, :], scalar=B.ss[:R, 2:3], in1=B.a_rep[:R, :],
                                                                 op0=ALU.mult, op1=ALU.mult),
              reads=[B.r_x[t], B.r_ss, B.r_a], writes=[B.r_ntmp])
        S.add("dve", lambda e, t=t, R=R: e.tensor_tensor(out=B.x[:R, t, :], in0=B.ntmp[:R, :], in1=B.sh_rep[:R, :], op=ALU.add),
              reads=[B.r_ntmp, B.r_sh, B.r_x[t]], writes=[B.r_x[t]])
        S.add("sp", lambda e, t=t, R=R: e.dma_start(out=dst[t * 128:t * 128 + R, :], in_=B.x[:R, t, :]),
              reads=[B.r_x[t]], dma=B.r_x[t])


def build_prompt_only(nseg=NSEG, nlayers=4):
    P = Prog(0, do_sample=False)
    with P.es:
        P.alloc_common()
        P.meansT = P.sb("meansT", [128, H_B, 8], F32)
        P.meansT_bf = P.sb("meansT_bf", [128, H_B, 8], BF16)
        P.r_means, P.r_means_bf = Res("means"), Res("means_bf")
        P.prepass_mods()
        B = P.alloc_pass("p", 4, TW)
        A = _alloc_attn(P, "p")
        for s in range(nseg):
            P.load_x(B, P.xp[s * TW:(s + 1) * TW, :], 128)
            for l in range(min(2, nlayers)):
                P.retention_layer(B, l, s, 0, 128, s == 0, s == nseg - 1, False)
            import os
            if nlayers >= 2 and os.environ.get("NOKV") != "1":
                kv_proj(P, B, A, s, 0, 128, False)
            for l in range(2, nlayers):
                moba_layer_prompt(P, B, A, l, s, 0)
            if nlayers == 4:
                final_norm(P, B, 0, P.y_p[s * TW:(s + 1) * TW, :], 128)
            else:
                P.dump_x(B, P.y_p[s * TW:(s + 1) * TW, :], 128)
        P.S.emit()
    return P


def _consts_sample():
    c = {}
    oh = np.zeros((4, 4, 128), np.float32)
    for q in range(4):
        oh[q, q, :] = 1.0
    c["onehot"] = oh.astype(ml_dtypes.bfloat16)
    bd = np.zeros((64, D), np.float32)
    sq = np.zeros((64, 4), np.float32)
    for h in range(H_B):
        for q in range(4):
            bd[h * 4 + q, h * 128:(h + 1) * 128] = 1.0
            sq[h * 4 + q, q] = 1.0
    c["blockdiag"] = bd.astype(ml_dtypes.bfloat16)
    c["selq"] = sq.astype(ml_dtypes.bfloat16)
    nm = np.zeros((4, 64), np.float32)
    for key in range(4):
        for h in range(H_B):
            for q in range(4):
                if key > q:
                    nm[key, h * 4 + q] = -30000.0
    c["negmaskn"] = nm
    c["piota"] = np.arange(128, dtype=np.float32).reshape(128, 1)
    return c


def alloc_sample_attn(P):
    A = _alloc_attn(P, "s", sample=True)
    A.KTn = P.sb("KTn", [128, H_B, 4], BF16)
    A.Vn = P.sb("Vn", [128, D], BF16)
    A.r_KTn, A.r_Vn = Res("KTn"), Res("Vn")
    A.ptf = P.sb("ptf", [128, NPAGES], F32)
    A.pti = P.sb("pti", [128, NPAGES], I32)
    A.idx = P.sb("idx", [128, NPAGES], I32)
    A.r_idx = Res("idx")
    A.kpage = [P.sb(f"kpage{i}", [128, D], F32) for i in range(2)]
    A.r_kpage = [Res(f"kpage{i}") for i in range(2)]
    A.KTp = [P.sb(f"KTp{i}", [128, H_B, 128], BF16) for i in range(2)]
    A.r_KTp = [Res(f"KTp{i}") for i in range(2)]
    A.ST = P.sb("ST", [128, NPAGES, 64], F32)
    A.r_ST = Res("ST")
    A.PTs = P.sb("PTs", [128, NPAGES, 64], BF16)
    A.r_PTs = Res("PTs")
    A.vb = [P.sb("vb0", [128, D], BF16)] * 2
    A.r_vb = [Res("vb0")] * 2
    A.qTs = P.sb("qTs", [128, H_B, 4], BF16)
    A.r_qTs = Res("qTs")
    A.msumS = P.sb("msumS", [128, H_B, NPAGES], F32)
    A.meansS_bf = P.sb("meansS_bf", [128, H_B, 64], BF16)
    A.r_msumS, A.r_meansS = Res("msumS"), Res("meansS")
    A.selrep = P.sb("selrep", [128, 4, 1024], BF16)
    A.r_selrep = Res("selrep")
    A.sc = P.sb("sc", [128, 512], F32)
    A.r_sc = Res("sc")
    A.scb = P.sb("scb", [128, 256], BF16)
    A.r_scb = Res("scb")
    A.gt = P.sb("gt", [4, 1024], F32)
    A.selb = P.sb("selb", [4, 1024], BF16)
    A.r_gt = Res("gt")
    A.Om = P.sb("Om", [64, D], BF16)
    A.r_Om = Res("Om")
    A.onehot = P.sb("onehot_t", [4, 4, 128], BF16)
    A.blockdiag = P.sb("blockdiag_t", [64, D], BF16)
    A.selq = P.sb("selq_t", [64, 4], BF16)
    A.negmaskn = P.sb("negmaskn_t", [4, 64], F32)
    A.piota = P.sb("piota_t", [128, 1], F32)
    A.r_c = Res("sconst")
    S = P.S
    for dst, src in ((A.onehot, P.c_onehot), (A.blockdiag, P.c_blockdiag), (A.selq, P.c_selq), (A.negmaskn, P.c_negmaskn), (A.piota, P.c_piota)):
        S.add("sp", lambda e, dst=dst, src=src: e.dma_start(out=dst[:], in_=src), writes=[A.r_c], dma=A.r_c)
    S.add("dve", lambda e: e.memset(A.scb[:, 128:129], 1.0), writes=[A.r_c])
    return A


def sample_page_index(P, A, si):
    S = P.S
    S.add("sp", lambda e: e.dma_start(out=A.pti[:], in_=P.ptab[si].partition_broadcast(128)), writes=[A.r_idx], dma=A.r_idx)
    S.add("dve", lambda e: e.tensor_copy(out=A.ptf[:], in_=A.pti[:]), reads=[A.r_idx], writes=[A.r_idx])
    S.add("dve", lambda e: e.tensor_scalar(out=A.ptf[:], in0=A.ptf[:], scalar1=128.0, scalar2=A.piota[:, 0:1], op0=ALU.mult, op1=ALU.add),
          reads=[A.r_idx, A.r_c], writes=[A.r_idx])
    S.add("dve", lambda e: e.tensor_copy(out=A.idx[:], in_=A.ptf[:]), reads=[A.r_idx], writes=[A.r_idx])


def moba_layer_sample(P, B, A, l, row, si, do_means):
    S = P.S
    NT = B.NT
    li = l - 2
    W = P.w_q_b[li]
    moff = l * 3 * D
    scale = 128 ** -0.5
    P.mod_norm(B, row, P.norm_g[l], moff, moff + D, 4)
    P.load_rep(B, B.gate_rep, B.r_gate, row, moff + 2 * D)
    for j in range(16):
        P.proj_T(B, W, 2048 + j * 128,
                 lambda pi, j=j: S.add("act", lambda e: e.activation(out=B.sgT[:, j, :], in_=P.psf[pi][:, 0:NT], func=AF.Silu),
                                       reads=[P.r_psf[pi]], writes=[B.r_sgT]))
    for h in range(H_B):
        P.proj_T(B, W, h * 128,
                 lambda pi, h=h: S.add("act", lambda e: e.activation(out=A.qTs[:, h, :], in_=P.psf[pi][:, 0:NT], func=AF.Copy),
                                       reads=[P.r_psf[pi]], writes=[A.r_qTs]))
    sc, scb = A.sc, A.scb
    import os
    mst = int(os.environ.get("MSTAGE", "9"))
    if mst < 2:
        out_proj_full(P, B, P.w_o_b[li], 4)
        return
    for j in range(NPAGES):
        b = j % 2
        S.add("pool", lambda e, j=j, b=b: e.indirect_dma_start(out=A.kpage[b][:], out_offset=None, in_=P.cache_k,
                                                              in_offset=bass.IndirectOffsetOnAxis(ap=A.idx[:, j:j + 1], axis=0)),
              reads=[A.r_idx], writes=[A.r_kpage[b]], dma=A.r_kpage[b])
        if mst < 3:
            S.add("dve", lambda e, j=j, b=b: e.tensor_copy(out=A.ST[:, j, :], in_=A.kpage[b][:, 0:64]), reads=[A.r_kpage[b]], writes=[A.r_ST])
            continue
        for g in range(4):
            pi = P.next_ps()
            for hh in range(4):
                h = g * 4 + hh
                S.add("pe", lambda e, pi=pi, hh=hh, h=h, b=b: e.transpose(out=P.psf[pi][:, hh * 128:(hh + 1) * 128], in_=A.kpage[b][:, h * 128:(h + 1) * 128],
                                                                        identity=P.ident_f[:]),
                      reads=[A.r_kpage[b], P.r_const], writes=[P.r_psf[pi]])
            S.add("act", lambda e, pi=pi, g=g, b=b: e.activation(out=A.KTp[b][:, g * 4:(g + 1) * 4, :], in_=P.psf[pi][:].rearrange("p (h n) -> p h n", n=128), func=AF.Copy),
                  reads=[P.r_psf[pi]], writes=[A.r_KTp[b]])
        if do_means:
            S.add("dve", lambda e, j=j, b=b: e.tensor_reduce(out=A.msumS[:, :, j], in_=A.KTp[b][:], axis=AX.X, op=ALU.add),
                  reads=[A.r_KTp[b]], writes=[A.r_msumS])
        pst = P.next_ps()
        for h in range(H_B):
            S.add("pe", lambda e, pst=pst, h=h, b=b: e.matmul(P.psf[pst][:, h * 4:(h + 1) * 4], lhsT=A.KTp[b][:, h, :], rhs=A.qTs[:, h, :], start=True, stop=True),
                  reads=[A.r_KTp[b], A.r_qTs], writes=[P.r_psf[pst]])
        S.add("act", lambda e, pst=pst, j=j: e.activation(out=A.ST[:, j, :], in_=P.psf[pst][:, 0:64], func=AF.Copy),
              reads=[P.r_psf[pst]], writes=[A.r_ST])
    if do_means:
        S.add("dve", lambda e: e.tensor_tensor(out=A.meansS_bf[:], in0=A.msumS[:].rearrange("p h (b g) -> p h b g", g=2)[:, :, :, 0],
                                               in1=A.msumS[:].rearrange("p h (b g) -> p h b g", g=2)[:, :, :, 1], op=ALU.add),
              reads=[A.r_msumS], writes=[A.r_meansS])
    if mst < 4:
        out_proj_full(P, B, P.w_o_b[li], 4)
        return
    pn = P.next_ps()
    for h in range(H_B):
        S.add("pe", lambda e, h=h: e.matmul(P.psf[pn][0:4, h * 4:(h + 1) * 4], lhsT=A.KTn[:, h, 0:4], rhs=A.qTs[:, h, :], start=True, stop=True),
              reads=[A.r_KTn, A.r_qTs], writes=[P.r_psf[pn]])
    S.add("dve", lambda e: e.tensor_tensor(out=sc[0:4, 128:192], in0=P.psf[pn][0:4, 0:64], in1=A.negmaskn[:], op=ALU.add),
          reads=[P.r_psf[pn], A.r_c], writes=[A.r_sc])
    S.add("dve", lambda e: e.tensor_reduce(out=sc[:, 0:64], in_=A.ST[:].rearrange("p j c -> p c j"), axis=AX.X, op=ALU.max),
          reads=[A.r_ST], writes=[A.r_sc])
    S.add("dve", lambda e: e.tensor_tensor(out=sc[0:4, 0:64], in0=sc[0:4, 0:64], in1=sc[0:4, 128:192], op=ALU.max), reads=[A.r_sc], writes=[A.r_sc])
    pm = P.next_ps()
    S.add("pe", lambda e: e.transpose(out=P.psf[pm][0:64, 0:128], in_=sc[:, 0:64], identity=P.ident_f[:]),
          reads=[A.r_sc, P.r_const], writes=[P.r_psf[pm]])
    S.add("dve", lambda e: e.tensor_reduce(out=sc[0:64, 320:321], in_=P.psf[pm][0:64, 0:128], axis=AX.X, op=ALU.max), reads=[P.r_psf[pm]], writes=[A.r_sc])
    S.add("dve", lambda e: e.tensor_scalar(out=scb[0:64, 0:64], in0=P.ident_f[0:64, 0:64], scalar1=sc[0:64, 320:321], scalar2=None, op0=ALU.mult),
          reads=[A.r_sc, P.r_const], writes=[A.r_scb])
    pm2 = P.next_ps()
    S.add("pe", lambda e: e.matmul(P.psf[pm2][:, 0:64], lhsT=P.ones_bf[0:64, :], rhs=scb[0:64, 0:64], start=True, stop=True),
          reads=[A.r_scb, P.r_const], writes=[P.r_psf[pm2]])
    S.add("dve", lambda e: e.tensor_copy(out=sc[:, 64:128], in_=P.psf[pm2][:, 0:64]), reads=[P.r_psf[pm2]], writes=[A.r_sc])
    pgs = [P.next_ps(), P.next_ps()]
    for h in range(H_B):
        S.add("pe", lambda e, h=h: e.matmul(P.psf[pgs[h // 8]][0:4, (h % 8) * 64:(h % 8 + 1) * 64], lhsT=A.qTs[:, h, :], rhs=A.meansS_bf[:, h, :], start=True, stop=True),
              reads=[A.r_qTs, A.r_meansS], writes=[P.r_psf[pgs[h // 8]]])
    for g in range(2):
        S.add("dve", lambda e, g=g: e.tensor_copy(out=A.gt[0:4, g * 512:(g + 1) * 512], in_=P.psf[pgs[g]][0:4, :]), reads=[P.r_psf[pgs[g]]], writes=[A.r_gt])
    for h in range(H_B):
        S.add("dve", lambda e, h=h: e.max(out=sc[0:4, 192 + h * 8:200 + h * 8], in_=A.gt[0:4, h * 64:(h + 1) * 64]), reads=[A.r_gt], writes=[A.r_sc])
        S.add("dve", lambda e, h=h: e.tensor_scalar(out=A.selb[0:4, h * 64:(h + 1) * 64], in0=A.gt[0:4, h * 64:(h + 1) * 64],
                                                    scalar1=sc[0:4, 194 + h * 8:195 + h * 8], scalar2=None, op0=ALU.is_ge),
              reads=[A.r_gt, A.r_sc], writes=[A.r_gt])
    for q in range(4):
        for half in range(2):
            pr = P.next_ps()
            S.add("pe", lambda e, q=q, half=half, pr=pr: e.matmul(P.psf[pr][:, :], lhsT=A.onehot[0:4, q, :], rhs=A.selb[0:4, half * 512:(half + 1) * 512], start=True, stop=True),
                  reads=[A.r_gt, A.r_c], writes=[P.r_psf[pr]])
            S.add("act", lambda e, q=q, half=half, pr=pr: e.activation(out=A.selrep[:, q, half * 512:(half + 1) * 512], in_=P.psf[pr][:, :], func=AF.Copy),
                  reads=[P.r_psf[pr]], writes=[A.r_selrep])
    S.add("dve", lambda e: e.tensor_tensor(out=A.ST[:], in0=A.ST[:], in1=sc[:, 64:128].unsqueeze(1).to_broadcast([128, NPAGES, 64]), op=ALU.subtract),
          reads=[A.r_ST, A.r_sc], writes=[A.r_ST])
    S.add("act", lambda e: e.activation(out=A.PTs[:], in_=A.ST[:], func=AF.Exp, scale=scale), reads=[A.r_ST], writes=[A.r_PTs])
    for q in range(4):
        S.add("dve", lambda e, q=q: e.tensor_tensor(
            out=A.PTs[:].rearrange("p (b g) (h q) -> p b g h q", g=2, q=4)[:, :, :, :, q],
            in0=A.PTs[:].rearrange("p (b g) (h q) -> p b g h q", g=2, q=4)[:, :, :, :, q],
            in1=A.selrep[:, q, :].rearrange("p (h b) -> p b h", b=64).unsqueeze(2).to_broadcast([128, 64, 2, H_B]), op=ALU.mult),
            reads=[A.r_PTs, A.r_selrep], writes=[A.r_PTs])
    S.add("dve", lambda e: e.tensor_tensor(out=sc[0:4, 128:192], in0=sc[0:4, 128:192], in1=sc[0:4, 64:128], op=ALU.subtract), reads=[A.r_sc], writes=[A.r_sc])
    S.add("act", lambda e: e.activation(out=scb[0:4, 64:128], in_=sc[0:4, 128:192], func=AF.Exp, scale=scale), reads=[A.r_sc], writes=[A.r_scb])
    if mst < 5:
        out_proj_full(P, B, P.w_o_b[li], 4)
        return
    po = [P.next_ps() for _ in range(4)]
    prs = P.next_ps()
    for j in range(NPAGES):
        b = j % 2
        S.add("pool", lambda e, j=j, b=b: e.indirect_dma_start(out=A.kpage[b][:], out_offset=None, in_=P.cache_v,
                                                              in_offset=bass.IndirectOffsetOnAxis(ap=A.idx[:, j:j + 1], axis=0)),
              reads=[A.r_idx], writes=[A.r_kpage[b]], dma=A.r_kpage[b])
        if j % 2 == 0:
            S.add("act", lambda e, b=b: e.activation(out=A.vb[b][:], in_=A.kpage[b][:], func=AF.Copy), reads=[A.r_kpage[b]], writes=[A.r_vb[b]])
        else:
            S.add("dve", lambda e, b=b: e.tensor_copy(out=A.vb[b][:], in_=A.kpage[b][:]), reads=[A.r_kpage[b]], writes=[A.r_vb[b]])
        for cbk in range(4):
            S.add("pe", lambda e, j=j, b=b, cbk=cbk: e.matmul(P.psf[po[cbk]][0:64, :], lhsT=A.PTs[:, j, :], rhs=A.vb[b][:, cbk * 512:(cbk + 1) * 512], start=(j == 0), stop=False),
                  reads=[A.r_PTs, A.r_vb[b]], writes=[P.r_psf[po[cbk]]])
        S.add("pe", lambda e, j=j: e.matmul(P.psf[prs][0:64, 0:8], lhsT=A.PTs[:, j, :], rhs=P.ones_bf[:, 0:8], start=(j == 0), stop=False),
              reads=[A.r_PTs, P.r_const], writes=[P.r_psf[prs]])
    for cbk in range(4):
        S.add("pe", lambda e, cbk=cbk: e.matmul(P.psf[po[cbk]][0:64, :], lhsT=scb[0:4, 64:128], rhs=A.Vn[0:4, cbk * 512:(cbk + 1) * 512], start=False, stop=True),
              reads=[A.r_scb, A.r_Vn], writes=[P.r_psf[po[cbk]]])
    S.add("pe", lambda e: e.matmul(P.psf[prs][0:64, 0:8], lhsT=scb[0:4, 64:128], rhs=P.ones_bf[0:4, 0:8], start=False, stop=True),
          reads=[A.r_scb, P.r_const], writes=[P.r_psf[prs]])
    if mst < 6:
        out_proj_full(P, B, P.w_o_b[li], 4)
        return
    S.add("dve", lambda e: e.reciprocal(out=sc[0:64, 321:322], in_=P.psf[prs][0:64, 0:1]), reads=[P.r_psf[prs]], writes=[A.r_sc])
    for cbk in range(4):
        S.add("dve", lambda e, cbk=cbk: e.scalar_tensor_tensor(out=A.Om[:, cbk * 512:(cbk + 1) * 512], in0=P.psf[po[cbk]][0:64, :], scalar=sc[0:64, 321:322],
                                                              in1=A.blockdiag[:, cbk * 512:(cbk + 1) * 512], op0=ALU.mult, op1=ALU.mult),
              reads=[P.r_psf[po[cbk]], A.r_sc, A.r_c], writes=[A.r_Om])
    pb = P.next_psb()
    for cbk in range(4):
        pt = P.next_ps()
        S.add("pe", lambda e, cbk=cbk, pt=pt: e.matmul(P.psf[pt][0:4, :], lhsT=A.selq[:, :], rhs=A.Om[:, cbk * 512:(cbk + 1) * 512], start=True, stop=True),
              reads=[A.r_Om, A.r_c], writes=[P.r_psf[pt]])
        S.add("act", lambda e, cbk=cbk, pt=pt: e.activation(out=B.hb[0:4, cbk * 512:(cbk + 1) * 512], in_=P.psf[pt][0:4, :], func=AF.Copy),
              reads=[P.r_psf[pt]], writes=[B.r_hb])
    for h in range(H_B):
        S.add("pe", lambda e, h=h: e.transpose(out=P.psb[pb][:, h * 4:(h + 1) * 4], in_=B.hb[0:4, h * 128:(h + 1) * 128], identity=P.ident_bf[0:4, 0:4]),
              reads=[B.r_hb, P.r_const], writes=[P.r_psb[pb]])
    S.add("dve", lambda e: e.tensor_tensor(out=B.gT[:, :, 0:4], in0=P.psb[pb][:, 0:64].rearrange("p (h q) -> p h q", q=4), in1=B.sgT[:, :, 0:4], op=ALU.mult),
          reads=[P.r_psb[pb], B.r_sgT], writes=[B.r_gT])
    out_proj_full(P, B, P.w_o_b[li], 4)


def build_full(n_pool, ns=2, nseg=NSEG, prompt=True):
    P = Prog(n_pool, do_sample=True, ns=ns)
    with P.es:
        P.alloc_common()
        P.ones_bf = P.sb("ones_bf", [128, 128], BF16)
        P.S.add("dve", lambda e: e.memset(P.ones_bf[:], 1.0), writes=[P.r_const])
        P.prepass_mods()
        P.barrier()
        if prompt:
            P.cur_es = contextlib.ExitStack()
            P.meansT = P.sb("meansT", [128, H_B, 8], F32)
            P.meansT_bf = P.sb("meansT_bf", [128, H_B, 8], BF16)
            P.r_means, P.r_means_bf = Res("means"), Res("means_bf")
            B = P.alloc_pass("p", 4, TW)
            A = _alloc_attn(P, "p")
            for s in range(nseg):
                P.load_x(B, P.xp[s * TW:(s + 1) * TW, :], 128)
                for l in range(2):
                    P.retention_layer(B, l, s, 0, 128, s == 0, s == nseg - 1, False)
                kv_proj(P, B, A, s, 0, 128, False)
                for l in range(2, 4):
                    moba_layer_prompt(P, B, A, l, s, 0)
                final_norm(P, B, 0, P.y_p[s * TW:(s + 1) * TW, :], 128)
            P.barrier()
            P.cur_es.close()
        P.cur_es = contextlib.ExitStack()
        Bs = P.alloc_pass("s", 1, 4)
        As = alloc_sample_attn(P)
        for si in range(ns):
            row = 1 + si
            sample_page_index(P, As, si)
            P.load_x(Bs, P.xs[si], 4)
            import os
            sst = int(os.environ.get("SSTAGE", "9"))
            for l in range(2):
                if sst >= 1:
                    P.retention_layer(Bs, l, NSEG, row, 4, False, False, True, st_in=P.st_in[si], sts=P.sts[si])
            if sst >= 2:
                kv_proj(P, Bs, As, NSEG, row, 4, True, samp_i=si)
            for l in range(2, 4):
                if sst >= 3:
                    moba_layer_sample(P, Bs, As, l, row, si, l == 2)
            final_norm(P, Bs, row, P.y_s[si], 4)
        P.S.emit()
        P.cur_es.close()
    return P


NS_PER_CORE = 2


def kernel(x_prompt, x_sample, state_ret, cache_k, cache_v, page_table, c_prompt, c_sample,
           norm_g, w_mod, b_mod, w_in_a, w_out_a, w_q_b, w_o_b,
           kv_norm_g, w_mod_kv, b_mod_kv, w_kv, final_g, w_mod_f, b_mod_f):
    ns = NS_PER_CORE
    ncores = 8 // ns
    f32 = lambda a: np.ascontiguousarray(np.asarray(a), dtype=np.float32)
    x_prompt, x_sample, state_ret = f32(x_prompt), f32(x_sample), f32(state_ret)
    cache_k, cache_v = f32(cache_k), f32(cache_v)
    page_table = np.ascontiguousarray(np.asarray(page_table), dtype=np.int32)
    n_pool = cache_k.shape[0]
    P = build_full(n_pool, ns=ns)
    shared = {"norm_g": f32(norm_g), "w_mod": f32(w_mod), "b_mod": f32(b_mod), "w_in_a": f32(w_in_a), "w_out_a": f32(w_out_a),
              "w_q_b": f32(w_q_b), "w_o_b": f32(w_o_b), "kv_norm_g": f32(kv_norm_g), "w_mod_kv": f32(w_mod_kv),
              "b_mod_kv": f32(b_mod_kv), "w_kv": f32(w_kv), "final_g": f32(final_g), "w_mod_f": f32(w_mod_f), "b_mod_f": f32(b_mod_f),
              "cache_k": cache_k.reshape(n_pool * 128, D), "cache_v": cache_v.reshape(n_pool * 128, D)}
    shared.update(_consts())
    shared.update(_consts_sample())
    c_prompt, c_sample = f32(c_prompt), f32(c_sample)
    in_maps = []
    for c in range(ncores):
        b = (c * ns) // 2
        s0 = c * ns
        m = dict(shared)
        m["xp"] = x_prompt[b]
        m["cp"] = c_prompt[b]
        m["xs"] = x_sample[s0:s0 + ns]
        m["cs"] = c_sample[s0:s0 + ns]
        m["st_in"] = np.ascontiguousarray(state_ret[:, s0:s0 + ns].transpose(1, 0, 2, 3, 4))
        m["ptab"] = page_table[s0:s0 + ns]
        in_maps.append(m)
    res = run_bass_kernel_spmd(P.nc, in_maps, core_ids=list(range(ncores)))
    r = res.results
    nb, nsamp = x_prompt.shape[0], x_sample.shape[0]
    y_prompt = np.zeros((nb, SEQ, D), np.float32)
    y_sample = np.zeros((nsamp, 4, D), np.float32)
    st_p = np.zeros((2, nb, H_A, 256, 512), np.float32)
    st_s = np.zeros((2, nsamp, H_A, 256, 512), np.float32)
    k_p = np.zeros((nb, SEQ, H_B, 128), np.float32)
    v_p = np.zeros((nb, SEQ, H_B, 128), np.float32)
    k_s = np.zeros((nsamp, 4, H_B, 128), np.float32)
    v_s = np.zeros((nsamp, 4, H_B, 128), np.float32)
    for c in range(ncores):
        b = (c * ns) // 2
        if (c * ns) % 2 == 0:
            y_prompt[b] = r[c]["y_p"]
            st_p[:, b] = r[c]["stp"]
            k_p[b] = r[c]["k_p"].reshape(SEQ, H_B, 128)
            v_p[b] = r[c]["v_p"].reshape(SEQ, H_B, 128)
        for i in range(ns):
            s = c * ns + i
            y_sample[s] = r[c]["y_s"][i]
            st_s[:, s] = r[c]["sts"][i]
            k_s[s] = r[c]["k_s"][i].reshape(4, H_B, 128)
            v_s[s] = r[c]["v_s"][i].reshape(4, H_B, 128)
    return (y_prompt, y_sample, st_p, st_s, k_p, v_p, k_s, v_s)
```

```python
import contextlib
import math
import numpy as np
import ml_dtypes
import concourse.bass as bass
import concourse.mybir as mybir
from concourse.bass_utils import run_bass_kernel_spmd

F32 = mybir.dt.float32
BF16 = mybir.dt.bfloat16
I32 = mybir.dt.int32
AF = mybir.ActivationFunctionType
ALU = mybir.AluOpType
AX = mybir.AxisListType

D = 2048
SEQ = 2048
TW = 512
NSEG = 4
H_A = 8
H_B = 16
EPS = 1e-6
NPAGES = 128
SAME_ENGINE_RAW_SYNC = True


class Res:
    __slots__ = ("name", "last_w", "readers", "dsem", "dcnt", "last_dma")

    def __init__(self, name):
        self.name = name
        self.last_w = None
        self.readers = []
        self.dsem = None
        self.dcnt = 0
        self.last_dma = None


class Op:
    __slots__ = ("eng", "fn", "is_dma", "deps", "marked", "seq", "dres", "dval")

    def __init__(self, eng, fn, is_dma):
        self.eng = eng
        self.fn = fn
        self.is_dma = is_dma
        self.deps = []
        self.marked = False
        self.seq = 0
        self.dres = None
        self.dval = 0


class Sched:
    ENGS = ("pe", "act", "dve", "pool", "sp")

    def __init__(self, nc):
        self.nc = nc
        self.streams = {e: [] for e in self.ENGS}
        self.dma_res = []

    def add(self, eng, fn, reads=(), writes=(), dma=None, extra_deps=()):
        op = Op(eng, fn, dma is not None)
        deps = {}
        for r in reads:
            lw = r.last_w
            if lw is not None:
                deps[id(lw)] = (lw, True)
        for w in writes:
            lw = w.last_w
            if lw is not None and id(lw) not in deps:
                deps[id(lw)] = (lw, False)
            for rd in w.readers:
                if id(rd) not in deps:
                    deps[id(rd)] = (rd, False)
        for d in extra_deps:
            deps[id(d)] = (d, True)
        for d, raw in deps.values():
            if d is op:
                continue
            if (not d.is_dma) and d.eng == eng:
                if eng == "pe" or eng == "sp":
                    continue
                if not (raw and SAME_ENGINE_RAW_SYNC):
                    continue
            op.deps.append(d)
            d.marked = True
        for r in reads:
            if not op.is_dma:
                r.readers = [x for x in r.readers if x.is_dma or x.eng != eng]
            r.readers.append(op)
        for w in writes:
            w.last_w = op
            w.readers = []
        if dma is not None:
            if dma.dsem is None:
                self.dma_res.append(dma)
                dma.dsem = True
            dma.dcnt += 16
            dma.last_dma = op
            op.dres = dma
            op.dval = dma.dcnt
            op.marked = True
        self.streams[eng].append(op)
        return op

    def emit(self):
        nc = self.nc
        with contextlib.ExitStack() as es:
            esem = {e: es.enter_context(nc.semaphore("sem_" + e)) for e in self.ENGS}
            for r in self.dma_res:
                r.dsem = es.enter_context(nc.semaphore("d_" + r.name))
            for e in self.ENGS:
                n = 0
                for op in self.streams[e]:
                    if op.marked and not op.is_dma:
                        n += 1
                        op.seq = n
            block = es.enter_context(nc.Block())
            streams = self.streams
            dma_res = self.dma_res

            def run(e, eng):
                waited = {}
                for op in streams[e]:
                    need = {}
                    for d in op.deps:
                        if d.is_dma:
                            key = ("d", id(d.dres))
                            sem, val = d.dres.dsem, d.dval
                        else:
                            key = ("e", d.eng)
                            sem, val = esem[d.eng], d.seq
                        if waited.get(key, 0) >= val:
                            continue
                        if key not in need or need[key][1] < val:
                            need[key] = (sem, val)
                    for key, (sem, val) in need.items():
                        eng.wait_ge(sem, val)
                        waited[key] = val
                    inst = op.fn(eng)
                    if op.is_dma:
                        inst.then_inc(op.dres.dsem, 16)
                    elif op.marked:
                        inst.then_inc(esem[e], 1)
                if e == "sp":
                    for r in dma_res:
                        if waited.get(("d", id(r)), 0) < r.dcnt:
                            eng.wait_ge(r.dsem, r.dcnt)

            @block.tensor
            def _(eng):
                run("pe", eng)

            @block.scalar
            def _(eng):
                run("act", eng)

            @block.vector
            def _(eng):
                run("dve", eng)

            @block.gpsimd
            def _(eng):
                run("pool", eng)

            @block.sync
            def _(eng):
                run("sp", eng)


def _gammas():
    return [1.0 - 2.0 ** (-5.0 - h) for h in range(H_A)]


def _rot_tables(pos, L):
    n = len(pos)
    inv = (1.0 / (10000.0 ** np.linspace(0.0, 1.0, 128, dtype=np.float32))).astype(np.float32)
    ang = (pos.astype(np.float32)[None, :] * inv[:, None]).astype(np.float32)
    cos = np.cos(ang).astype(np.float64)
    sin = np.sin(ang).astype(np.float64)
    out = np.zeros((H_A, 4, 128, n), np.float32)
    i = (np.arange(n) % L).astype(np.float64)
    for h, g in enumerate(_gammas()):
        fq = g ** (i + 1.0)
        fk = g ** (-(i + 1.0)) / 16.0
        out[h, 0] = cos * fq
        out[h, 1] = sin * fq
        out[h, 2] = cos * fk
        out[h, 3] = sin * fk
    return out


def _consts():
    c = {}
    c["ident_bf"] = np.eye(128, dtype=np.float32).astype(ml_dtypes.bfloat16)
    c["ident_f"] = np.eye(128, dtype=np.float32)
    m = np.arange(128)
    c["caus_ml"] = (m[:, None] <= m[None, :]).astype(np.float32)
    c["tri_qk"] = (m[None, :] <= m[:, None]).astype(np.float32)
    rot = np.zeros((NSEG + 1, H_A, 4, 128, TW), np.float32)
    for s in range(NSEG):
        rot[s] = _rot_tables(np.arange(s * TW, (s + 1) * TW), 128)
    rot[NSEG, :, :, :, :4] = _rot_tables(np.arange(16384, 16388), 4)
    c["rot"] = rot
    wk = np.zeros((2, 128, H_A), np.float32)
    for h, g in enumerate(_gammas()):
        wk[0, :, h] = g ** 128.0
        wk[1, :4, h] = g ** 4.0
    c["wk"] = wk
    return c


class Prog:
    def __init__(self, n_pool, do_sample=True, nseg=NSEG, debug=False, ns=2):
        self.ns = ns
        self.n_pool = n_pool
        self.cur_es = None
        self.do_sample = do_sample
        self.nseg = nseg
        self.nc = nc = bass.Bass("TRN2", target_bir_lowering=False)
        self.es = contextlib.ExitStack()
        self.S = Sched(nc)
        self.wctr = 0
        self.pctr = 0
        self.tctr = 0

        def din(name, shape, dt=F32):
            return nc.dram_tensor(name, list(shape), dt, kind="ExternalInput").ap()

        def dout(name, shape, dt=F32):
            return nc.dram_tensor(name, list(shape), dt, kind="ExternalOutput").ap()

        def dscr(name, shape, dt=F32):
            return nc.dram_tensor(name, list(shape), dt, kind="Internal").ap()

        self.xp = din("xp", [SEQ, D])
        self.cp = din("cp", [D])
        self.norm_g = din("norm_g", [4, D])
        self.w_mod = din("w_mod", [4, D, 3 * D])
        self.b_mod = din("b_mod", [4, 3 * D])
        self.w_in_a = din("w_in_a", [2, D, 12288])
        self.w_out_a = din("w_out_a", [2, 4096, D])
        self.w_q_b = din("w_q_b", [2, D, 2 * D])
        self.w_o_b = din("w_o_b", [2, D, D])
        self.kv_norm_g = din("kv_norm_g", [D])
        self.w_mod_kv = din("w_mod_kv", [D, 2 * D])
        self.b_mod_kv = din("b_mod_kv", [2 * D])
        self.w_kv = din("w_kv", [D, 2 * D])
        self.final_g = din("final_g", [D])
        self.w_mod_f = din("w_mod_f", [D, 2 * D])
        self.b_mod_f = din("b_mod_f", [2 * D])
        self.c_ident_bf = din("ident_bf", [128, 128], BF16)
        self.c_ident_f = din("ident_f", [128, 128])
        self.c_caus_ml = din("caus_ml", [128, 128])
        self.c_tri_qk = din("tri_qk", [128, 128])
        self.c_rot = din("rot", [NSEG + 1, H_A, 4, 128, TW])
        self.c_wk = din("wk", [2, 128, H_A])
        if do_sample:
            self.xs = din("xs", [ns, 4, D])
            self.cs = din("cs", [ns, D])
            self.st_in = din("st_in", [ns, 2, H_A, 256, 512])
            self.c_onehot = din("onehot", [4, 4, 128], BF16)
            self.c_blockdiag = din("blockdiag", [64, D], BF16)
            self.c_selq = din("selq", [64, 4], BF16)
            self.c_negmaskn = din("negmaskn", [4, 64])
            self.c_piota = din("piota", [128, 1])
            self.cache_k = din("cache_k", [n_pool * 128, D])
            self.cache_v = din("cache_v", [n_pool * 128, D])
            self.ptab = din("ptab", [ns, NPAGES], I32)

        self.y_p = dout("y_p", [SEQ, D])
        self.stp = dout("stp", [2, H_A, 256, 512])
        self.k_p = dout("k_p", [SEQ, D])
        self.v_p = dout("v_p", [SEQ, D])
        if do_sample:
            self.y_s = dout("y_s", [ns, 4, D])
            self.sts = dout("sts", [ns, 2, H_A, 256, 512])
            self.k_s = dout("k_s", [ns, 4, D])
            self.v_s = dout("v_s", [ns, 4, D])

        self.NMOD = 4 * 3 * D + 2 * 2 * D
        self.mod_scr = dscr("mod_scr", [3, self.NMOD])
        self.bar_scr = dscr("bar_scr", [128, 1])
        self.r_mod_scr = Res("mod_scr")
        self.st_scr = dscr("st_scr", [2, H_A, 256, 512])
        self.r_st_scr = [[Res(f"st_scr{l}_{h}") for h in range(H_A)] for l in range(2)]
        self.kt_scr = dscr("kt_scr", [H_B, 128, SEQ], BF16)
        self.v_scr = dscr("v_scr", [SEQ, D], BF16)
        self.r_kt_scr = Res("kt_scr")
        self.r_v_scr = Res("v_scr")
        self.r_out = Res("outputs")

    def sb(self, name, shape, dt):
        es = self.cur_es if self.cur_es is not None else self.es
        return es.enter_context(self.nc.sbuf_tensor("sb_" + name, list(shape), dt))

    def ps(self, name, shape, dt):
        return self.es.enter_context(self.nc.psum_tensor(name, list(shape), dt))

    def alloc_common(self):
        self.ident_bf = self.sb("ident_bf_t", [128, 128], BF16)
        self.ident_f = self.sb("ident_f_t", [128, 128], F32)
        self.caus_ml = self.sb("caus_ml_t", [128, 128], F32)
        self.tri_qk = self.sb("tri_qk_t", [128, 128], F32)
        self.wk_t = self.sb("wk_t", [128, 2, H_A], F32)
        self.r_const = Res("const")
        S = self.S
        self.bar_t = self.sb("bar_t", [128, 4], F32)
        self.r_bar = [Res(f"bar{i}") for i in range(4)]
        self.eps_t = self.sb("eps_t", [128, 1], F32)
        S.add("dve", lambda e: e.memset(self.eps_t[:], EPS), writes=[self.r_const])
        S.add("sp", lambda e: e.dma_start(out=self.ident_bf[:], in_=self.c_ident_bf), writes=[self.r_const], dma=self.r_const)
        S.add("sp", lambda e: e.dma_start(out=self.ident_f[:], in_=self.c_ident_f), writes=[self.r_const], dma=self.r_const)
        S.add("sp", lambda e: e.dma_start(out=self.caus_ml[:], in_=self.c_caus_ml), writes=[self.r_const], dma=self.r_const)
        S.add("sp", lambda e: e.dma_start(out=self.tri_qk[:], in_=self.c_tri_qk), writes=[self.r_const], dma=self.r_const)
        S.add("sp", lambda e: e.dma_start(out=self.wk_t[:], in_=self.c_wk.rearrange("a p h -> p a h")), writes=[self.r_const], dma=self.r_const)
        self.NSLOT = 6
        self.wslot = [self.sb(f"wslot{i}", [128, 2048], BF16) for i in range(self.NSLOT)]
        self.r_w = [Res(f"wslot{i}") for i in range(self.NSLOT)]
        self.NPS = 6
        self.psf = [self.ps(f"psf{i}", [128, 512], F32) for i in range(self.NPS)]
        self.r_psf = [Res(f"psf{i}") for i in range(self.NPS)]
        self.psb = [self.ps(f"psb{i}", [128, 1024], BF16) for i in range(2)]
        self.r_psb = [Res(f"psb{i}") for i in range(2)]

    def barrier(self):
        S = self.S
        lasts = [S.streams[e][-1] for e in S.ENGS if S.streams[e]]
        dmas = [r.last_dma for r in S.dma_res if r.last_dma is not None]
        deps = lasts + dmas
        bt = self.bar_t
        S.add("dve", lambda e: e.memset(bt[:, 0:1], 0.0), writes=[self.r_bar[0]], extra_deps=deps)
        S.add("act", lambda e: e.activation(out=bt[:, 1:2], in_=self.eps_t[:, 0:1], func=AF.Copy), writes=[self.r_bar[1]], extra_deps=deps)
        S.add("pool", lambda e: e.memset(bt[:, 2:3], 0.0), writes=[self.r_bar[2]], extra_deps=deps)
        S.add("pe", lambda e: e.matmul(self.psf[0][0:1, 0:1], lhsT=self.ident_bf[0:1, 0:1], rhs=self.ident_bf[0:1, 0:1], start=True, stop=True),
              writes=[self.r_psf[0]], extra_deps=deps)
        S.add("sp", lambda e: e.dma_start(out=self.bar_scr, in_=self.eps_t[:, 0:1]), writes=[self.r_bar[3]], dma=self.r_bar[3], extra_deps=deps)

    def next_slot(self):
        i = self.wctr % self.NSLOT
        self.wctr += 1
        return i

    def next_ps(self):
        i = self.pctr % self.NPS
        self.pctr += 1
        return i

    def next_psb(self):
        i = self.tctr % 2
        self.tctr += 1
        return i

    def wload_T(self, W, c0):
        i = self.next_slot()
        src = W.rearrange("(k p) n -> p k n", p=128)[:, :, c0:c0 + 128]
        dst = self.wslot[i][:].rearrange("p (k n) -> p k n", n=128)
        self.S.add("pool", lambda e: e.dma_start(out=dst, in_=src), writes=[self.r_w[i]], dma=self.r_w[i])
        return i

    def wload_R(self, W, r0, c0):
        i = self.next_slot()
        src = W[r0:r0 + 512, c0:c0 + 512].rearrange("(j p) n -> p j n", p=128)
        dst = self.wslot[i][:].rearrange("p (j n) -> p j n", n=512)
        self.S.add("pool", lambda e: e.dma_start(out=dst, in_=src), writes=[self.r_w[i]], dma=self.r_w[i])
        return i

    def slotT(self, i):
        return self.wslot[i][:].rearrange("p (k n) -> p k n", n=128)

    def slotR(self, i):
        return self.wslot[i][:].rearrange("p (j n) -> p j n", n=512)

    def prepass_mods(self):
        S = self.S
        with contextlib.ExitStack() as es:
            nc = self.nc
            ccol = es.enter_context(nc.sbuf_tensor("ccol", [128, 3, 16], F32))
            cT = es.enter_context(nc.sbuf_tensor("cT", [128, 16, 128], BF16))
            r_ccol, r_cT = Res("ccol"), Res("cT")
            S.add("sp", lambda e: e.dma_start(out=ccol[:, 0, :], in_=self.cp.rearrange("(k p) -> p k", p=128),
                                              allow_slow_non_contiguous=True),
                  writes=[r_ccol], dma=r_ccol)
            S.add("dve", lambda e: e.memset(ccol[:, 1:3, :], 0.0), writes=[r_ccol])
            if self.do_sample:
                for si in range(self.ns):
                    S.add("sp", lambda e, si=si: e.dma_start(out=ccol[:, 1 + si, :], in_=self.cs[si].rearrange("(k p) -> p k", p=128),
                                                             allow_slow_non_contiguous=True),
                          writes=[r_ccol], dma=r_ccol)
            for g in range(4):
                S.add("dve", lambda e, g=g: e.tensor_copy(out=cT[:, :, g * 32:(g + 1) * 32],
                                                          in_=ccol[:, min(g, 2), :].unsqueeze(2).to_broadcast([128, 16, 32])),
                      reads=[r_ccol], writes=[r_cT])
            mats = [(self.w_mod[l], self.b_mod[l], 3 * D) for l in range(4)] + \
                   [(self.w_mod_kv, self.b_mod_kv, 2 * D), (self.w_mod_f, self.b_mod_f, 2 * D)]
            GW = 1024
            bwide = [es.enter_context(nc.sbuf_tensor(f"bwide{i}", [128, GW], F32)) for i in range(2)]
            mwide = [es.enter_context(nc.sbuf_tensor(f"mwide{i}", [128, GW], F32)) for i in range(2)]
            r_bw = [Res("bwide0"), Res("bwide1")]
            r_mw = [Res("mwide0"), Res("mwide1")]
            off = 0
            it = 0
            for W, b, n in mats:
                for g0 in range(0, n, GW):
                    gi = it % 2
                    it += 1
                    S.add("sp", lambda e, b=b, g0=g0, gi=gi: e.dma_start(out=bwide[gi][:], in_=b[g0:g0 + GW].partition_broadcast(128)),
                          writes=[r_bw[gi]], dma=r_bw[gi])
                    for c0 in range(g0, g0 + GW, 128):
                        si = self.wload_T(W, c0)
                        pi = self.next_ps()
                        for k in range(16):
                            S.add("pe", lambda e, k=k, si=si, pi=pi: e.matmul(self.psf[pi][:, 0:128], lhsT=cT[:, k, :], rhs=self.slotT(si)[:, k, :],
                                                                                start=(k == 0), stop=(k == 15)),
                                  reads=[r_cT, self.r_w[si]], writes=[self.r_psf[pi]])
                        cc = c0 - g0
                        S.add("dve", lambda e, pi=pi, gi=gi, cc=cc: e.tensor_tensor(out=mwide[gi][:, cc:cc + 128], in0=self.psf[pi][:, 0:128],
                                                                                    in1=bwide[gi][:, cc:cc + 128], op=ALU.add),
                              reads=[self.r_psf[pi], r_bw[gi]], writes=[r_mw[gi]])
                    o = off + g0
                    for ri, prow in enumerate((0, 32, 64)):
                        S.add("sp", lambda e, gi=gi, o=o, ri=ri, prow=prow: e.dma_start(out=self.mod_scr[ri:ri + 1, o:o + GW], in_=mwide[gi][prow:prow + 1, :]),
                              reads=[r_mw[gi]], writes=[self.r_mod_scr], dma=r_mw[gi])
                off += n

    def alloc_pass(self, tag, TT, NT):
        B = type("B", (), {})()
        B.TT, B.NT = TT, NT
        B.x = self.sb(f"x_{tag}", [128, TT, D], F32)
        B.r_x = [Res(f"x_{tag}{t}") for t in range(TT)]
        B.hT = self.sb(f"hT_{tag}", [128, 16, NT], BF16)
        B.r_hT = Res(f"hT_{tag}")
        B.a_rep = self.sb(f"a_rep_{tag}", [128, D], F32)
        B.sh_rep = self.sb(f"sh_rep_{tag}", [128, D], F32)
        B.gate_rep = self.sb(f"gate_rep_{tag}", [128, D], F32)
        B.r_a, B.r_sh, B.r_gate = Res("a_rep" + tag), Res("sh_rep" + tag), Res("gate_rep" + tag)
        B.ntmp = self.sb(f"ntmp_{tag}", [128, D], F32)
        B.r_ntmp = Res("ntmp" + tag)
        B.hb = self.sb(f"hb_{tag}", [128, D], BF16)
        B.r_hb = Res("hb" + tag)
        B.ss = self.sb(f"ss_{tag}", [128, 8], F32)
        B.r_ss = Res("ss" + tag)
        B.qT = self.sb(f"qT_{tag}", [128, 2, NT], BF16)
        B.kT = self.sb(f"kT_{tag}", [128, 2, NT], BF16)
        B.r_qT, B.r_kT = Res("qT" + tag), Res("kT" + tag)
        B.kw = self.sb(f"kw_{tag}", [128, TT, 256], BF16)
        B.r_kw = Res("kw" + tag)
        B.vT = self.sb(f"vT_{tag}", [128, 4, NT], BF16)
        B.r_vT = Res("vT" + tag)
        B.v = self.sb(f"v_{tag}", [128, TT, 512], BF16)
        B.r_v = Res("v" + tag)
        B.sgT = self.sb(f"sgT_{tag}", [128, 16, NT], BF16)
        B.r_sgT = Res("sgT" + tag)
        B.gT = self.sb(f"gT_{tag}", [128, 16, NT], BF16)
        B.r_gT = Res("gT" + tag)
        B.St = self.sb(f"St_{tag}", [128, 2, 512], F32)
        B.Sb = self.sb(f"Sb_{tag}", [128, 2, 512], BF16)
        B.r_St, B.r_Sb = Res("St" + tag), Res("Sb" + tag)
        B.rot = self.sb(f"rot_{tag}", [128, 4, NT], F32)
        B.r_rot = Res("rot" + tag)
        B.rt = self.sb(f"rt_{tag}", [128, 2, NT], F32)
        B.r_rt = Res("rt" + tag)
        B.AT = self.sb(f"AT_{tag}", [128, 128], BF16)
        B.r_AT = Res("AT" + tag)
        B.onb = self.sb(f"onb_{tag}", [128, 512], BF16)
        B.r_onb = Res("onb" + tag)
        return B

    def load_rep(self, B, dst, r_dst, row, off, n=D):
        self.S.add("sp", lambda e: e.dma_start(out=dst[:, 0:n], in_=self.mod_scr[row, off:off + n].partition_broadcast(128)),
                   reads=[self.r_mod_scr], writes=[r_dst], dma=r_dst)

    def mod_norm(self, B, row, g_dram, off_shift, off_scale, rows_last):
        S = self.S
        TT = B.TT
        self.load_rep(B, B.sh_rep, B.r_sh, row, off_shift)
        self.load_rep(B, B.a_rep, B.r_a, row, off_scale)
        S.add("sp", lambda e: e.dma_start(out=B.ntmp[:], in_=g_dram.partition_broadcast(128)), writes=[B.r_ntmp], dma=B.r_ntmp)
        S.add("dve", lambda e: e.scalar_tensor_tensor(out=B.a_rep[:], in0=B.a_rep[:], scalar=1.0, in1=B.ntmp[:], op0=ALU.add, op1=ALU.mult),
              reads=[B.r_a, B.r_ntmp], writes=[B.r_a])
        for t in range(TT):
            R = rows_last if t == TT - 1 else 128
            S.add("dve", lambda e, R=R: e.memset(B.ss[:R, 0:1], 0.0), writes=[B.r_ss])
            S.add("act", lambda e, t=t, R=R: e.activation(out=B.ntmp[:R, :], in_=B.x[:R, t, :], func=AF.Square, accum_out=B.ss[:R, 0:1]),
                  reads=[B.r_x[t], B.r_ss], writes=[B.r_ntmp, B.r_ss])
            S.add("act", lambda e, R=R: e.activation(out=B.ss[:R, 1:2], in_=B.ss[:R, 0:1], func=AF.Sqrt, scale=1.0 / D, bias=self.eps_t[:R, 0:1]),
                  reads=[B.r_ss, self.r_const], writes=[B.r_ss])
            S.add("dve", lambda e, R=R: e.reciprocal(out=B.ss[:R, 2:3], in_=B.ss[:R, 1:2]),
                  reads=[B.r_ss], writes=[B.r_ss])
            S.add("dve", lambda e, t=t, R=R: e.scalar_tensor_tensor(out=B.ntmp[:R, :], in0=B.x[:R, t, :], scalar=B.ss[:R, 2:3], in1=B.a_rep[:R, :],
                                                                     op0=ALU.mult, op1=ALU.mult),
                  reads=[B.r_x[t], B.r_ss, B.r_a], writes=[B.r_ntmp])
            S.add("dve", lambda e, R=R: e.tensor_tensor(out=B.hb[:R, :], in0=B.ntmp[:R, :], in1=B.sh_rep[:R, :], op=ALU.add),
                  reads=[B.r_ntmp, B.r_sh], writes=[B.r_hb])
            for g in range(2):
                pb = self.next_psb()
                for j in range(8):
                    k = g * 8 + j
                    S.add("pe", lambda e, k=k, j=j, pb=pb, R=R: e.transpose(out=self.psb[pb][:, j * 128:j * 128 + R], in_=B.hb[:R, k * 128:(k + 1) * 128],
                                                                            identity=self.ident_bf[:R, :R]),
                          reads=[B.r_hb, self.r_const], writes=[self.r_psb[pb]])
                S.add("act", lambda e, g=g, pb=pb, t=t, R=R: e.activation(
                    out=B.hT[:, g * 8:(g + 1) * 8, t * 128:t * 128 + R],
                    in_=self.psb[pb][:].rearrange("p (j n) -> p j n", n=128)[:, :, 0:R], func=AF.Copy),
                    reads=[self.r_psb[pb]], writes=[B.r_hT])

    def proj_T(self, B, W, c0, evac):
        S = self.S
        si = self.wload_T(W, c0)
        pi = self.next_ps()
        for k in range(16):
            S.add("pe", lambda e, k=k, si=si, pi=pi: e.matmul(self.psf[pi][:, 0:B.NT], lhsT=self.slotT(si)[:, k, :], rhs=B.hT[:, k, :],
                                                                start=(k == 0), stop=(k == 15)),
                  reads=[self.r_w[si], B.r_hT], writes=[self.r_psf[pi]])
        evac(pi)

    def residual_add(self, B, t, R, cb, pi):
        S = self.S
        S.add("dve", lambda e: e.tensor_tensor(out=B.ntmp[:R, cb * 512:(cb + 1) * 512], in0=self.psf[pi][:R, :], in1=B.gate_rep[:R, cb * 512:(cb + 1) * 512],
                                               op=ALU.mult),
              reads=[self.r_psf[pi], B.r_gate], writes=[B.r_ntmp])
        S.add("dve", lambda e: e.tensor_tensor(out=B.x[:R, t, cb * 512:(cb + 1) * 512], in0=B.x[:R, t, cb * 512:(cb + 1) * 512],
                                               in1=B.ntmp[:R, cb * 512:(cb + 1) * 512], op=ALU.add),
              reads=[B.r_ntmp, B.r_x[t]], writes=[B.r_x[t]])

    def retention_layer(self, B, l, pidx, row, rows_last, first, last, sample, st_in=None, sts=None):
        S = self.S
        TT, NT = B.TT, B.NT
        W = self.w_in_a[l]
        moff = l * 3 * D
        self.mod_norm(B, row, self.norm_g[l], moff, moff + D, rows_last)
        self.load_rep(B, B.gate_rep, B.r_gate, row, moff + 2 * D)
        gam = _gammas()
        L = rows_last if TT == 1 else 128
        for h in range(H_A):
            S.add("sp", lambda e, h=h: e.dma_start(out=B.rot[:], in_=self.c_rot[pidx, h, :, :, 0:NT].rearrange("a p n -> p a n")),
                  writes=[B.r_rot], dma=B.r_rot)
            for which, dstT, r_dst in ((0, B.qT, B.r_qT), (1, B.kT, B.r_kT)):
                pis = []
                for c in range(2):
                    self.proj_T(B, W, which * 2048 + h * 256 + c * 128, lambda pi: pis.append(pi))
                p1, p2 = pis
                cs_, sn_ = 2 * which, 2 * which + 1
                S.add("dve", lambda e, p1=p1, cs_=cs_: e.tensor_tensor(out=B.rt[:, 0, :], in0=self.psf[p1][:, 0:NT], in1=B.rot[:, cs_, :], op=ALU.mult),
                      reads=[self.r_psf[p1], B.r_rot], writes=[B.r_rt])
                S.add("dve", lambda e, p2=p2, sn_=sn_: e.tensor_tensor(out=B.rt[:, 1, :], in0=self.psf[p2][:, 0:NT], in1=B.rot[:, sn_, :], op=ALU.mult),
                      reads=[self.r_psf[p2], B.r_rot], writes=[B.r_rt])
                S.add("dve", lambda e, dstT=dstT: e.tensor_tensor(out=dstT[:, 0, :], in0=B.rt[:, 0, :], in1=B.rt[:, 1, :], op=ALU.subtract),
                      reads=[B.r_rt], writes=[r_dst])
                S.add("dve", lambda e, p1=p1, sn_=sn_: e.tensor_tensor(out=B.rt[:, 0, :], in0=self.psf[p1][:, 0:NT], in1=B.rot[:, sn_, :], op=ALU.mult),
                      reads=[self.r_psf[p1], B.r_rot, r_dst], writes=[B.r_rt])
                S.add("dve", lambda e, p2=p2, cs_=cs_: e.tensor_tensor(out=B.rt[:, 1, :], in0=self.psf[p2][:, 0:NT], in1=B.rot[:, cs_, :], op=ALU.mult),
                      reads=[self.r_psf[p2], B.r_rot], writes=[B.r_rt])
                S.add("dve", lambda e, dstT=dstT: e.tensor_tensor(out=dstT[:, 1, :], in0=B.rt[:, 0, :], in1=B.rt[:, 1, :], op=ALU.add),
                      reads=[B.r_rt], writes=[r_dst])
            for t in range(TT):
                R = rows_last if t == TT - 1 else 128
                pb = self.next_psb()
                for c in range(2):
                    S.add("pe", lambda e, c=c, pb=pb, t=t, R=R: e.transpose(out=self.psb[pb][:R, c * 128:(c + 1) * 128], in_=B.kT[:, c, t * 128:t * 128 + R],
                                                                            identity=self.ident_bf[:]),
                          reads=[B.r_kT, self.r_const], writes=[self.r_psb[pb]])
                S.add("act", lambda e, pb=pb, t=t, R=R, h=h: e.activation(out=B.kw[:R, t, :], in_=self.psb[pb][:R, 0:256], func=AF.Copy,
                                                                          scale=self.wk_t[:R, 1 if sample else 0, h:h + 1]),
                      reads=[self.r_psb[pb], self.r_const], writes=[B.r_kw])
            for j in range(4):
                self.proj_T(B, W, 4096 + h * 512 + j * 128,
                            lambda pi, j=j: S.add("act", lambda e: e.activation(out=B.vT[:, j, :], in_=self.psf[pi][:, 0:NT], func=AF.Copy),
                                                  reads=[self.r_psf[pi]], writes=[B.r_vT]))
            for t in range(TT):
                R = rows_last if t == TT - 1 else 128
                pb = self.next_psb()
                for j in range(4):
                    S.add("pe", lambda e, j=j, pb=pb, t=t, R=R: e.transpose(out=self.psb[pb][:R, j * 128:(j + 1) * 128], in_=B.vT[:, j, t * 128:t * 128 + R],
                                                                            identity=self.ident_bf[:]),
                          reads=[B.r_vT, self.r_const], writes=[self.r_psb[pb]])
                S.add("act", lambda e, pb=pb, t=t, R=R: e.activation(out=B.v[:R, t, :], in_=self.psb[pb][:R, 0:512], func=AF.Copy),
                      reads=[self.r_psb[pb]], writes=[B.r_v])
            for j in range(4):
                self.proj_T(B, W, 8192 + h * 512 + j * 128,
                            lambda pi, j=j: S.add("act", lambda e: e.activation(out=B.sgT[:, j, :], in_=self.psf[pi][:, 0:NT], func=AF.Silu),
                                                  reads=[self.r_psf[pi]], writes=[B.r_sgT]))
            r_scr = self.r_st_scr[l][h]
            if sample:
                S.add("sp", lambda e, h=h: e.dma_start(out=B.St[:], in_=st_in[l, h].rearrange("(c p) e -> p c e", p=128)),
                      writes=[B.r_St], dma=B.r_St)
            elif first:
                S.add("dve", lambda e: e.memset(B.St[:], 0.0), writes=[B.r_St])
            else:
                S.add("sp", lambda e, h=h: e.dma_start(out=B.St[:], in_=self.st_scr[l, h].rearrange("(c p) e -> p c e", p=128)),
                      reads=[r_scr], writes=[B.r_St], dma=B.r_St)
            S.add("act", lambda e: e.activation(out=B.Sb[:], in_=B.St[:], func=AF.Copy), reads=[B.r_St], writes=[B.r_Sb])
            gL = gam[h] ** L
            for t in range(TT):
                R = rows_last if t == TT - 1 else 128
                ts = slice(t * 128, t * 128 + R)
                pa = self.next_ps()
                for c in range(2):
                    S.add("pe", lambda e, c=c, pa=pa, ts=ts, R=R: e.matmul(self.psf[pa][:R, 0:R], lhsT=B.kT[:, c, ts], rhs=B.qT[:, c, ts], start=(c == 0), stop=(c == 1)),
                          reads=[B.r_kT, B.r_qT], writes=[self.r_psf[pa]])
                S.add("dve", lambda e, pa=pa, R=R: e.tensor_tensor(out=B.AT[:R, :R], in0=self.psf[pa][:R, 0:R], in1=self.caus_ml[:R, :R], op=ALU.mult),
                      reads=[self.r_psf[pa], self.r_const], writes=[B.r_AT])
                po = self.next_ps()
                for c in range(2):
                    S.add("pe", lambda e, c=c, po=po, ts=ts, R=R: e.matmul(self.psf[po][:R, :], lhsT=B.qT[:, c, ts], rhs=B.Sb[:, c, :], start=(c == 0), stop=False),
                          reads=[B.r_qT, B.r_Sb], writes=[self.r_psf[po]])
                S.add("pe", lambda e, po=po, t=t, R=R: e.matmul(self.psf[po][:R, :], lhsT=B.AT[:R, :R], rhs=B.v[:R, t, :], start=False, stop=True),
                      reads=[B.r_AT, B.r_v], writes=[self.r_psf[po]])
                for c in range(2):
                    pss = self.next_ps()
                    S.add("pe", lambda e, c=c, pss=pss, t=t, R=R: e.matmul(self.psf[pss][:, :], lhsT=B.kw[:R, t, c * 128:(c + 1) * 128], rhs=B.v[:R, t, :], start=True, stop=True),
                          reads=[B.r_kw, B.r_v], writes=[self.r_psf[pss]])
                    S.add("dve", lambda e, c=c, pss=pss, gL=gL: e.scalar_tensor_tensor(out=B.St[:, c, :], in0=B.St[:, c, :], scalar=float(gL), in1=self.psf[pss][:, :],
                                                                               op0=ALU.mult, op1=ALU.add),
                          reads=[B.r_St, self.r_psf[pss]], writes=[B.r_St])
                if t < TT - 1:
                    S.add("act", lambda e: e.activation(out=B.Sb[:], in_=B.St[:], func=AF.Copy), reads=[B.r_St], writes=[B.r_Sb])
                S.add("dve", lambda e, R=R: e.memset(B.ss[:R, 4:5], 0.0), writes=[B.r_ss])
                S.add("act", lambda e, po=po, R=R: e.activation(out=B.ntmp[:R, 0:512], in_=self.psf[po][:R, :], func=AF.Square, accum_out=B.ss[:R, 4:5]),
                      reads=[self.r_psf[po], B.r_ss], writes=[B.r_ntmp, B.r_ss])
                S.add("act", lambda e, R=R: e.activation(out=B.ss[:R, 5:6], in_=B.ss[:R, 4:5], func=AF.Sqrt, scale=1.0 / 512, bias=self.eps_t[:R, 0:1]),
                      reads=[B.r_ss, self.r_const], writes=[B.r_ss])
                S.add("dve", lambda e, R=R: e.reciprocal(out=B.ss[:R, 6:7], in_=B.ss[:R, 5:6]),
                      reads=[B.r_ss], writes=[B.r_ss])
                S.add("dve", lambda e, po=po, R=R: e.tensor_scalar(out=B.onb[:R, :], in0=self.psf[po][:R, :], scalar1=B.ss[:R, 6:7], scalar2=None, op0=ALU.mult),
                      reads=[self.r_psf[po], B.r_ss], writes=[B.r_onb])
                pb = self.next_psb()
                for j in range(4):
                    S.add("pe", lambda e, j=j, pb=pb, R=R: e.transpose(out=self.psb[pb][:, j * 128:j * 128 + R], in_=B.onb[:R, j * 128:(j + 1) * 128],
                                                                       identity=self.ident_bf[:R, :R]),
                          reads=[B.r_onb, self.r_const], writes=[self.r_psb[pb]])
                S.add("dve", lambda e, pb=pb, ts=ts, R=R: e.tensor_tensor(out=B.gT[:, 0:4, ts],
                                                                          in0=self.psb[pb][:, 0:512].rearrange("p (j n) -> p j n", n=128)[:, :, 0:R],
                                                                          in1=B.sgT[:, 0:4, ts], op=ALU.mult),
                      reads=[self.r_psb[pb], B.r_sgT], writes=[B.r_gT])
            if sample:
                S.add("sp", lambda e, h=h: e.dma_start(out=sts[l, h].rearrange("(c p) e -> p c e", p=128), in_=B.St[:]),
                      reads=[B.r_St], dma=B.r_St)
            elif last:
                S.add("sp", lambda e, h=h: e.dma_start(out=self.stp[l, h].rearrange("(c p) e -> p c e", p=128), in_=B.St[:]),
                      reads=[B.r_St], dma=B.r_St)
            else:
                S.add("sp", lambda e, h=h: e.dma_start(out=self.st_scr[l, h].rearrange("(c p) e -> p c e", p=128), in_=B.St[:]),
                      reads=[B.r_St], writes=[r_scr], dma=B.r_St)
            for cb in range(4):
                si = self.wload_R(self.w_out_a[l], h * 512, cb * 512)
                for t in range(TT):
                    R = rows_last if t == TT - 1 else 128
                    pi = self.next_ps()
                    for j in range(4):
                        S.add("pe", lambda e, j=j, si=si, pi=pi, t=t, R=R: e.matmul(self.psf[pi][:R, :], lhsT=B.gT[:, j, t * 128:t * 128 + R], rhs=self.slotR(si)[:, j, :],
                                                                                   start=(j == 0), stop=(j == 3)),
                              reads=[B.r_gT, self.r_w[si]], writes=[self.r_psf[pi]])
                    self.residual_add(B, t, R, cb, pi)

    def load_x(self, B, src, rows_last):
        for t in range(B.TT):
            R = rows_last if t == B.TT - 1 else 128
            self.S.add("sp", lambda e, t=t, R=R: e.dma_start(out=B.x[:R, t, :], in_=src[t * 128:t * 128 + R, :]),
                       writes=[B.r_x[t]], dma=B.r_x[t])

    def dump_x(self, B, dst, rows_last):
        for t in range(B.TT):
            R = rows_last if t == B.TT - 1 else 128
            self.S.add("sp", lambda e, t=t, R=R: e.dma_start(out=dst[t * 128:t * 128 + R, :], in_=B.x[:R, t, :]),
                       reads=[B.r_x[t]], dma=B.r_x[t])


def build_debug(stage):
    P = Prog(0, do_sample=False)
    P.dbg = P.nc.dram_tensor("dbg", [TW, D], F32, kind="ExternalOutput").ap()
    with P.es:
        P.alloc_common()
        P.prepass_mods()
        B = P.alloc_pass("p", 4, TW)
        P.load_x(B, P.xp[0:TW, :], 128)
        if stage >= 1:
            P.retention_layer(B, 0, 0, 0, 128, True, False, False)
        P.dump_x(B, P.dbg, 128)
        P.S.emit()
    return P


def _alloc_attn(P, tag, sample=False):
    A = type("A", (), {})()
    nb_ = 1 if sample else 2
    A.nb = nb_
    A.kvf = [P.sb(f"kvf{i}{tag}", [128, 512], F32) for i in range(nb_)]
    A.r_kvf = [Res(f"kvf{i}{tag}") for i in range(nb_)]
    A.kvb = [P.sb(f"kvb{i}{tag}", [128, 512], BF16) for i in range(nb_)]
    A.r_kvb = [Res(f"kvb{i}{tag}") for i in range(nb_)]
    if not sample:
        A.ktt = [P.sb(f"ktt{i}{tag}", [128, 4, 128], BF16) for i in range(2)]
        A.r_ktt = [Res(f"ktt{i}{tag}") for i in range(2)]
    A.msum = P.sb(f"msum{tag}", [128, H_B, 4], F32)
    A.r_msum = Res("msum" + tag)
    A.ctr = 0
    if not sample:
        A.KTh = P.sb("KTh", [128, SEQ], BF16)
        A.Vh = P.sb("Vh", [128, 16, 128], BF16)
        A.r_KTh, A.r_Vh = Res("KTh"), Res("Vh")
        A.Pms = [P.sb(f"Pm{i}", [128, SEQ], BF16) for i in range(2)]
        A.r_Pms = [Res(f"Pm{i}") for i in range(2)]
        A.PT = P.sb("PT", [128, 16, 128], BF16)
        A.r_PT = Res("PT")
        A.Powns = [P.sb(f"Pown{i}", [128, 128], F32) for i in range(2)]
        A.r_Powns = [Res(f"Pown{i}") for i in range(2)]
        A.sms = [P.sb(f"sm{i}", [128, 64], F32) for i in range(2)]
        A.r_sms = [Res(f"sm{i}") for i in range(2)]
    return A


def kv_proj(P, B, A, pidx, row, rows_last, sample, samp_i=0):
    S = P.S
    TT = B.TT
    off = 4 * 3 * D
    P.mod_norm(B, row, P.kv_norm_g, off, off + D, rows_last)
    import os
    kvmode = os.environ.get("KVMODE", "kv")
    for cb in range(8):
        isK = cb < 4
        if (isK and "k" not in kvmode) or ((not isK) and "v" not in kvmode):
            continue
        slots = [P.wload_R(P.w_kv, kk * 512, cb * 512) for kk in range(4)]
        for t in range(TT):
            R = rows_last if t == TT - 1 else 128
            pi = P.next_ps()
            for kk in range(4):
                for j in range(4):
                    S.add("pe", lambda e, kk=kk, j=j, pi=pi, t=t, R=R, si=slots[kk]: e.matmul(
                        P.psf[pi][:R, :], lhsT=B.hT[:, kk * 4 + j, t * 128:t * 128 + R], rhs=P.slotR(si)[:, j, :],
                        start=(kk == 0 and j == 0), stop=(kk == 3 and j == 3)),
                        reads=[B.r_hT, P.r_w[slots[kk]]], writes=[P.r_psf[pi]])
            bi = A.ctr % A.nb
            A.ctr += 1
            kvstep = int(os.environ.get("KVSTEP", "9"))
            if kvstep < 1:
                continue
            S.add("act", lambda e, pi=pi, bi=bi, R=R: e.activation(out=A.kvf[bi][:R, :], in_=P.psf[pi][:R, :], func=AF.Copy),
                  reads=[P.r_psf[pi]], writes=[A.r_kvf[bi]])
            if kvstep < 2:
                continue
            S.add("dve", lambda e, pi=pi, bi=bi, R=R: e.tensor_copy(out=A.kvb[bi][:R, :], in_=A.kvf[bi][:R, :]),
                  reads=[A.r_kvf[bi]], writes=[A.r_kvb[bi]])
            if kvstep < 3:
                continue
            cc = (cb % 4) * 512
            if sample:
                dst = (P.k_s if isK else P.v_s)[samp_i, 0:R, cc:cc + 512]
            else:
                r0 = pidx * TW + t * 128
                dst = (P.k_p if isK else P.v_p)[r0:r0 + R, cc:cc + 512]
            if os.environ.get("NOKVOUT") != "1":
                S.add("sp", lambda e, dst=dst, bi=bi, R=R: e.dma_start(out=dst, in_=A.kvf[bi][:R, :]),
                      reads=[A.r_kvf[bi]], dma=A.r_kvf[bi])
            if sample:
                if isK:
                    pb = P.next_psb()
                    for hh in range(4):
                        S.add("pe", lambda e, hh=hh, pb=pb, bi=bi, R=R: e.transpose(out=P.psb[pb][:, hh * 128:hh * 128 + R], in_=A.kvb[bi][:R, hh * 128:(hh + 1) * 128],
                                                                                  identity=P.ident_bf[:R, :R]),
                              reads=[A.r_kvb[bi], P.r_const], writes=[P.r_psb[pb]])
                    S.add("act", lambda e, pb=pb, cb=cb, R=R: e.activation(out=A.KTn[:, cb * 4:(cb + 1) * 4, 0:R],
                                                                           in_=P.psb[pb][:, 0:512].rearrange("p (h n) -> p h n", n=128)[:, :, 0:R], func=AF.Copy),
                          reads=[P.r_psb[pb]], writes=[A.r_KTn])
                else:
                    S.add("dve", lambda e, bi=bi, cc=cc, R=R: e.tensor_copy(out=A.Vn[:R, cc:cc + 512], in_=A.kvb[bi][:R, :]),
                          reads=[A.r_kvb[bi]], writes=[A.r_Vn])
                continue
            r0 = pidx * TW + t * 128
            if not isK:
                if os.environ.get("NOVSCR") != "1":
                    S.add("sp", lambda e, bi=bi, r0=r0, cc=cc: e.dma_start(out=P.v_scr[r0:r0 + 128, cc:cc + 512], in_=A.kvb[bi][:, :]),
                          reads=[A.r_kvb[bi]], writes=[P.r_v_scr], dma=A.r_kvb[bi])
            else:
                pb = P.next_psb()
                for hh in range(4):
                    S.add("pe", lambda e, hh=hh, pb=pb, bi=bi: e.transpose(out=P.psb[pb][:, hh * 128:(hh + 1) * 128], in_=A.kvb[bi][:, hh * 128:(hh + 1) * 128],
                                                                        identity=P.ident_bf[:]),
                          reads=[A.r_kvb[bi], P.r_const], writes=[P.r_psb[pb]])
                S.add("act", lambda e, pb=pb, bi=bi: e.activation(out=A.ktt[bi][:], in_=P.psb[pb][:, 0:512].rearrange("p (h n) -> p h n", n=128), func=AF.Copy),
                      reads=[P.r_psb[pb]], writes=[A.r_ktt[bi]])
                S.add("dve", lambda e, bi=bi, cb=cb, t=t: e.tensor_reduce(out=A.msum[:, cb * 4:(cb + 1) * 4, t], in_=A.ktt[bi][:], axis=AX.X, op=ALU.add),
                      reads=[A.r_ktt[bi]], writes=[A.r_msum])
                S.add("sp", lambda e, bi=bi, cb=cb, r0=r0: e.dma_start(out=P.kt_scr[cb * 4:(cb + 1) * 4, :, r0:r0 + 128].rearrange("h p n -> p h n"), in_=A.ktt[bi][:]),
                      reads=[A.r_ktt[bi]], writes=[P.r_kt_scr], dma=A.r_ktt[bi])
    if not sample:
        for bb in range(2):
            blk = pidx * 2 + bb
            S.add("dve", lambda e, bb=bb, blk=blk: e.tensor_tensor(out=P.meansT[:, :, blk], in0=A.msum[:, :, 2 * bb], in1=A.msum[:, :, 2 * bb + 1], op=ALU.add),
                  reads=[A.r_msum], writes=[P.r_means])
            S.add("dve", lambda e, blk=blk: e.tensor_scalar(out=P.meansT_bf[:, :, blk], in0=P.meansT[:, :, blk], scalar1=1.0 / 256, scalar2=None, op0=ALU.mult),
                  reads=[P.r_means], writes=[P.r_means_bf])


def moba_layer_prompt(P, B, A, l, pidx, row):
    S = P.S
    TT, NT = B.TT, B.NT
    li = l - 2
    W = P.w_q_b[li]
    moff = l * 3 * D
    P.mod_norm(B, row, P.norm_g[l], moff, moff + D, 128)
    P.load_rep(B, B.gate_rep, B.r_gate, row, moff + 2 * D)
    scale = 128 ** -0.5
    for j in range(16):
        P.proj_T(B, W, 2048 + j * 128,
                 lambda pi, j=j: S.add("act", lambda e: e.activation(out=B.sgT[:, j, :], in_=P.psf[pi][:, 0:NT], func=AF.Silu),
                                       reads=[P.r_psf[pi]], writes=[B.r_sgT]))
    nkmax = (pidx + 1) * TW
    for h in range(H_B):
        P.proj_T(B, W, h * 128,
                 lambda pi: S.add("act", lambda e: e.activation(out=B.qT[:, 0, :], in_=P.psf[pi][:, 0:NT], func=AF.Copy),
                                  reads=[P.r_psf[pi]], writes=[B.r_qT]))
        S.add("sp", lambda e, h=h: e.dma_start(out=A.KTh[:, 0:nkmax], in_=P.kt_scr[h, :, 0:nkmax]),
              reads=[P.r_kt_scr], writes=[A.r_KTh], dma=A.r_KTh)
        S.add("sp", lambda e, h=h: e.dma_start(out=A.Vh[:, 0:nkmax // 128, :],
                                               in_=P.v_scr[0:nkmax, h * 128:(h + 1) * 128].rearrange("(c p) d -> p c d", p=128)),
              reads=[P.r_v_scr], writes=[A.r_Vh], dma=A.r_Vh)
        ctxs = []
        for qc in range(4):
            ctxs.append(_moba_chunk_A(P, B, A, h, qc, pidx, scale, qc % 2))
            if qc >= 1:
                _moba_chunk_B(P, B, A, h, ctxs[qc - 1])
        _moba_chunk_B(P, B, A, h, ctxs[3])
    out_proj_full(P, B, P.w_o_b[li], 128)


def _moba_chunk_A(P, B, A, h, qc, pidx, scale, bi):
    S = P.S
    sm, r_sm = A.sms[bi], A.r_sms[bi]
    Pm, r_Pm = A.Pms[bi], A.r_Pms[bi]
    Pown, r_Pown = A.Powns[bi], A.r_Powns[bi]
    if True:
        if True:
            ac = pidx * 4 + qc
            ob = ac // 2
            nkf = ob * 256
            own = 128 if ac % 2 == 0 else 256
            NK = nkf + own
            qs = slice(qc * 128, (qc + 1) * 128)
            nb = (NK + 511) // 512
            banks = [P.next_ps() for _ in range(nb)]
            for i, bk in enumerate(banks):
                cols = min(512, NK - i * 512)
                S.add("pe", lambda e, i=i, bk=bk, cols=cols: e.matmul(P.psf[bk][:, 0:cols], lhsT=B.qT[:, 0, qs], rhs=A.KTh[:, i * 512:i * 512 + cols], start=True, stop=True),
                      reads=[B.r_qT, A.r_KTh], writes=[P.r_psf[bk]])
                S.add("dve", lambda e, i=i, bk=bk, cols=cols: e.tensor_reduce(out=sm[:, i:i + 1], in_=P.psf[bk][:, 0:cols], axis=AX.X, op=ALU.max),
                      reads=[P.r_psf[bk]], writes=[r_sm])
            S.add("dve", lambda e: e.tensor_reduce(out=sm[:, 4:5], in_=sm[:, 0:nb], axis=AX.X, op=ALU.max), reads=[r_sm], writes=[r_sm])
            S.add("dve", lambda e: e.tensor_scalar(out=sm[:, 5:6], in0=sm[:, 4:5], scalar1=-scale, scalar2=None, op0=ALU.mult), reads=[r_sm], writes=[r_sm])
            if ob > 3:
                pg = P.next_ps()
                S.add("pe", lambda e, pg=pg, h=h: e.matmul(P.psf[pg][:, 0:ob], lhsT=B.qT[:, 0, qs], rhs=P.meansT_bf[:, h, 0:ob], start=True, stop=True),
                      reads=[B.r_qT, P.r_means_bf], writes=[P.r_psf[pg]])
                S.add("dve", lambda e: e.memset(sm[:, 16:24], -1e30), writes=[r_sm])
                S.add("dve", lambda e, pg=pg: e.tensor_copy(out=sm[:, 16:16 + ob], in_=P.psf[pg][:, 0:ob]), reads=[P.r_psf[pg]], writes=[r_sm])
                S.add("dve", lambda e: e.max(out=sm[:, 24:32], in_=sm[:, 16:24]), reads=[r_sm], writes=[r_sm])
                S.add("dve", lambda e: e.tensor_scalar(out=sm[:, 8:8 + ob], in0=sm[:, 16:16 + ob], scalar1=sm[:, 26:27], scalar2=None, op0=ALU.is_ge),
                      reads=[r_sm], writes=[r_sm])
                S.add("dve", lambda e: e.tensor_scalar(out=sm[:, 8:8 + ob], in0=sm[:, 8:8 + ob], scalar1=-1.0, scalar2=30000.0, op0=ALU.add, op1=ALU.mult),
                      reads=[r_sm], writes=[r_sm])
                S.add("dve", lambda e: e.tensor_scalar(out=sm[:, 8:8 + ob], in0=sm[:, 8:8 + ob], scalar1=sm[:, 5:6], scalar2=None, op0=ALU.add),
                      reads=[r_sm], writes=[r_sm])
            elif ob > 0:
                S.add("dve", lambda e: e.tensor_copy(out=sm[:, 8:8 + ob], in_=sm[:, 5:6].to_broadcast([128, ob])), reads=[r_sm], writes=[r_sm])
            S.add("dve", lambda e: e.memset(sm[:, 32:48], 0.0), writes=[r_sm])
            for n in range(ob):
                bk = banks[(n * 256) // 512]
                co = (n * 256) % 512
                S.add("act", lambda e, n=n, bk=bk, co=co: e.activation(out=Pm[:, n * 256:(n + 1) * 256], in_=P.psf[bk][:, co:co + 256], func=AF.Exp,
                                                                       scale=scale, bias=sm[:, 8 + n:9 + n], accum_out=sm[:, 32 + n:33 + n]),
                      reads=[P.r_psf[bk], r_sm], writes=[r_Pm, r_sm])
            bk = banks[nkf // 512]
            co = nkf % 512
            nr = ob
            if own == 256:
                S.add("act", lambda e, bk=bk, co=co, nr=nr: e.activation(out=Pm[:, nkf:nkf + 128], in_=P.psf[bk][:, co:co + 128], func=AF.Exp,
                                                                         scale=scale, bias=sm[:, 5:6], accum_out=sm[:, 32 + nr:33 + nr]),
                      reads=[P.r_psf[bk], r_sm], writes=[r_Pm, r_sm])
                nr += 1
                co += 128
            oc = nkf + own - 128
            S.add("act", lambda e, bk=bk, co=co: e.activation(out=Pown[:], in_=P.psf[bk][:, co:co + 128], func=AF.Exp, scale=scale, bias=sm[:, 5:6]),
                  reads=[P.r_psf[bk], r_sm], writes=[r_Pown])
            S.add("dve", lambda e, oc=oc: e.tensor_tensor(out=Pm[:, oc:oc + 128], in0=Pown[:], in1=P.tri_qk[:], op=ALU.mult),
                  reads=[r_Pown, P.r_const], writes=[r_Pm])
            S.add("dve", lambda e, oc=oc, nr=nr: e.tensor_reduce(out=sm[:, 32 + nr:33 + nr], in_=Pm[:, oc:oc + 128], axis=AX.X, op=ALU.add),
                  reads=[r_Pm], writes=[r_sm])
            nr += 1
            S.add("dve", lambda e, nr=nr: e.tensor_reduce(out=sm[:, 6:7], in_=sm[:, 32:32 + nr], axis=AX.X, op=ALU.add), reads=[r_sm], writes=[r_sm])
            S.add("dve", lambda e: e.reciprocal(out=sm[:, 7:8], in_=sm[:, 6:7]), reads=[r_sm], writes=[r_sm])
            return (NK, qs, bi)


def _moba_chunk_B(P, B, A, h, ctx):
    S = P.S
    NK, qs, bi = ctx
    sm, r_sm = A.sms[bi], A.r_sms[bi]
    Pm, r_Pm = A.Pms[bi], A.r_Pms[bi]
    if True:
        if True:
            nkc = NK // 128
            for g0 in range(0, nkc, 8):
                g1 = min(nkc, g0 + 8)
                pb = P.next_psb()
                for kc in range(g0, g1):
                    S.add("pe", lambda e, kc=kc, pb=pb, g0=g0: e.transpose(out=P.psb[pb][:, (kc - g0) * 128:(kc - g0 + 1) * 128], in_=Pm[:, kc * 128:(kc + 1) * 128],
                                                                           identity=P.ident_bf[:]),
                          reads=[r_Pm, P.r_const], writes=[P.r_psb[pb]])
                S.add("dve", lambda e, pb=pb, g0=g0, g1=g1: e.tensor_copy(out=A.PT[:, g0:g1, :],
                                                                         in_=P.psb[pb][:, 0:(g1 - g0) * 128].rearrange("p (c n) -> p c n", n=128)),
                      reads=[P.r_psb[pb]], writes=[A.r_PT])
            po = P.next_ps()
            for kc in range(nkc):
                S.add("pe", lambda e, kc=kc, po=po: e.matmul(P.psf[po][:, 0:128], lhsT=A.PT[:, kc, :], rhs=A.Vh[:, kc, :], start=(kc == 0), stop=(kc == nkc - 1)),
                      reads=[A.r_PT, A.r_Vh], writes=[P.r_psf[po]])
            S.add("dve", lambda e, po=po: e.tensor_scalar(out=B.onb[:, 0:128], in0=P.psf[po][:, 0:128], scalar1=sm[:, 7:8], scalar2=None, op0=ALU.mult),
                  reads=[P.r_psf[po], r_sm], writes=[B.r_onb])
            pb = P.next_psb()
            S.add("pe", lambda e, pb=pb: e.transpose(out=P.psb[pb][:, 0:128], in_=B.onb[:, 0:128], identity=P.ident_bf[:]),
                  reads=[B.r_onb, P.r_const], writes=[P.r_psb[pb]])
            S.add("dve", lambda e, pb=pb, h=h: e.tensor_tensor(out=B.gT[:, h, qs], in0=P.psb[pb][:, 0:128], in1=B.sgT[:, h, qs], op=ALU.mult),
                  reads=[P.r_psb[pb], B.r_sgT], writes=[B.r_gT])


def out_proj_full(P, B, Wo, rows_last):
    S = P.S
    TT = B.TT
    for cb in range(4):
        slots = [P.wload_R(Wo, kk * 512, cb * 512) for kk in range(4)]
        for t in range(TT):
            R = rows_last if t == TT - 1 else 128
            pi = P.next_ps()
            for kk in range(4):
                for j in range(4):
                    S.add("pe", lambda e, kk=kk, j=j, pi=pi, t=t, R=R, si=slots[kk]: e.matmul(
                        P.psf[pi][:R, :], lhsT=B.gT[:, kk * 4 + j, t * 128:t * 128 + R], rhs=P.slotR(si)[:, j, :],
                        start=(kk == 0 and j == 0), stop=(kk == 3 and j == 3)),
                        reads=[B.r_gT, P.r_w[slots[kk]]], writes=[P.r_psf[pi]])
            P.residual_add(B, t, R, cb, pi)


def final_norm(P, B, row, dst, rows_last):
    S = P.S
    off = 4 * 3 * D + 2 * D
    P.load_rep(B, B.sh_rep, B.r_sh, row, off)
    P.load_rep(B, B.a_rep, B.r_a, row, off + D)
    S.add("sp", lambda e: e.dma_start(out=B.ntmp[:], in_=P.final_g.partition_broadcast(128)), writes=[B.r_ntmp], dma=B.r_ntmp)
    S.add("dve", lambda e: e.scalar_tensor_tensor(out=B.a_rep[:], in0=B.a_rep[:], scalar=1.0, in1=B.ntmp[:], op0=ALU.add, op1=ALU.mult),
          reads=[B.r_a, B.r_ntmp], writes=[B.r_a])
    for t in range(B.TT):
        R = rows_last if t == B.TT - 1 else 128
        S.add("dve", lambda e, R=R: e.memset(B.ss[:R, 0:1], 0.0), writes=[B.r_ss])
        S.add("act", lambda e, t=t, R=R: e.activation(out=B.ntmp[:R, :], in_=B.x[:R, t, :], func=AF.Square, accum_out=B.ss[:R, 0:1]),
              reads=[B.r_x[t], B.r_ss], writes=[B.r_ntmp, B.r_ss])
        S.add("act", lambda e, R=R: e.activation(out=B.ss[:R, 1:2], in_=B.ss[:R, 0:1], func=AF.Sqrt, scale=1.0 / D, bias=P.eps_t[:R, 0:1]),
              reads=[B.r_ss, P.r_const], writes=[B.r_ss])
        S.add("dve", lambda e, R=R: e.reciprocal(out=B.ss[:R, 2:3], in_=B.ss[:R, 1:2]), reads=[B.r_ss], writes=[B.r_ss])
        S.add("dve", lambda e, t=t, R=R: e.scalar_tensor_tensor(out=B.ntmp[:R, :], in0=B.x[:R, t, :], scalar=B.ss[:R, 2:3], in1=B.a_rep[:R, :],
                                                                 op0=ALU.mult, op1=ALU.mult),
              reads=[B.r_x[t], B.r_ss, B.r_a], writes=[B.r_ntmp])
        S.add("dve", lambda e, t=t, R=R: e.tensor_tensor(out=B.x[:R, t, :], in0=B.ntmp[:R, :], in1=B.sh_rep[:R, :], op=ALU.add),
              reads=[B.r_ntmp, B.r_sh, B.r_x[t]], writes=[B.r_x[t]])
        S.add("sp", lambda e, t=t, R=R: e.dma_start(out=dst[t * 128:t * 128 + R, :], in_=B.x[:R, t, :]),
              reads=[B.r_x[t]], dma=B.r_x[t])


def build_prompt_only(nseg=NSEG, nlayers=4):
    P = Prog(0, do_sample=False)
    with P.es:
        P.alloc_common()
        P.meansT = P.sb("meansT", [128, H_B, 8], F32)
        P.meansT_bf = P.sb("meansT_bf", [128, H_B, 8], BF16)
        P.r_means, P.r_means_bf = Res("means"), Res("means_bf")
        P.prepass_mods()
        B = P.alloc_pass("p", 4, TW)
        A = _alloc_attn(P, "p")
        for s in range(nseg):
            P.load_x(B, P.xp[s * TW:(s + 1) * TW, :], 128)
            for l in range(min(2, nlayers)):
                P.retention_layer(B, l, s, 0, 128, s == 0, s == nseg - 1, False)
            import os
            if nlayers >= 2 and os.environ.get("NOKV") != "1":
                kv_proj(P, B, A, s, 0, 128, False)
            for l in range(2, nlayers):
                moba_layer_prompt(P, B, A, l, s, 0)
            if nlayers == 4:
                final_norm(P, B, 0, P.y_p[s * TW:(s + 1) * TW, :], 128)
            else:
                P.dump_x(B, P.y_p[s * TW:(s + 1) * TW, :], 128)
        P.S.emit()
    return P


def _consts_sample():
    c = {}
    oh = np.zeros((4, 4, 128), np.float32)
    for q in range(4):
        oh[q, q, :] = 1.0
    c["onehot"] = oh.astype(ml_dtypes.bfloat16)
    bd = np.zeros((64, D), np.float32)
    sq = np.zeros((64, 4), np.float32)
    for h in range(H_B):
        for q in range(4):
            bd[h * 4 + q, h * 128:(h + 1) * 128] = 1.0
            sq[h * 4 + q, q] = 1.0
    c["blockdiag"] = bd.astype(ml_dtypes.bfloat16)
    c["selq"] = sq.astype(ml_dtypes.bfloat16)
    nm = np.zeros((4, 64), np.float32)
    for key in range(4):
        for h in range(H_B):
            for q in range(4):
                if key > q:
                    nm[key, h * 4 + q] = -30000.0
    c["negmaskn"] = nm
    c["piota"] = np.arange(128, dtype=np.float32).reshape(128, 1)
    return c


def alloc_sample_attn(P):
    A = _alloc_attn(P, "s", sample=True)
    A.KTn = P.sb("KTn", [128, H_B, 4], BF16)
    A.Vn = P.sb("Vn", [128, D], BF16)
    A.r_KTn, A.r_Vn = Res("KTn"), Res("Vn")
    A.ptf = P.sb("ptf", [128, NPAGES], F32)
    A.pti = P.sb("pti", [128, NPAGES], I32)
    A.idx = P.sb("idx", [128, NPAGES], I32)
    A.r_idx = Res("idx")
    A.kpage = [P.sb(f"kpage{i}", [128, D], F32) for i in range(2)]
    A.r_kpage = [Res(f"kpage{i}") for i in range(2)]
    A.KTp = [P.sb(f"KTp{i}", [128, H_B, 128], BF16) for i in range(2)]
    A.r_KTp = [Res(f"KTp{i}") for i in range(2)]
    A.ST = P.sb("ST", [128, NPAGES, 64], F32)
    A.r_ST = Res("ST")
    A.PTs = P.sb("PTs", [128, NPAGES, 64], BF16)
    A.r_PTs = Res("PTs")
    A.vb = [P.sb("vb0", [128, D], BF16)] * 2
    A.r_vb = [Res("vb0")] * 2
    A.qTs = P.sb("qTs", [128, H_B, 4], BF16)
    A.r_qTs = Res("qTs")
    A.msumS = P.sb("msumS", [128, H_B, NPAGES], F32)
    A.meansS_bf = P.sb("meansS_bf", [128, H_B, 64], BF16)
    A.r_msumS, A.r_meansS = Res("msumS"), Res("meansS")
    A.selrep = P.sb("selrep", [128, 4, 1024], BF16)
    A.r_selrep = Res("selrep")
    A.sc = P.sb("sc", [128, 512], F32)
    A.r_sc = Res("sc")
    A.scb = P.sb("scb", [128, 256], BF16)
    A.r_scb = Res("scb")
    A.gt = P.sb("gt", [4, 1024], F32)
    A.selb = P.sb("selb", [4, 1024], BF16)
    A.r_gt = Res("gt")
    A.Om = P.sb("Om", [64, D], BF16)
    A.r_Om = Res("Om")
    A.onehot = P.sb("onehot_t", [4, 4, 128], BF16)
    A.blockdiag = P.sb("blockdiag_t", [64, D], BF16)
    A.selq = P.sb("selq_t", [64, 4], BF16)
    A.negmaskn = P.sb("negmaskn_t", [4, 64], F32)
    A.piota = P.sb("piota_t", [128, 1], F32)
    A.r_c = Res("sconst")
    S = P.S
    for dst, src in ((A.onehot, P.c_onehot), (A.blockdiag, P.c_blockdiag), (A.selq, P.c_selq), (A.negmaskn, P.c_negmaskn), (A.piota, P.c_piota)):
        S.add("sp", lambda e, dst=dst, src=src: e.dma_start(out=dst[:], in_=src), writes=[A.r_c], dma=A.r_c)
    S.add("dve", lambda e: e.memset(A.scb[:, 128:129], 1.0), writes=[A.r_c])
    return A


def sample_page_index(P, A, si):
    S = P.S
    S.add("sp", lambda e: e.dma_start(out=A.pti[:], in_=P.ptab[si].partition_broadcast(128)), writes=[A.r_idx], dma=A.r_idx)
    S.add("dve", lambda e: e.tensor_copy(out=A.ptf[:], in_=A.pti[:]), reads=[A.r_idx], writes=[A.r_idx])
    S.add("dve", lambda e: e.tensor_scalar(out=A.ptf[:], in0=A.ptf[:], scalar1=128.0, scalar2=A.piota[:, 0:1], op0=ALU.mult, op1=ALU.add),
          reads=[A.r_idx, A.r_c], writes=[A.r_idx])
    S.add("dve", lambda e: e.tensor_copy(out=A.idx[:], in_=A.ptf[:]), reads=[A.r_idx], writes=[A.r_idx])


def moba_layer_sample(P, B, A, l, row, si, do_means):
    S = P.S
    NT = B.NT
    li = l - 2
    W = P.w_q_b[li]
    moff = l * 3 * D
    scale = 128 ** -0.5
    P.mod_norm(B, row, P.norm_g[l], moff, moff + D, 4)
    P.load_rep(B, B.gate_rep, B.r_gate, row, moff + 2 * D)
    for j in range(16):
        P.proj_T(B, W, 2048 + j * 128,
                 lambda pi, j=j: S.add("act", lambda e: e.activation(out=B.sgT[:, j, :], in_=P.psf[pi][:, 0:NT], func=AF.Silu),
                                       reads=[P.r_psf[pi]], writes=[B.r_sgT]))
    for h in range(H_B):
        P.proj_T(B, W, h * 128,
                 lambda pi, h=h: S.add("act", lambda e: e.activation(out=A.qTs[:, h, :], in_=P.psf[pi][:, 0:NT], func=AF.Copy),
                                       reads=[P.r_psf[pi]], writes=[A.r_qTs]))
    sc, scb = A.sc, A.scb
    import os
    mst = int(os.environ.get("MSTAGE", "9"))
    if mst < 2:
        out_proj_full(P, B, P.w_o_b[li], 4)
        return
    for j in range(NPAGES):
        b = j % 2
        S.add("pool", lambda e, j=j, b=b: e.indirect_dma_start(out=A.kpage[b][:], out_offset=None, in_=P.cache_k,
                                                              in_offset=bass.IndirectOffsetOnAxis(ap=A.idx[:, j:j + 1], axis=0)),
              reads=[A.r_idx], writes=[A.r_kpage[b]], dma=A.r_kpage[b])
        if mst < 3:
            S.add("dve", lambda e, j=j, b=b: e.tensor_copy(out=A.ST[:, j, :], in_=A.kpage[b][:, 0:64]), reads=[A.r_kpage[b]], writes=[A.r_ST])
            continue
        for g in range(4):
            pi = P.next_ps()
            for hh in range(4):
                h = g * 4 + hh
                S.add("pe", lambda e, pi=pi, hh=hh, h=h, b=b: e.transpose(out=P.psf[pi][:, hh * 128:(hh + 1) * 128], in_=A.kpage[b][:, h * 128:(h + 1) * 128],
                                                                        identity=P.ident_f[:]),
                      reads=[A.r_kpage[b], P.r_const], writes=[P.r_psf[pi]])
            S.add("act", lambda e, pi=pi, g=g, b=b: e.activation(out=A.KTp[b][:, g * 4:(g + 1) * 4, :], in_=P.psf[pi][:].rearrange("p (h n) -> p h n", n=128), func=AF.Copy),
                  reads=[P.r_psf[pi]], writes=[A.r_KTp[b]])
        if do_means:
            S.add("dve", lambda e, j=j, b=b: e.tensor_reduce(out=A.msumS[:, :, j], in_=A.KTp[b][:], axis=AX.X, op=ALU.add),
                  reads=[A.r_KTp[b]], writes=[A.r_msumS])
        pst = P.next_ps()
        for h in range(H_B):
            S.add("pe", lambda e, pst=pst, h=h, b=b: e.matmul(P.psf[pst][:, h * 4:(h + 1) * 4], lhsT=A.KTp[b][:, h, :], rhs=A.qTs[:, h, :], start=True, stop=True),
                  reads=[A.r_KTp[b], A.r_qTs], writes=[P.r_psf[pst]])
        S.add("act", lambda e, pst=pst, j=j: e.activation(out=A.ST[:, j, :], in_=P.psf[pst][:, 0:64], func=AF.Copy),
              reads=[P.r_psf[pst]], writes=[A.r_ST])
    if do_means:
        S.add("dve", lambda e: e.tensor_tensor(out=A.meansS_bf[:], in0=A.msumS[:].rearrange("p h (b g) -> p h b g", g=2)[:, :, :, 0],
                                               in1=A.msumS[:].rearrange("p h (b g) -> p h b g", g=2)[:, :, :, 1], op=ALU.add),
              reads=[A.r_msumS], writes=[A.r_meansS])
    if mst < 4:
        out_proj_full(P, B, P.w_o_b[li], 4)
        return
    pn = P.next_ps()
    for h in range(H_B):
        S.add("pe", lambda e, h=h: e.matmul(P.psf[pn][0:4, h * 4:(h + 1) * 4], lhsT=A.KTn[:, h, 0:4], rhs=A.qTs[:, h, :], start=True, stop=True),
              reads=[A.r_KTn, A.r_qTs], writes=[P.r_psf[pn]])
    S.add("dve", lambda e: e.tensor_tensor(out=sc[0:4, 128:192], in0=P.psf[pn][0:4, 0:64], in1=A.negmaskn[:], op=ALU.add),
          reads=[P.r_psf[pn], A.r_c], writes=[A.r_sc])
    S.add("dve", lambda e: e.tensor_reduce(out=sc[:, 0:64], in_=A.ST[:].rearrange("p j c -> p c j"), axis=AX.X, op=ALU.max),
          reads=[A.r_ST], writes=[A.r_sc])
    S.add("dve", lambda e: e.tensor_tensor(out=sc[0:4, 0:64], in0=sc[0:4, 0:64], in1=sc[0:4, 128:192], op=ALU.max), reads=[A.r_sc], writes=[A.r_sc])
    pm = P.next_ps()
    S.add("pe", lambda e: e.transpose(out=P.psf[pm][0:64, 0:128], in_=sc[:, 0:64], identity=P.ident_f[:]),
          reads=[A.r_sc, P.r_const], writes=[P.r_psf[pm]])
    S.add("dve", lambda e: e.tensor_reduce(out=sc[0:64, 320:321], in_=P.psf[pm][0:64, 0:128], axis=AX.X, op=ALU.max), reads=[P.r_psf[pm]], writes=[A.r_sc])
    S.add("dve", lambda e: e.tensor_scalar(out=scb[0:64, 0:64], in0=P.ident_f[0:64, 0:64], scalar1=sc[0:64, 320:321], scalar2=None, op0=ALU.mult),
          reads=[A.r_sc, P.r_const], writes=[A.r_scb])
    pm2 = P.next_ps()
    S.add("pe", lambda e: e.matmul(P.psf[pm2][:, 0:64], lhsT=P.ones_bf[0:64, :], rhs=scb[0:64, 0:64], start=True, stop=True),
          reads=[A.r_scb, P.r_const], writes=[P.r_psf[pm2]])
    S.add("dve", lambda e: e.tensor_copy(out=sc[:, 64:128], in_=P.psf[pm2][:, 0:64]), reads=[P.r_psf[pm2]], writes=[A.r_sc])
    pgs = [P.next_ps(), P.next_ps()]
    for h in range(H_B):
        S.add("pe", lambda e, h=h: e.matmul(P.psf[pgs[h // 8]][0:4, (h % 8) * 64:(h % 8 + 1) * 64], lhsT=A.qTs[:, h, :], rhs=A.meansS_bf[:, h, :], start=True, stop=True),
              reads=[A.r_qTs, A.r_meansS], writes=[P.r_psf[pgs[h // 8]]])
    for g in range(2):
        S.add("dve", lambda e, g=g: e.tensor_copy(out=A.gt[0:4, g * 512:(g + 1) * 512], in_=P.psf[pgs[g]][0:4, :]), reads=[P.r_psf[pgs[g]]], writes=[A.r_gt])
    for h in range(H_B):
        S.add("dve", lambda e, h=h: e.max(out=sc[0:4, 192 + h * 8:200 + h * 8], in_=A.gt[0:4, h * 64:(h + 1) * 64]), reads=[A.r_gt], writes=[A.r_sc])
        S.add("dve", lambda e, h=h: e.tensor_scalar(out=A.selb[0:4, h * 64:(h + 1) * 64], in0=A.gt[0:4, h * 64:(h + 1) * 64],
                                                    scalar1=sc[0:4, 194 + h * 8:195 + h * 8], scalar2=None, op0=ALU.is_ge),
              reads=[A.r_gt, A.r_sc], writes=[A.r_gt])
    for q in range(4):
        for half in range(2):
            pr = P.next_ps()
            S.add("pe", lambda e, q=q, half=half, pr=pr: e.matmul(P.psf[pr][:, :], lhsT=A.onehot[0:4, q, :], rhs=A.selb[0:4, half * 512:(half + 1) * 512], start=True, stop=True),
                  reads=[A.r_gt, A.r_c], writes=[P.r_psf[pr]])
            S.add("act", lambda e, q=q, half=half, pr=pr: e.activation(out=A.selrep[:, q, half * 512:(half + 1) * 512], in_=P.psf[pr][:, :], func=AF.Copy),
                  reads=[P.r_psf[pr]], writes=[A.r_selrep])
    S.add("dve", lambda e: e.tensor_tensor(out=A.ST[:], in0=A.ST[:], in1=sc[:, 64:128].unsqueeze(1).to_broadcast([128, NPAGES, 64]), op=ALU.subtract),
          reads=[A.r_ST, A.r_sc], writes=[A.r_ST])
    S.add("act", lambda e: e.activation(out=A.PTs[:], in_=A.ST[:], func=AF.Exp, scale=scale), reads=[A.r_ST], writes=[A.r_PTs])
    for q in range(4):
        S.add("dve", lambda e, q=q: e.tensor_tensor(
            out=A.PTs[:].rearrange("p (b g) (h q) -> p b g h q", g=2, q=4)[:, :, :, :, q],
            in0=A.PTs[:].rearrange("p (b g) (h q) -> p b g h q", g=2, q=4)[:, :, :, :, q],
            in1=A.selrep[:, q, :].rearrange("p (h b) -> p b h", b=64).unsqueeze(2).to_broadcast([128, 64, 2, H_B]), op=ALU.mult),
            reads=[A.r_PTs, A.r_selrep], writes=[A.r_PTs])
    S.add("dve", lambda e: e.tensor_tensor(out=sc[0:4, 128:192], in0=sc[0:4, 128:192], in1=sc[0:4, 64:128], op=ALU.subtract), reads=[A.r_sc], writes=[A.r_sc])
    S.add("act", lambda e: e.activation(out=scb[0:4, 64:128], in_=sc[0:4, 128:192], func=AF.Exp, scale=scale), reads=[A.r_sc], writes=[A.r_scb])
    if mst < 5:
        out_proj_full(P, B, P.w_o_b[li], 4)
        return
    po = [P.next_ps() for _ in range(4)]
    prs = P.next_ps()
    for j in range(NPAGES):
        b = j % 2
        S.add("pool", lambda e, j=j, b=b: e.indirect_dma_start(out=A.kpage[b][:], out_offset=None, in_=P.cache_v,
                                                              in_offset=bass.IndirectOffsetOnAxis(ap=A.idx[:, j:j + 1], axis=0)),
              reads=[A.r_idx], writes=[A.r_kpage[b]], dma=A.r_kpage[b])
        if j % 2 == 0:
            S.add("act", lambda e, b=b: e.activation(out=A.vb[b][:], in_=A.kpage[b][:], func=AF.Copy), reads=[A.r_kpage[b]], writes=[A.r_vb[b]])
        else:
            S.add("dve", lambda e, b=b: e.tensor_copy(out=A.vb[b][:], in_=A.kpage[b][:]), reads=[A.r_kpage[b]], writes=[A.r_vb[b]])
        for cbk in range(4):
            S.add("pe", lambda e, j=j, b=b, cbk=cbk: e.matmul(P.psf[po[cbk]][0:64, :], lhsT=A.PTs[:, j, :], rhs=A.vb[b][:, cbk * 512:(cbk + 1) * 512], start=(j == 0), stop=False),
                  reads=[A.r_PTs, A.r_vb[b]], writes=[P.r_psf[po[cbk]]])
        S.add("pe", lambda e, j=j: e.matmul(P.psf[prs][0:64, 0:8], lhsT=A.PTs[:, j, :], rhs=P.ones_bf[:, 0:8], start=(j == 0), stop=False),
              reads=[A.r_PTs, P.r_const], writes=[P.r_psf[prs]])
    for cbk in range(4):
        S.add("pe", lambda e, cbk=cbk: e.matmul(P.psf[po[cbk]][0:64, :], lhsT=scb[0:4, 64:128], rhs=A.Vn[0:4, cbk * 512:(cbk + 1) * 512], start=False, stop=True),
              reads=[A.r_scb, A.r_Vn], writes=[P.r_psf[po[cbk]]])
    S.add("pe", lambda e: e.matmul(P.psf[prs][0:64, 0:8], lhsT=scb[0:4, 64:128], rhs=P.ones_bf[0:4, 0:8], start=False, stop=True),
          reads=[A.r_scb, P.r_const], writes=[P.r_psf[prs]])
    if mst < 6:
        out_proj_full(P, B, P.w_o_b[li], 4)
        return
    S.add("dve", lambda e: e.reciprocal(out=sc[0:64, 321:322], in_=P.psf[prs][0:64, 0:1]), reads=[P.r_psf[prs]], writes=[A.r_sc])
    for cbk in range(4):
        S.add("dve", lambda e, cbk=cbk: e.scalar_tensor_tensor(out=A.Om[:, cbk * 512:(cbk + 1) * 512], in0=P.psf[po[cbk]][0:64, :], scalar=sc[0:64, 321:322],
                                                              in1=A.blockdiag[:, cbk * 512:(cbk + 1) * 512], op0=ALU.mult, op1=ALU.mult),
              reads=[P.r_psf[po[cbk]], A.r_sc, A.r_c], writes=[A.r_Om])
    pb = P.next_psb()
    for cbk in range(4):
        pt = P.next_ps()
        S.add("pe", lambda e, cbk=cbk, pt=pt: e.matmul(P.psf[pt][0:4, :], lhsT=A.selq[:, :], rhs=A.Om[:, cbk * 512:(cbk + 1) * 512], start=True, stop=True),
              reads=[A.r_Om, A.r_c], writes=[P.r_psf[pt]])
        S.add("act", lambda e, cbk=cbk, pt=pt: e.activation(out=B.hb[0:4, cbk * 512:(cbk + 1) * 512], in_=P.psf[pt][0:4, :], func=AF.Copy),
              reads=[P.r_psf[pt]], writes=[B.r_hb])
    for h in range(H_B):
        S.add("pe", lambda e, h=h: e.transpose(out=P.psb[pb][:, h * 4:(h + 1) * 4], in_=B.hb[0:4, h * 128:(h + 1) * 128], identity=P.ident_bf[0:4, 0:4]),
              reads=[B.r_hb, P.r_const], writes=[P.r_psb[pb]])
    S.add("dve", lambda e: e.tensor_tensor(out=B.gT[:, :, 0:4], in0=P.psb[pb][:, 0:64].rearrange("p (h q) -> p h q", q=4), in1=B.sgT[:, :, 0:4], op=ALU.mult),
          reads=[P.r_psb[pb], B.r_sgT], writes=[B.r_gT])
    out_proj_full(P, B, P.w_o_b[li], 4)


def build_full(n_pool, ns=2, nseg=NSEG, prompt=True):
    P = Prog(n_pool, do_sample=True, ns=ns)
    with P.es:
        P.alloc_common()
        P.ones_bf = P.sb("ones_bf", [128, 128], BF16)
        P.S.add("dve", lambda e: e.memset(P.ones_bf[:], 1.0), writes=[P.r_const])
        P.prepass_mods()
        P.barrier()
        if prompt:
            P.cur_es = contextlib.ExitStack()
            P.meansT = P.sb("meansT", [128, H_B, 8], F32)
            P.meansT_bf = P.sb("meansT_bf", [128, H_B, 8], BF16)
            P.r_means, P.r_means_bf = Res("means"), Res("means_bf")
            B = P.alloc_pass("p", 4, TW)
            A = _alloc_attn(P, "p")
            for s in range(nseg):
                P.load_x(B, P.xp[s * TW:(s + 1) * TW, :], 128)
                for l in range(2):
                    P.retention_layer(B, l, s, 0, 128, s == 0, s == nseg - 1, False)
                kv_proj(P, B, A, s, 0, 128, False)
                for l in range(2, 4):
                    moba_layer_prompt(P, B, A, l, s, 0)
                final_norm(P, B, 0, P.y_p[s * TW:(s + 1) * TW, :], 128)
            P.barrier()
            P.cur_es.close()
        P.cur_es = contextlib.ExitStack()
        Bs = P.alloc_pass("s", 1, 4)
        As = alloc_sample_attn(P)
        for si in range(ns):
            row = 1 + si
            sample_page_index(P, As, si)
            P.load_x(Bs, P.xs[si], 4)
            import os
            sst = int(os.environ.get("SSTAGE", "9"))
            for l in range(2):
                if sst >= 1:
                    P.retention_layer(Bs, l, NSEG, row, 4, False, False, True, st_in=P.st_in[si], sts=P.sts[si])
            if sst >= 2:
                kv_proj(P, Bs, As, NSEG, row, 4, True, samp_i=si)
            for l in range(2, 4):
                if sst >= 3:
                    moba_layer_sample(P, Bs, As, l, row, si, l == 2)
            final_norm(P, Bs, row, P.y_s[si], 4)
        P.S.emit()
        P.cur_es.close()
    return P


NS_PER_CORE = 2


def kernel(x_prompt, x_sample, state_ret, cache_k, cache_v, page_table, c_prompt, c_sample,
           norm_g, w_mod, b_mod, w_in_a, w_out_a, w_q_b, w_o_b,
           kv_norm_g, w_mod_kv, b_mod_kv, w_kv, final_g, w_mod_f, b_mod_f):
    ns = NS_PER_CORE
    ncores = 8 // ns
    f32 = lambda a: np.ascontiguousarray(np.asarray(a), dtype=np.float32)
    x_prompt, x_sample, state_ret = f32(x_prompt), f32(x_sample), f32(state_ret)
    cache_k, cache_v = f32(cache_k), f32(cache_v)
    page_table = np.ascontiguousarray(np.asarray(page_table), dtype=np.int32)
    n_pool = cache_k.shape[0]
    P = build_full(n_pool, ns=ns)
    shared = {"norm_g": f32(norm_g), "w_mod": f32(w_mod), "b_mod": f32(b_mod), "w_in_a": f32(w_in_a), "w_out_a": f32(w_out_a),
              "w_q_b": f32(w_q_b), "w_o_b": f32(w_o_b), "kv_norm_g": f32(kv_norm_g), "w_mod_kv": f32(w_mod_kv),
              "b_mod_kv": f32(b_mod_kv), "w_kv": f32(w_kv), "final_g": f32(final_g), "w_mod_f": f32(w_mod_f), "b_mod_f": f32(b_mod_f),
              "cache_k": cache_k.reshape(n_pool * 128, D), "cache_v": cache_v.reshape(n_pool * 128, D)}
    shared.update(_consts())
    shared.update(_consts_sample())
    c_prompt, c_sample = f32(c_prompt), f32(c_sample)
    in_maps = []
    for c in range(ncores):
        b = (c * ns) // 2
        s0 = c * ns
        m = dict(shared)
        m["xp"] = x_prompt[b]
        m["cp"] = c_prompt[b]
        m["xs"] = x_sample[s0:s0 + ns]
        m["cs"] = c_sample[s0:s0 + ns]
        m["st_in"] = np.ascontiguousarray(state_ret[:, s0:s0 + ns].transpose(1, 0, 2, 3, 4))
        m["ptab"] = page_table[s0:s0 + ns]
        in_maps.append(m)
    res = run_bass_kernel_spmd(P.nc, in_maps, core_ids=list(range(ncores)))
    r = res.results
    nb, nsamp = x_prompt.shape[0], x_sample.shape[0]
    y_prompt = np.zeros((nb, SEQ, D), np.float32)
    y_sample = np.zeros((nsamp, 4, D), np.float32)
    st_p = np.zeros((2, nb, H_A, 256, 512), np.float32)
    st_s = np.zeros((2, nsamp, H_A, 256, 512), np.float32)
    k_p = np.zeros((nb, SEQ, H_B, 128), np.float32)
    v_p = np.zeros((nb, SEQ, H_B, 128), np.float32)
    k_s = np.zeros((nsamp, 4, H_B, 128), np.float32)
    v_s = np.zeros((nsamp, 4, H_B, 128), np.float32)
    for c in range(ncores):
        b = (c * ns) // 2
        if (c * ns) % 2 == 0:
            y_prompt[b] = r[c]["y_p"]
            st_p[:, b] = r[c]["stp"]
            k_p[b] = r[c]["k_p"].reshape(SEQ, H_B, 128)
            v_p[b] = r[c]["v_p"].reshape(SEQ, H_B, 128)
        for i in range(ns):
            s = c * ns + i
            y_sample[s] = r[c]["y_s"][i]
            st_s[:, s] = r[c]["sts"][i]
            k_s[s] = r[c]["k_s"][i].reshape(4, H_B, 128)
            v_s[s] = r[c]["v_s"][i].reshape(4, H_B, 128)
    return (y_prompt, y_sample, st_p, st_s, k_p, v_p, k_s, v_s)
```
